# Optimizing a Trainium2 kernel written in Bass

```python
import math
import jax, jax.numpy as jnp
from jax import lax
import numpy as np

D_MODEL = 1024
BATCH = 8
SEQ = 4096
DEPTH = 1
DEC_BATCH = 128
DEC_SEQ = 8
PAST_LEN = 16384
PAGE_SIZE = 128

N_META = 16
HEAD_DIM = 64
ATTN_WIDTH = D_MODEL // 2
N_Q_HEADS = ATTN_WIDTH // HEAD_DIM
Q_PER_KV = 4
N_KV_HEADS = N_Q_HEADS // Q_PER_KV
KV_WIDTH = N_KV_HEADS * HEAD_DIM
WINDOW = 128
BLOCK = 128
SSM_WIDTH = D_MODEL - ATTN_WIDTH
SSM_GROUP = 16
N_SSM_GROUPS = SSM_WIDTH // SSM_GROUP
SSM_STATE = 64
IN_COLS = ATTN_WIDTH + 2 * KV_WIDTH + SSM_WIDTH
D_FF = -(-8 * D_MODEL // (3 * 256)) * 256
EPS = 1e-6
ATTN_SCALE = HEAD_DIM ** -0.5

kernel_name = "hymba_s5_swa_sink_decoder_step"


def rms_norm(x, g):
    xf = x.astype(jnp.float32)
    y = xf * lax.rsqrt(jnp.mean(xf * xf, axis=-1, keepdims=True) + EPS)
    return (y * g.astype(jnp.float32)).astype(x.dtype)


def mixer_front(x, g_mix, w_in, g_q, g_k):
    b, l = x.shape[:2]
    z = rms_norm(x, g_mix) @ w_in
    q, k, v, u = jnp.split(z, [ATTN_WIDTH, ATTN_WIDTH + KV_WIDTH, ATTN_WIDTH + 2 * KV_WIDTH], axis=-1)
    q = rms_norm(q.reshape(b, l, N_Q_HEADS, HEAD_DIM), g_q)
    k = rms_norm(k.reshape(b, l, N_KV_HEADS, HEAD_DIM), g_k)
    v = v.reshape(b, l, N_KV_HEADS, HEAD_DIM)
    u = u.reshape(b, l, N_SSM_GROUPS, SSM_GROUP)
    return q, k, v, u


def sink_attention(q, k, v, mask, sinks):
    lead = q.shape[:-3]
    tq = q.shape[-3]
    qg = q.reshape(lead + (tq, N_KV_HEADS, Q_PER_KV, HEAD_DIM))
    s = jnp.einsum('...qhgd,...khd->...hgqk', qg, k).astype(jnp.float32) * ATTN_SCALE
    s = jnp.where(mask, s, -jnp.inf)
    sink = sinks.astype(jnp.float32).reshape(N_KV_HEADS, Q_PER_KV, 1, 1)
    m = jnp.maximum(jnp.max(s, axis=-1, keepdims=True), sink)
    p = jnp.exp(s - m)
    p = p / (jnp.sum(p, axis=-1, keepdims=True) + jnp.exp(sink - m))
    o = jnp.einsum('...hgqk,...khd->...qhgd', p.astype(v.dtype), v)
    return o.reshape(lead + (tq, N_Q_HEADS * HEAD_DIM))


def prompt_window_attention(q, k, v, sinks):
    b, l = q.shape[:2]
    lpad = (-N_META) % BLOCK
    rpad = (-(lpad + l)) % BLOCK
    padw = ((0, 0), (lpad, rpad), (0, 0), (0, 0))
    qp, kp, vp = jnp.pad(q, padw), jnp.pad(k, padw), jnp.pad(v, padw)
    lp = lpad + l + rpad
    nb = lp // BLOCK
    qb = qp.reshape(b, nb, BLOCK, N_Q_HEADS, HEAD_DIM)
    kb = kp.reshape(b, nb, BLOCK, N_KV_HEADS, HEAD_DIM)
    vb = vp.reshape(b, nb, BLOCK, N_KV_HEADS, HEAD_DIM)

    def prev(t):
        return jnp.pad(t, ((0, 0), (1, 0), (0, 0), (0, 0), (0, 0)))[:, :-1]

    kk = jnp.concatenate([prev(kb), kb], axis=2)
    vv = jnp.concatenate([prev(vb), vb], axis=2)
    qpos = jnp.arange(nb)[:, None] * BLOCK + jnp.arange(BLOCK)[None, :]
    kpos = jnp.arange(nb)[:, None] * BLOCK - BLOCK + jnp.arange(2 * BLOCK)[None, :]
    dist = qpos[:, :, None] - kpos[:, None, :]
    valid_k = (kpos >= lpad) & (kpos < lpad + l)
    mask = (dist >= 0) & (dist <= WINDOW) & valid_k[:, None, :]
    o = sink_attention(qb, kk, vv, mask[None, :, None, None], sinks)
    return o.reshape(b, lp, ATTN_WIDTH)[:, lpad:lpad + l]


def sample_window_attention(q, k, v, cache_k, cache_v, sinks):
    t = q.shape[1]
    kk = jnp.concatenate([cache_k.astype(k.dtype), k], axis=1)
    vv = jnp.concatenate([cache_v.astype(v.dtype), v], axis=1)
    dist = (jnp.arange(t)[:, None] + WINDOW) - jnp.arange(WINDOW + t)[None, :]
    mask = (dist >= 0) & (dist <= WINDOW)
    o = sink_attention(q, kk, vv, mask[None, None, None], sinks)
    return o, kk[:, t:], vv[:, t:]


def zoh_discretize(a_re, a_im, log_dt, b_re, b_im):
    f32 = jnp.float32
    ar, ai = a_re.astype(f32), a_im.astype(f32)
    dt = jnp.exp(log_dt.astype(f32))[:, None]
    mag = jnp.exp(ar * dt)
    lr, li = mag * jnp.cos(ai * dt), mag * jnp.sin(ai * dt)
    nr, ni = lr - 1.0, li
    den = ar * ar + ai * ai
    fr, fi = (nr * ar + ni * ai) / den, (ni * ar - nr * ai) / den
    br, bi = b_re.astype(f32), b_im.astype(f32)
    bbr = fr[..., None] * br - fi[..., None] * bi
    bbi = fr[..., None] * bi + fi[..., None] * br
    return lr, li, bbr, bbi


def _ssm_combine(e1, e2):
    a1r, a1i, b1r, b1i = e1
    a2r, a2i, b2r, b2i = e2
    return (a2r * a1r - a2i * a1i, a2r * a1i + a2i * a1r,
            a2r * b1r - a2i * b1i + b2r, a2r * b1i + a2i * b1r + b2i)


def ssm_mixer(u, h0_re, h0_im, a_re, a_im, log_dt, b_re, b_im, c_re, c_im, d_skip, w_glu, b_glu):
    f32 = jnp.float32
    b, l = u.shape[:2]
    lr, li, bbr, bbi = zoh_discretize(a_re, a_im, log_dt, b_re, b_im)
    uf = u.astype(f32)
    xr = jnp.einsum('blgh,gph->blgp', uf, bbr)
    xi = jnp.einsum('blgh,gph->blgp', uf, bbi)
    h0r, h0i = h0_re.astype(f32), h0_im.astype(f32)
    xr = xr.at[:, 0].add(lr * h0r - li * h0i)
    xi = xi.at[:, 0].add(lr * h0i + li * h0r)
    ar = jnp.broadcast_to(lr, (1, l) + lr.shape)
    ai = jnp.broadcast_to(li, (1, l) + li.shape)
    _, _, hr, hi = lax.associative_scan(_ssm_combine, (ar, ai, xr, xi), axis=1)
    y = (jnp.einsum('blgp,ghp->blgh', hr, c_re.astype(f32))
         - jnp.einsum('blgp,ghp->blgh', hi, c_im.astype(f32))
         + d_skip.astype(f32) * uf)
    g = jax.nn.gelu(y)
    gate = jnp.einsum('blgh,ghk->blgk', g, w_glu.astype(f32)) + b_glu.astype(f32)
    out = (g * jax.nn.sigmoid(gate)).reshape(b, l, SSM_WIDTH).astype(u.dtype)
    return out, hr[:, -1], hi[:, -1]


def mixer_back(x, o_att, o_ssm, g_att_out, g_ssm_out, w_out, g_ffn, w_gate, w_up, w_down):
    mix = jnp.concatenate([rms_norm(o_att, g_att_out), rms_norm(o_ssm.astype(o_att.dtype), g_ssm_out)], axis=-1)
    h = x + mix @ w_out
    f = rms_norm(h, g_ffn)
    return h + (jax.nn.silu(f @ w_gate) * (f @ w_up)) @ w_down


def setup_inputs(seed: int = 0) -> dict:
    key = jax.random.key(seed)
    ks = jax.random.split(key, 32)
    f32 = jnp.float32
    G, P, H = N_SSM_GROUPS, SSM_STATE, SSM_GROUP

    def nrm(k, shape, scale=1.0):
        return scale * jax.random.normal(k, shape, f32)

    def gain(k, shape):
        return 1.0 + 0.02 * jax.random.normal(k, shape, f32)

    n_idx = jnp.arange(P, dtype=f32)
    return {
        "x_prompt": nrm(ks[0], (BATCH, SEQ, D_MODEL)),
        "x_sample": nrm(ks[1], (DEC_BATCH, DEC_SEQ, D_MODEL)),
        "cache_k_win": nrm(ks[2], (DEPTH, DEC_BATCH, WINDOW, N_KV_HEADS, HEAD_DIM)),
        "cache_v_win": nrm(ks[3], (DEPTH, DEC_BATCH, WINDOW, N_KV_HEADS, HEAD_DIM)),
        "state_ssm_re": nrm(ks[4], (DEPTH, DEC_BATCH, G, P), 0.1),
        "state_ssm_im": nrm(ks[5], (DEPTH, DEC_BATCH, G, P), 0.1),
        "meta_tokens": nrm(ks[6], (N_META, D_MODEL)),
        "g_mix": gain(ks[7], (DEPTH, D_MODEL)),
        "w_in": nrm(ks[8], (DEPTH, D_MODEL, IN_COLS), D_MODEL ** -0.5),
        "g_q": gain(ks[9], (DEPTH, HEAD_DIM)),
        "g_k": gain(ks[10], (DEPTH, HEAD_DIM)),
        "sinks": nrm(ks[11], (DEPTH, N_Q_HEADS), 0.5),
        "ssm_a_re": -0.5 + nrm(ks[12], (DEPTH, G, P), 0.01),
        "ssm_a_im": math.pi * n_idx + nrm(ks[13], (DEPTH, G, P), 0.01),
        "ssm_log_dt": jax.random.uniform(ks[14], (DEPTH, G), f32, math.log(1e-3), math.log(1e-1)),
        "ssm_b_re": nrm(ks[15], (DEPTH, G, P, H), (2 * H) ** -0.5),
        "ssm_b_im": nrm(ks[16], (DEPTH, G, P, H), (2 * H) ** -0.5),
        "ssm_c_re": nrm(ks[17], (DEPTH, G, H, P), P ** -0.5),
        "ssm_c_im": nrm(ks[18], (DEPTH, G, H, P), P ** -0.5),
        "ssm_d": nrm(ks[19], (DEPTH, G, H)),
        "ssm_w_glu": nrm(ks[20], (DEPTH, G, H, H), H ** -0.5),
        "ssm_b_glu": nrm(ks[21], (DEPTH, G, H), 0.01),
        "g_att_out": gain(ks[22], (DEPTH, ATTN_WIDTH)),
        "g_ssm_out": gain(ks[23], (DEPTH, SSM_WIDTH)),
        "w_out": nrm(ks[24], (DEPTH, D_MODEL, D_MODEL), D_MODEL ** -0.5),
        "g_ffn": gain(ks[25], (DEPTH, D_MODEL)),
        "w_gate": nrm(ks[26], (DEPTH, D_MODEL, D_FF), D_MODEL ** -0.5),
        "w_up": nrm(ks[27], (DEPTH, D_MODEL, D_FF), D_MODEL ** -0.5),
        "w_down": nrm(ks[28], (DEPTH, D_FF, D_MODEL), D_FF ** -0.5),
    }


def reference(x_prompt, x_sample, cache_k_win, cache_v_win, state_ssm_re, state_ssm_im,
              meta_tokens, g_mix, w_in, g_q, g_k, sinks,
              ssm_a_re, ssm_a_im, ssm_log_dt, ssm_b_re, ssm_b_im, ssm_c_re, ssm_c_im,
              ssm_d, ssm_w_glu, ssm_b_glu, g_att_out, g_ssm_out, w_out,
              g_ffn, w_gate, w_up, w_down):
    bp = x_prompt.shape[0]
    meta = jnp.broadcast_to(meta_tokens.astype(x_prompt.dtype), (bp, N_META, D_MODEL))
    xp = jnp.concatenate([meta, x_prompt], axis=1)
    xs = x_sample
    kp_l, vp_l, rp_l, ip_l = [], [], [], []
    ks_l, vs_l, rs_l, is_l = [], [], [], []
    for li in range(DEPTH):
        ssm_p = (ssm_a_re[li], ssm_a_im[li], ssm_log_dt[li], ssm_b_re[li], ssm_b_im[li],
                 ssm_c_re[li], ssm_c_im[li], ssm_d[li], ssm_w_glu[li], ssm_b_glu[li])
        back_p = (g_att_out[li], g_ssm_out[li], w_out[li], g_ffn[li], w_gate[li], w_up[li], w_down[li])

        q, k, v, u = mixer_front(xp, g_mix[li], w_in[li], g_q[li], g_k[li])
        o_att = prompt_window_attention(q, k, v, sinks[li])
        h0 = jnp.zeros((bp, N_SSM_GROUPS, SSM_STATE), jnp.float32)
        o_ssm, hr, hi = ssm_mixer(u, h0, h0, *ssm_p)
        xp = mixer_back(xp, o_att, o_ssm, *back_p)
        kp_l.append(k[:, -WINDOW:])
        vp_l.append(v[:, -WINDOW:])
        rp_l.append(hr)
        ip_l.append(hi)

        q, k, v, u = mixer_front(xs, g_mix[li], w_in[li], g_q[li], g_k[li])
        o_att, nk, nv = sample_window_attention(q, k, v, cache_k_win[li], cache_v_win[li], sinks[li])
        o_ssm, hr, hi = ssm_mixer(u, state_ssm_re[li], state_ssm_im[li], *ssm_p)
        xs = mixer_back(xs, o_att, o_ssm, *back_p)
        ks_l.append(nk)
        vs_l.append(nv)
        rs_l.append(hr)
        is_l.append(hi)

    y_prompt = xp[:, N_META:]
    y_sample = xs
    return (y_prompt, y_sample,
            jnp.stack(kp_l), jnp.stack(vp_l), jnp.stack(rp_l), jnp.stack(ip_l),
            jnp.stack(ks_l), jnp.stack(vs_l), jnp.stack(rs_l), jnp.stack(is_l))
```

```python
import math
import numpy as np
import concourse.bass as bass
import concourse.mybir as mybir
from concourse.bass_utils import run_bass_kernel_spmd

F32 = mybir.dt.float32
BF16 = mybir.dt.bfloat16
AF = mybir.ActivationFunctionType
ALU = mybir.AluOpType

NCORES = 8
D = 1024
DFF = 2816
NF = 22
EPS = 1e-6
NPT = 33
SCALE = 0.125
ARENA_WORDS = 53100
DEBUG = False
STOP_AT = None


class _StopBuild(Exception):
    pass


class Buf:
    __slots__ = ("name", "w", "r", "excl")

    def __init__(self, name, excl=False):
        self.name = name
        self.w = None
        self.r = {}
        self.excl = excl


class EngQ:
    def __init__(self, nc, eng, name, same=False):
        self.eng = eng
        self.name = name
        self.sem = nc.alloc_semaphore("sem_" + name)
        self.n = 0
        self.waited = {}
        self.same = same


class Slot:
    def __init__(self, nc, name):
        self.sem = nc.alloc_semaphore("dsem_" + name)
        self.n = 0
        self.name = name


class KB:
    def __init__(self, nc):
        self.nc = nc
        self.PE = EngQ(nc, nc.tensor, "pe")
        self.ACT = EngQ(nc, nc.scalar, "act", same=True)
        self.DVE = EngQ(nc, nc.vector, "dve", same=True)
        self.POOL = EngQ(nc, nc.gpsimd, "pool", same=True)
        self.SP = EngQ(nc, nc.sync, "sp")
        self.nslots = 0

    def slot(self, name):
        self.nslots += 1
        return Slot(self.nc, name)

    def _deps(self, R, W):
        deps = []
        for b in R:
            if b.w is not None:
                deps.append((b.w, True))
        for b in W:
            if b.w is not None:
                deps.append((b.w, False))
            deps.extend((t, False) for t in b.r.values())
        return deps

    def _wait(self, q, deps):
        for ((obj, val), raw) in deps:
            if obj is q and not (q.same and raw):
                continue
            if isinstance(obj, Slot):
                val = obj.n
            if q.waited.get(id(obj), 0) >= val:
                continue
            q.eng.wait_ge(obj.sem, val)
            q.waited[id(obj)] = val

    def _mark(self, tok, R, W):
        obj, val = tok
        for b in R:
            cur = b.r.get(id(obj))
            if cur is None or cur[1] < val:
                b.r[id(obj)] = tok
        for b in W:
            b.w = tok
            b.r = {}

    def op(self, q, fns, R=(), W=()):
        if not isinstance(fns, (list, tuple)):
            fns = [fns]
        if any(b.excl for b in R):
            W = list(W) + [b for b in R if b.excl]
            R = [b for b in R if not b.excl]
        self._wait(q, self._deps(R, W))
        ins = None
        for f in fns:
            ins = f()
        q.n += 1
        ins.then_inc(q.sem, 1)
        tok = (q, q.n)
        self._mark(tok, R, W)
        return tok

    def dma(self, q, fn, slot, R=(), W=()):
        self._wait(q, self._deps(R, W))
        ins = fn()
        slot.n += 16
        ins.then_inc(slot.sem, 16)
        tok = (slot, slot.n)
        self._mark(tok, R, W)
        return tok

    def barrier(self, qs=None, slots=()):
        qs = qs or [self.PE, self.ACT, self.DVE, self.POOL, self.SP]
        for q in qs:
            for sl in slots:
                if sl.n and q.waited.get(id(sl), 0) < sl.n:
                    q.eng.wait_ge(sl.sem, sl.n)
                    q.waited[id(sl)] = sl.n
            for o in qs:
                if o is q or o.n == 0:
                    continue
                if q.waited.get(id(o), 0) >= o.n:
                    continue
                q.eng.wait_ge(o.sem, o.n)
                q.waited[id(o)] = o.n


class Arena:
    def __init__(self, nc, words):
        self.nc = nc
        self.slab = nc.alloc_sbuf_tensor("arena", [128, words], F32)
        self.base = int(nc.lookup_mloc(self.slab).addr)
        self.words = words
        self.top = 0
        self.peak = 0
        self.n = 0

    def _at(self, off_words, n_elem, dtype):
        self.n += 1
        h = self.nc.alloc_sbuf_tensor_at("b%d" % self.n, [128, n_elem], dtype, offset=self.base + 4 * off_words, align_bytes=4)
        return h[:, :]

    def raw_f32(self, off_words, words):
        return self._at(off_words, words, F32)

    def raw_bf(self, off_words, elems):
        return self._at(off_words, elems, BF16)

    def f32(self, words, shape=None):
        off = self.top
        self.top += words
        self.peak = max(self.peak, self.top)
        assert self.top <= self.words, ("arena overflow", self.top, self.words)
        ap = self._at(off, words, F32)
        if shape is not None:
            ap = self._shape(ap, shape)
        return ap

    def bf(self, elems, shape=None):
        words = (elems + 1) // 2
        off = self.top
        self.top += words
        self.peak = max(self.peak, self.top)
        assert self.top <= self.words, ("arena overflow", self.top, self.words)
        ap = self._at(off, elems, BF16)
        if shape is not None:
            ap = self._shape(ap, shape)
        return ap

    @staticmethod
    def _shape(ap, shape):
        if len(shape) == 2:
            return ap.rearrange("p (a b) -> p a b", a=shape[0])
        if len(shape) == 3:
            return ap.rearrange("p (a b c) -> p a b c", a=shape[0], b=shape[1])
        if len(shape) == 4:
            return ap.rearrange("p (a b c d) -> p a b c d", a=shape[0], b=shape[1], c=shape[2])
        raise ValueError(shape)


def build_nc():
    nc = bass.Bass("TRN2", target_bir_lowering=False)

    def din(name, shape):
        return nc.dram_tensor(name, list(shape), F32, kind="ExternalInput").ap()

    def dout(name, shape):
        return nc.dram_tensor(name, list(shape), F32, kind="ExternalOutput").ap()

    xp = din("xp", [4096, D]); xs = din("xs", [128, D]); meta = din("meta", [16, D])
    ck = din("ck", [16, 128, 128]); cv = din("cv", [16, 128, 128])
    sre = din("sre", [256, 128]); sim = din("sim", [256, 128])
    g_mix = din("g_mix", [D]); w_in = din("w_in", [D, 1280]); g_q = din("g_q", [64]); g_k = din("g_k", [64])
    sinks = din("sinks", [8]); a_re = din("a_re", [2048]); a_im = din("a_im", [2048]); log_dt = din("log_dt", [32])
    b_re = din("b_re", [32768]); b_im = din("b_im", [32768]); c_re = din("c_re", [512, 64]); c_im = din("c_im", [512, 64])
    ssm_d = din("ssm_d", [512]); w_glu = din("w_glu", [512, 16]); b_glu = din("b_glu", [512])
    g_att = din("g_att", [512]); g_ssm = din("g_ssm", [512]); w_out = din("w_out", [D, D]); g_ffn = din("g_ffn", [D])
    w_gate = din("w_gate", [D, DFF]); w_up = din("w_up", [D, DFF]); w_down = din("w_down", [DFF, D])

    yp = dout("yp", [4096, D]); ys = dout("ys", [128, D])
    kwp = dout("kwp", [128, 128]); vwp = dout("vwp", [128, 128])
    srp = dout("srp", [16, 128]); sip = dout("sip", [16, 128])
    kws = dout("kws", [16, 128, 128]); vws = dout("vws", [16, 128, 128])
    srs = dout("srs", [256, 128]); sis = dout("sis", [256, 128])

    K = KB(nc)
    PE, ACT, DVE, POOL, SP = K.PE, K.ACT, K.DVE, K.POOL, K.SP
    T, A, V, G, S = nc.tensor, nc.scalar, nc.vector, nc.gpsimd, nc.sync
    ar = Arena(nc, ARENA_WORDS)
    ps = nc.alloc_psum_tensor("ps", [128, 8, 512], F32)
    pb = [Buf("psb%d" % i, excl=True) for i in range(8)]
    out_slots = []
    ms_ctr = [0]
    dbg_slot = [None]
    dbg_names = []

    def dump(name, ap, bufs):
        if not DEBUG:
            return
        if dbg_slot[0] is None:
            dbg_slot[0] = K.slot("dbg")
        shp = list(ap.shape)
        dt_ = nc.dram_tensor("dbg_" + name, shp, ap.dtype, kind="ExternalOutput").ap()
        dbg_names.append(name)
        K.dma(SP, lambda: S.dma_start(out=dt_, in_=ap), dbg_slot[0], R=bufs)

    def milestone(name):
        ms_ctr[0] += 1
        if STOP_AT is not None and ms_ctr[0] >= STOP_AT:
            raise _StopBuild(name)

    nc_ctx = nc.allow_non_contiguous_dma(reason="small parameter layouts")
    nc_ctx.__enter__()

    ident = ar.f32(128); ones_f = ar.f32(128)
    mcur = ar.bf(128); mprev = ar.bf(128); m1prev = ar.bf(128); m0cur = ar.bf(128); msamp = ar.bf(128); mcache = ar.bf(128); mzero = ar.bf(128)
    blk64 = ar.bf(128); onesA = ar.bf(128); onesF = ar.bf(128); ones_b = ar.bf(128)
    gmix_c = ar.f32(8); gffn_c = ar.f32(8); gatt_c = ar.f32(4); gssm_c = ar.f32(4)
    gq_c = ar.f32(1); gk_c = ar.f32(1); esk = ar.f32(4); dcol = ar.f32(4); bglu_c = ar.f32(4)
    mh16 = ar.f32(16); rho8 = ar.f32(16); bd32 = ar.f32(128)
    hl_r = ar.f32(16); hl_i = ar.f32(16)
    NWIN = 1408
    winb = ar.bf(8 * NWIN, [8, NWIN])
    woutb = ar.bf(8 * D, [8, D])
    KT = ar.bf(4 * 8 * 128, [4, 8, 128])
    BL = ar.bf(4 * 8 * 2 * 128, [4, 8, 2, 128])
    CL = ar.bf(8 * 2 * 512, [8, 2, 512])
    glub = ar.bf(4 * 128, [4, 128])
    cs = ar.f32(2048, [16, 128]); sn = ar.f32(2048, [16, 128])
    kdupT = ar.bf(2 * 640, [2, 640])
    vtok = ar.bf(5 * 128, [5, 128])
    PERS_TOP = ar.top

    B = {}

    def bb(name):
        if name not in B:
            B[name] = Buf(name)
        return B[name]

    consts = bb("consts")
    Bwin = bb("winb"); Bwout = bb("woutb")
    setup_slot = K.slot("setup")
    wslot = K.slot("wres")

    def pool(fn, R=(), W=()):
        return K.op(POOL, fn, R, W)

    def dve(fn, R=(), W=()):
        return K.op(DVE, fn, R, W)

    def act(fn, R=(), W=()):
        return K.op(ACT, fn, R, W)

    def pe(fns, R=(), W=()):
        return K.op(PE, fns, R, W)

    C1 = [consts]
    pool(lambda: G.memset(ident, 0.0), W=C1)
    pool(lambda: G.affine_select(out=ident, in_=ident, pattern=[[-1, 128]], compare_op=ALU.not_equal,
                                 fill=1.0, base=0, channel_multiplier=1), R=C1, W=C1)
    pool(lambda: G.memset(ones_f, 1.0), W=C1)
    pool(lambda: G.memset(mh16, -0.5), W=C1)
    SET0 = ar.top
    mtmp = ar.f32(128 * 6, [6, 128])
    Bm = bb("mtmp")
    pool(lambda: G.memset(mtmp, 1.0), W=[Bm])
    pool(lambda: G.affine_select(out=mtmp[:, 0, :], in_=mtmp[:, 0, :], pattern=[[1, 128]], compare_op=ALU.is_ge,
                                 fill=0.0, base=0, channel_multiplier=-1), R=[Bm], W=[Bm])
    pool(lambda: G.affine_select(out=mtmp[:, 1, :], in_=mtmp[:, 1, :], pattern=[[-1, 128]], compare_op=ALU.is_ge,
                                 fill=0.0, base=0, channel_multiplier=1), R=[Bm], W=[Bm])
    pool(lambda: G.affine_select(out=mtmp[:, 2, :], in_=mtmp[:, 2, :], pattern=[[1, 128]], compare_op=ALU.is_ge,
                                 fill=0.0, base=0, channel_multiplier=-1), R=[Bm], W=[Bm])
    pool(lambda: G.affine_select(out=mtmp[:, 2, :], in_=mtmp[:, 2, :], pattern=[[0, 128]], compare_op=ALU.is_ge,
                                 fill=0.0, base=-112, channel_multiplier=1), R=[Bm], W=[Bm])
    v3 = mtmp[:, 3, :].rearrange("p (s i) -> p s i", s=16)
    pool(lambda: G.affine_select(out=v3, in_=v3, pattern=[[8, 16], [1, 8]], compare_op=ALU.is_ge,
                                 fill=0.0, base=0, channel_multiplier=-1), R=[Bm], W=[Bm])
    pool(lambda: G.affine_select(out=v3, in_=v3, pattern=[[-8, 16], [0, 8]], compare_op=ALU.is_ge,
                                 fill=0.0, base=0, channel_multiplier=1), R=[Bm], W=[Bm])
    v4 = mtmp[:, 4, :].rearrange("p (s i) -> p s i", s=16)
    pool(lambda: G.affine_select(out=v4, in_=v4, pattern=[[0, 16], [-1, 8]], compare_op=ALU.is_ge,
                                 fill=0.0, base=0, channel_multiplier=1), R=[Bm], W=[Bm])
    pool(lambda: G.affine_select(out=mtmp[:, 5, :], in_=mtmp[:, 5, :], pattern=[[-1, 128]], compare_op=ALU.is_ge,
                                 fill=0.0, base=0, channel_multiplier=1), R=[Bm], W=[Bm])
    pool(lambda: G.affine_select(out=mtmp[:, 5, :], in_=mtmp[:, 5, :], pattern=[[0, 128]], compare_op=ALU.is_ge,
                                 fill=0.0, base=-112, channel_multiplier=1), R=[Bm], W=[Bm])
    Bbd = bb("bd32")
    pool(lambda: G.memset(bd32, 1.0), W=[Bbd])
    bdv = bd32.rearrange("p (a b) -> p a b", a=4)
    pool(lambda: G.affine_select(out=bdv, in_=bdv, pattern=[[32, 4], [0, 32]], compare_op=ALU.is_ge,
                                 fill=0.0, base=31, channel_multiplier=-1), R=[Bbd], W=[Bbd])
    pool(lambda: G.affine_select(out=bdv, in_=bdv, pattern=[[-32, 4], [0, 32]], compare_op=ALU.is_ge,
                                 fill=0.0, base=0, channel_multiplier=1), R=[Bbd], W=[Bbd])
    bd16 = ar.f32(128)
    Bbd16 = bb("bd16")
    pool(lambda: G.memset(bd16, 1.0), W=[Bbd16])
    bdv16 = bd16.rearrange("p (a b) -> p a b", a=8)
    pool(lambda: G.affine_select(out=bdv16, in_=bdv16, pattern=[[16, 8], [0, 16]], compare_op=ALU.is_ge,
                                 fill=0.0, base=15, channel_multiplier=-1), R=[Bbd16], W=[Bbd16])
    pool(lambda: G.affine_select(out=bdv16, in_=bdv16, pattern=[[-16, 8], [0, 16]], compare_op=ALU.is_ge,
                                 fill=0.0, base=0, channel_multiplier=1), R=[Bbd16], W=[Bbd16])
    for i, m in enumerate([mcur, mprev, m0cur, msamp, mcache, m1prev]):
        dve(lambda i=i, m=m: V.tensor_copy(out=m, in_=mtmp[:, i, :]), R=[Bm], W=C1)
    dve(lambda: V.memset(mzero, 0.0), W=C1)
    dve(lambda: V.memset(blk64, 0.0), W=C1)
    dve(lambda: V.memset(blk64[0:64, 0:64], 1.0 / 64), W=C1)
    dve(lambda: V.memset(blk64[64:128, 64:128], 1.0 / 64), W=C1)
    dve(lambda: V.memset(onesA, 1.0 / 512), W=C1)
    dve(lambda: V.memset(onesF, 1.0 / 1024), W=C1)
    dve(lambda: V.memset(ones_b, 1.0), W=C1)
    Bkd = bb("kdupT"); Bvt = bb("vtok")
    dve(lambda: V.memset(kdupT, 0.0), W=[Bkd])
    dve(lambda: V.memset(vtok, 0.0), W=[Bvt])

    def wload(dst, src):
        K.dma(POOL, lambda: G.dma_start(out=dst, in_=src), wslot, W=[Bwin, Bwout])

    sgate = nc.dram_tensor("scr_gate", [NF, 128, 8 * 128], BF16, kind="Internal").ap()
    sup = nc.dram_tensor("scr_up", [NF, 128, 8 * 128], BF16, kind="Internal").ap()
    sdown = nc.dram_tensor("scr_down", [16, 128, 11 * 128], BF16, kind="Internal").ap()
    Bscr = bb("scratchW")
    cvslot = K.slot("conv")
    conv_jobs = []
    wg_v = w_gate.rearrange("(k p) c -> p k c", p=128)
    wu_v = w_up.rearrange("(k p) c -> p k c", p=128)
    wd_v = w_down.rearrange("(f p) c -> p f c", p=128)
    for f_ in range(NF):
        conv_jobs.append((sgate[f_].rearrange("p (k c) -> p k c", k=8), wg_v[:, :, f_ * 128:(f_ + 1) * 128]))
        conv_jobs.append((sup[f_].rearrange("p (k c) -> p k c", k=8), wu_v[:, :, f_ * 128:(f_ + 1) * 128]))
        if f_ == 10 or f_ == 21:
            hf_ = 0 if f_ == 10 else 1
            for m_ in range(8):
                conv_jobs.append((sdown[hf_ * 8 + m_].rearrange("p (f c) -> p f c", f=11),
                                  wd_v[:, 11 * hf_:11 * hf_ + 11, m_ * 128:(m_ + 1) * 128]))
    conv_state = {"i": 0}

    def conv_issue(n):
        for _ in range(n):
            i = conv_state["i"]
            if i >= len(conv_jobs):
                return
            if i >= 6:
                need = 16 * (i - 5)
                if POOL.waited.get(id(cvslot), 0) < need:
                    G.wait_ge(cvslot.sem, need)
                    POOL.waited[id(cvslot)] = need
            dst, src = conv_jobs[i]
            K.dma(POOL, lambda dst=dst, src=src: G.dma_start(out=dst, in_=src), cvslot, W=[Bscr])
            conv_state["i"] = i + 1

    Bpar = bb("params")
    sload_list = []

    def sload(dst, src):
        K.dma(SP, lambda: S.dma_start(out=dst, in_=src), setup_slot, W=[Bpar])

    BparA = bb("paramsA")
    setupA_slot = K.slot("setupA")

    def sloadA(dst, src):
        K.dma(SP, lambda: S.dma_start(out=dst, in_=src), setupA_slot, W=[BparA])

    are = ar.f32(16); aim = ar.f32(16); ldt = ar.f32(16)
    sloadA(are, a_re.rearrange("(k p) -> p k", p=128))
    sloadA(aim, a_im.rearrange("(k p) -> p k", p=128))
    ldt2 = log_dt.rearrange("(P two) -> two P", two=2)
    sloadA(ldt[0:64, :], ldt2[0, :].partition_broadcast(64))
    sloadA(ldt[64:128, :], ldt2[1, :].partition_broadcast(64))
    Bre = ar.f32(256, [16, 16]); Bim = ar.f32(256, [16, 16])
    sloadA(Bre, b_re.rearrange("(P q h) -> q P h", P=16, q=128))
    sloadA(Bim, b_im.rearrange("(P q h) -> q P h", P=16, q=128))
    BparA.w = (setupA_slot, setupA_slot.n)
    sload(gmix_c, g_mix.rearrange("(k p) -> p k", p=128))
    sload(gffn_c, g_ffn.rearrange("(k p) -> p k", p=128))
    sload(gatt_c, g_att.rearrange("(k p) -> p k", p=128))
    sload(gssm_c, g_ssm.rearrange("(k p) -> p k", p=128))
    sload(dcol, ssm_d.rearrange("(k p) -> p k", p=128))
    sload(bglu_c, b_glu.rearrange("(k p) -> p k", p=128))
    gq2 = g_q.rearrange("(p o) -> p o", o=1)
    gk2 = g_k.rearrange("(p o) -> p o", o=1)
    sload(gq_c[0:64, :], gq2); sload(gq_c[64:128, :], gq2)
    sload(gk_c[0:64, :], gk2); sload(gk_c[64:128, :], gk2)
    sk2 = sinks.rearrange("(t two) -> two t", two=2)
    sload(esk[0:64, :], sk2[0:1, :].partition_broadcast(64) if False else sinks.rearrange("(t two) -> two t", two=2)[0, :].partition_broadcast(64))
    sload(esk[64:128, :], sinks.rearrange("(t two) -> two t", two=2)[1, :].partition_broadcast(64))
    Cin = ar.f32(4 * 2 * 128, [4, 2, 128])
    Cld = ar.f32(4 * 2 * 64, [4, 2, 64])
    BCin = bb("Cin"); BCld = bb("Cld")
    for ri, csrc in enumerate([c_re, c_im]):
        K.dma(SP, lambda ri=ri, csrc=csrc: S.dma_start(out=Cld[:, :, ri, :], in_=csrc.rearrange("(o r) p -> r o p", o=4)),
              setup_slot, W=[BCld])
    gluf = ar.f32(4 * 128, [4, 128])
    gld = ar.f32(4 * 16, [4, 16])
    Bgl = bb("gluf"); Bgld = bb("gld")
    K.dma(SP, lambda: S.dma_start(out=gld, in_=w_glu.rearrange("(o r) k -> r o k", o=4)), setup_slot, W=[Bgld])
    fin = (setup_slot, setup_slot.n)
    for b in (Bpar, BCld, Bgld):
        b.w = fin
    G.wait_ge(setupA_slot.sem, setupA_slot.n)
    POOL.waited[id(setupA_slot)] = setupA_slot.n
    w_in_v = w_in.rearrange("(k p) c -> p k c", p=128)
    wload(winb[:, :, 896:1408], w_in_v[:, :, 768:1280])
    wload(winb[:, :, 0:512], w_in_v[:, :, 0:512])
    wload(winb[:, :, 512:576], w_in_v[:, :, 512:576])
    wload(winb[:, :, 576:640], w_in_v[:, :, 512:576])
    wload(winb[:, :, 640:704], w_in_v[:, :, 576:640])
    wload(winb[:, :, 704:768], w_in_v[:, :, 576:640])
    wload(winb[:, :, 768:896], w_in_v[:, :, 640:768])
    w_out_v = w_out.rearrange("(k p) c -> p k c", p=128)
    for k in range(0, 8, 2):
        wload(woutb[:, k:k + 2, :], w_out_v[:, k:k + 2, :])

    conv_issue(len(conv_jobs))
    P1 = [Bpar]

    def t16():
        return ar.f32(16)

    Bs = bb("ssmtmp")
    RS = [BparA, Bs]
    WS = [Bs]
    act(lambda: A.activation(out=esk, in_=esk, func=AF.Exp), R=P1, W=[Bpar])
    dt_ = t16(); adt = t16(); mag = t16(); th = t16()
    TWO_PI = 2.0 * math.pi

    def sin_of(dst, src, shift):
        u = t16(); ki = ar.f32(16).bitcast(mybir.dt.int32); kf = t16(); r = t16(); m1 = t16(); m2 = t16(); x2 = t16(); qq = t16()
        dve(lambda: V.tensor_scalar(out=u, in0=src, scalar1=shift, scalar2=1.0 / TWO_PI, op0=ALU.add, op1=ALU.mult), R=RS, W=WS)
        dve(lambda: V.tensor_copy(out=ki, in_=u), R=RS, W=WS)
        dve(lambda: V.tensor_copy(out=kf, in_=ki), R=RS, W=WS)
        dve(lambda: V.tensor_tensor(out=r, in0=u, in1=kf, op=ALU.subtract), R=RS, W=WS)
        for (thr, op_, sgn) in ((0.5, ALU.is_gt, -1.0), (-0.5, ALU.is_lt, 1.0)):
            dve(lambda thr=thr, op_=op_: V.tensor_scalar(out=m1, in0=r, scalar1=thr, scalar2=None, op0=op_), R=RS, W=WS)
            dve(lambda sgn=sgn: V.scalar_tensor_tensor(out=r, in0=m1, scalar=sgn, in1=r, op0=ALU.mult, op1=ALU.add), R=RS, W=WS)
        for (thr, op_, c0) in ((0.25, ALU.is_gt, 0.5), (-0.25, ALU.is_lt, -0.5)):
            dve(lambda thr=thr, op_=op_: V.tensor_scalar(out=m1, in0=r, scalar1=thr, scalar2=None, op0=op_), R=RS, W=WS)
            dve(lambda c0=c0: V.tensor_scalar(out=m2, in0=r, scalar1=-2.0, scalar2=c0, op0=ALU.mult, op1=ALU.add), R=RS, W=WS)
            dve(lambda: V.tensor_tensor(out=m2, in0=m2, in1=m1, op=ALU.mult), R=RS, W=WS)
            dve(lambda: V.tensor_tensor(out=r, in0=r, in1=m2, op=ALU.add), R=RS, W=WS)
        dve(lambda: V.tensor_scalar(out=r, in0=r, scalar1=TWO_PI, scalar2=None, op0=ALU.mult), R=RS, W=WS)
        dve(lambda: V.tensor_tensor(out=x2, in0=r, in1=r, op=ALU.mult), R=RS, W=WS)
        cf = [-1.0 / 6, 1.0 / 120, -1.0 / 5040, 1.0 / 362880, -1.0 / 39916800, 1.0 / 6227020800]
        dve(lambda: V.tensor_scalar(out=qq, in0=x2, scalar1=cf[5], scalar2=None, op0=ALU.mult), R=RS, W=WS)
        for c_ in (cf[4], cf[3], cf[2], cf[1], cf[0]):
            dve(lambda c_=c_: V.scalar_tensor_tensor(out=qq, in0=qq, scalar=c_, in1=x2, op0=ALU.add, op1=ALU.mult), R=RS, W=WS)
        dve(lambda: V.scalar_tensor_tensor(out=dst, in0=qq, scalar=1.0, in1=r, op0=ALU.add, op1=ALU.mult), R=RS, W=WS)

    def exp_of(dst, src, nsq):
        y = t16(); qq = t16()
        dve(lambda: V.tensor_scalar(out=y, in0=src, scalar1=1.0 / (2 ** nsq), scalar2=None, op0=ALU.mult), R=RS, W=WS)
        dve(lambda: V.tensor_scalar(out=qq, in0=y, scalar1=1.0 / 8, scalar2=1.0, op0=ALU.mult, op1=ALU.add), R=RS, W=WS)
        for k_ in (7, 6, 5, 4, 3, 2, 1):
            dve(lambda k_=k_: V.scalar_tensor_tensor(out=qq, in0=qq, scalar=1.0 / k_, in1=y, op0=ALU.mult, op1=ALU.mult), R=RS, W=WS)
            dve(lambda: V.tensor_scalar(out=qq, in0=qq, scalar1=1.0, scalar2=None, op0=ALU.add), R=RS, W=WS)
        for _ in range(nsq):
            dve(lambda: V.tensor_tensor(out=qq, in0=qq, in1=qq, op=ALU.mult), R=RS, W=WS)
        dve(lambda: V.tensor_copy(out=dst, in_=qq), R=RS, W=WS)

    exp_of(dt_, ldt, 4)
    dve(lambda: V.tensor_tensor(out=adt, in0=are, in1=dt_, op=ALU.mult), R=RS, W=WS)
    dve(lambda: V.tensor_tensor(out=th, in0=aim, in1=dt_, op=ALU.mult), R=RS, W=WS)
    exp_of(mag, adt, 1)
    sth = t16(); cth = t16()
    sin_of(sth, th, 0.0)
    sin_of(cth, th, math.pi / 2)
    Lr = ar.f32(9 * 16, [9, 16]); Li = ar.f32(9 * 16, [9, 16])
    dve(lambda: V.memset(Lr[:, 0, :], 1.0), W=WS)
    dve(lambda: V.memset(Li[:, 0, :], 0.0), W=WS)
    dve(lambda: V.tensor_tensor(out=Lr[:, 1, :], in0=mag, in1=cth, op=ALU.mult), R=RS, W=WS)
    dve(lambda: V.tensor_tensor(out=Li[:, 1, :], in0=mag, in1=sth, op=ALU.mult), R=RS, W=WS)
    ta = t16(); tb = t16()

    def cmul(dr, di, ar_, ai_, br_, bi_, shape_bc=None):
        dve(lambda: V.tensor_tensor(out=ta, in0=ai_, in1=bi_, op=ALU.mult), R=RS, W=WS)
        dve(lambda: V.tensor_tensor(out=tb, in0=ar_, in1=bi_, op=ALU.mult), R=RS, W=WS)
        dve(lambda: V.tensor_tensor(out=dr, in0=ar_, in1=br_, op=ALU.mult), R=RS, W=WS)
        dve(lambda: V.tensor_tensor(out=di, in0=ai_, in1=br_, op=ALU.mult), R=RS, W=WS)
        dve(lambda: V.tensor_tensor(out=dr, in0=dr, in1=ta, op=ALU.subtract), R=RS, W=WS)
        dve(lambda: V.tensor_tensor(out=di, in0=di, in1=tb, op=ALU.add), R=RS, W=WS)

    for n in range(2, 9):
        cmul(Lr[:, n, :], Li[:, n, :], Lr[:, n - 1, :], Li[:, n - 1, :], Lr[:, 1, :], Li[:, 1, :])
    nr = t16(); den = t16(); fr = t16(); fi = t16(); t3 = t16()
    dve(lambda: V.tensor_scalar(out=nr, in0=Lr[:, 1, :], scalar1=-1.0, scalar2=None, op0=ALU.add), R=RS, W=WS)
    dve(lambda: V.tensor_tensor(out=den, in0=are, in1=are, op=ALU.mult), R=RS, W=WS)
    dve(lambda: V.tensor_tensor(out=t3, in0=aim, in1=aim, op=ALU.mult), R=RS, W=WS)
    dve(lambda: V.tensor_tensor(out=den, in0=den, in1=t3, op=ALU.add), R=RS, W=WS)
    dve(lambda: V.reciprocal(out=den, in_=den), R=RS, W=WS)
    ni = Li[:, 1, :]
    dve(lambda: V.tensor_tensor(out=fr, in0=nr, in1=are, op=ALU.mult), R=RS, W=WS)
    dve(lambda: V.tensor_tensor(out=t3, in0=ni, in1=aim, op=ALU.mult), R=RS, W=WS)
    dve(lambda: V.tensor_tensor(out=fr, in0=fr, in1=t3, op=ALU.add), R=RS, W=WS)
    dve(lambda: V.tensor_tensor(out=fr, in0=fr, in1=den, op=ALU.mult), R=RS, W=WS)
    dve(lambda: V.tensor_tensor(out=fi, in0=ni, in1=are, op=ALU.mult), R=RS, W=WS)
    dve(lambda: V.tensor_tensor(out=t3, in0=nr, in1=aim, op=ALU.mult), R=RS, W=WS)
    dve(lambda: V.tensor_tensor(out=fi, in0=fi, in1=t3, op=ALU.subtract), R=RS, W=WS)
    dve(lambda: V.tensor_tensor(out=fi, in0=fi, in1=den, op=ALU.mult), R=RS, W=WS)
    wr = t16(); wi = t16(); w2 = t16()
    dve(lambda: V.tensor_tensor(out=w2, in0=Lr[:, 8, :], in1=Lr[:, 8, :], op=ALU.mult), R=RS, W=WS)
    dve(lambda: V.tensor_tensor(out=t3, in0=Li[:, 8, :], in1=Li[:, 8, :], op=ALU.mult), R=RS, W=WS)
    dve(lambda: V.tensor_tensor(out=w2, in0=w2, in1=t3, op=ALU.add), R=RS, W=WS)
    w2a = t16(); w2y = t16(); w2t = t16()
    dve(lambda: V.tensor_copy(out=w2a, in_=w2), R=RS, W=WS)
    act(lambda: A.activation(out=w2y, in_=w2a, func=AF.Ln), R=RS, W=WS)
    act(lambda: A.activation(out=w2y, in_=w2y, func=AF.Exp, scale=-0.5), R=RS, W=WS)
    dve(lambda: V.tensor_tensor(out=w2t, in0=w2y, in1=w2y, op=ALU.mult), R=RS, W=WS)
    dve(lambda: V.tensor_tensor(out=w2t, in0=w2t, in1=w2a, op=ALU.mult), R=RS, W=WS)
    dve(lambda: V.tensor_scalar(out=w2t, in0=w2t, scalar1=-0.5, scalar2=1.5, op0=ALU.mult, op1=ALU.add), R=RS, W=WS)
    dve(lambda: V.tensor_tensor(out=w2, in0=w2y, in1=w2t, op=ALU.mult), R=RS, W=WS)
    dve(lambda: V.reciprocal(out=rho8, in_=w2), R=RS, W=C1 + [Bs])
    dve(lambda: V.tensor_tensor(out=wr, in0=Lr[:, 8, :], in1=w2, op=ALU.mult), R=RS, W=WS)
    dve(lambda: V.tensor_tensor(out=wi, in0=Li[:, 8, :], in1=w2, op=ALU.mult), R=RS, W=WS)
    Btab = bb("tables")
    WT = [Btab, Bs]
    RT = [Btab, Bs, Bpar]
    dve(lambda: V.tensor_copy(out=cs[:, :, 0], in_=wr), R=RT, W=WT)
    dve(lambda: V.tensor_copy(out=sn[:, :, 0], in_=wi), R=RT, W=WT)
    tq1 = ar.f32(16 * 64, [16, 64]); tq2 = ar.f32(16 * 64, [16, 64])
    n = 1
    while n < 128:
        kr = cs[:, :, n - 1:n].to_broadcast([128, 16, n]); ki_ = sn[:, :, n - 1:n].to_broadcast([128, 16, n])
        sr = cs[:, :, 0:n]; si = sn[:, :, 0:n]; dr = cs[:, :, n:2 * n]; di = sn[:, :, n:2 * n]
        a1 = tq1[:, :, 0:n]; a2 = tq2[:, :, 0:n]
        dve(lambda a1=a1, si=si, ki_=ki_: V.tensor_tensor(out=a1, in0=si, in1=ki_, op=ALU.mult), R=RT, W=WT)
        dve(lambda a2=a2, sr=sr, ki_=ki_: V.tensor_tensor(out=a2, in0=sr, in1=ki_, op=ALU.mult), R=RT, W=WT)
        dve(lambda dr=dr, sr=sr, kr=kr: V.tensor_tensor(out=dr, in0=sr, in1=kr, op=ALU.mult), R=RT, W=WT)
        dve(lambda di=di, si=si, kr=kr: V.tensor_tensor(out=di, in0=si, in1=kr, op=ALU.mult), R=RT, W=WT)
        dve(lambda dr=dr, a1=a1: V.tensor_tensor(out=dr, in0=dr, in1=a1, op=ALU.subtract), R=RT, W=WT)
        dve(lambda di=di, a2=a2: V.tensor_tensor(out=di, in0=di, in1=a2, op=ALU.add), R=RT, W=WT)
        n *= 2
    bbr = ar.f32(256, [16, 16]); bbi = ar.f32(256, [16, 16]); tB1 = ar.f32(256, [16, 16]); tB2 = ar.f32(256, [16, 16])

    def bc16(x):
        return x.unsqueeze(2).to_broadcast([128, 16, 16])

    def cmulB(dr, di, sr, si, xr_, xi_, dr_halves=None):
        dve(lambda: V.tensor_tensor(out=tB1, in0=si, in1=bc16(xi_), op=ALU.mult), R=RS, W=WS)
        dve(lambda: V.tensor_tensor(out=tB2, in0=sr, in1=bc16(xi_), op=ALU.mult), R=RS, W=WS)
        dve(lambda: V.tensor_tensor(out=dr, in0=sr, in1=bc16(xr_), op=ALU.mult), R=RS, W=WS)
        dve(lambda: V.tensor_tensor(out=di, in0=si, in1=bc16(xr_), op=ALU.mult), R=RS, W=WS)
        dve(lambda: V.tensor_tensor(out=dr, in0=dr, in1=tB1, op=ALU.subtract), R=RS, W=WS)
        dve(lambda: V.tensor_tensor(out=di, in0=di, in1=tB2, op=ALU.add), R=RS, W=WS)

    cmulB(bbr, bbi, Bre, Bim, fr, fi)
    Mr = ar.f32(512, [16, 2, 16]); Mi = ar.f32(512, [16, 2, 16])
    MB0r = ar.f32(512, [16, 2, 16]); MB0i = ar.f32(512, [16, 2, 16])
    Wr_ = ar.f32(256, [16, 16]); Wi_ = ar.f32(256, [16, 16])
    BM = bb("Mtiles")
    for t_ in (Mr, Mi, MB0r, MB0i):
        dve(lambda t_=t_: V.memset(t_, 0.0), W=[BM])
    BBL = bb("BL")
    pbank = [0]

    def next_bank():
        b = pbank[0]
        pbank[0] = (b + 1) % 8
        return b

    for n in range(8):
        if n == 0:
            srcr, srci = bbr, bbi
        else:
            cmulB(Wr_, Wi_, bbr, bbi, Lr[:, n, :], Li[:, n, :])
            srcr, srci = Wr_, Wi_
        dstr, dsti = (MB0r, MB0i) if n == 0 else (Mr, Mi)
        for (dst, src) in ((dstr, srcr), (dsti, srci)):
            dve(lambda dst=dst, src=src: V.tensor_copy(out=dst[0:64, :, 0, :], in_=src[0:64]), R=RS, W=[BM])
            dve(lambda dst=dst, src=src: V.tensor_copy(out=dst[64:128, :, 1, :], in_=src[64:128]), R=RS, W=[BM])
        for part, msrc in enumerate((dstr, dsti)):
            for o in range(4):
                bk = next_bank()
                mv = msrc[:, 4 * o:4 * o + 4, :, :].rearrange("p a b c -> p (a b c)")
                pe(lambda bk=bk, mv=mv: T.matmul(ps[:, bk, 0:128], lhsT=mv, rhs=ident, is_transpose=True, start=True, stop=True), R=[BM, consts], W=[pb[bk]])
                act(lambda bk=bk, o=o, n=n, part=part: A.copy(out=BL[:, o, 7 - n, part, :], in_=ps[:, bk, 0:128]),
                    R=[pb[bk]], W=[BBL])
    glhot = ar.f32(2)
    dve(lambda: V.tensor_reduce(out=glhot, in_=bd16.rearrange("p (pl gl k) -> p gl pl k", pl=4, gl=2), axis=mybir.AxisListType.XY, op=ALU.add),
        R=[Bbd16], W=[BCin])
    dve(lambda: V.tensor_scalar(out=glhot, in0=glhot, scalar1=1.0 / 16, scalar2=None, op0=ALU.mult), R=[BCin], W=[BCin])
    for o in range(4):
        for ri in range(2):
            for gl in range(2):
                dve(lambda o=o, ri=ri, gl=gl: V.tensor_scalar(out=Cin[:, o, ri, gl * 64:(gl + 1) * 64], in0=Cld[:, o, ri, :],
                                                              scalar1=glhot[:, gl:gl + 1], scalar2=None, op0=ALU.mult), R=[BCld, BCin], W=[BCin])
        dve(lambda o=o: V.tensor_tensor(out=gluf[:, o, :].rearrange("p (g k) -> p g k", g=8),
                                        in0=gld[:, o, :].unsqueeze(1).to_broadcast([128, 8, 16]),
                                        in1=bd16.rearrange("p (g k) -> p g k", g=8), op=ALU.mult), R=[Bgld, Bbd16], W=[Bgl])

    BCT = bb("CT")
    CTB = [pb[0], pb[1]]
    for ri in range(2):
        pe([lambda o=o, ri=ri: T.matmul(ps[:, ri, o * 128:(o + 1) * 128], lhsT=Cin[:, o, ri, :], rhs=ident, is_transpose=True, start=True, stop=True)
            for o in range(4)], R=[BCin, consts], W=[pb[ri]])
    CTr = ps[:, 0, :].rearrange("p (a b) -> p a b", a=16)
    CTi = ps[:, 1, :].rearrange("p (a b) -> p a b", a=16)
    CLf = ar.f32(9 * 2 * 512, [9, 2, 512])
    BCLf = bb("CLf"); BCL = bb("CL")
    tC1 = ps[:, 2, :].rearrange("p (a b) -> p a b", a=16)
    tC2 = ar.f32(512, [16, 32])
    T1 = [pb[2]]

    def bc32(x):
        return x.unsqueeze(2).to_broadcast([128, 16, 32])

    def v512(x):
        return x.rearrange("p (a b) -> p a b", a=16)

    dve(lambda: V.tensor_copy(out=v512(CLf[:, 0, 0, :]), in_=CTr), R=CTB, W=[BCLf])
    dve(lambda: V.tensor_scalar(out=v512(CLf[:, 0, 1, :]), in0=CTi, scalar1=-1.0, scalar2=None, op0=ALU.mult), R=CTB, W=[BCLf])
    for n in range(1, 9):
        lr_, li_ = Lr[:, n, :], Li[:, n, :]
        o_r = v512(CLf[:, n, 0, :]); o_i = v512(CLf[:, n, 1, :])
        dve(lambda lr_=lr_: V.tensor_tensor(out=tC1, in0=CTr, in1=bc32(lr_), op=ALU.mult), R=RS + CTB, W=T1)
        dve(lambda li_=li_: V.tensor_tensor(out=tC2, in0=CTi, in1=bc32(li_), op=ALU.mult), R=RS + CTB, W=WS)
        dve(lambda o_r=o_r: V.tensor_tensor(out=o_r, in0=tC1, in1=tC2, op=ALU.subtract), R=RS + T1, W=[BCLf, Bs])
        dve(lambda li_=li_: V.tensor_tensor(out=tC1, in0=CTr, in1=bc32(li_), op=ALU.mult), R=RS + CTB, W=T1)
        dve(lambda lr_=lr_: V.tensor_tensor(out=tC2, in0=CTi, in1=bc32(lr_), op=ALU.mult), R=RS + CTB, W=WS)
        dve(lambda o_i=o_i: V.scalar_tensor_tensor(out=o_i, in0=tC1, scalar=-1.0, in1=tC2, op0=ALU.mult, op1=ALU.subtract),
            R=RS + T1, W=[BCLf, Bs])
        act(lambda n=n: A.copy(out=CL[:, n - 1, :, :], in_=CLf[:, n, :, :]), R=[BCLf], W=[BCL])
    BKT = bb("KT")
    ktmp = ar.f32(128)
    Bkt = bb("ktmp")
    for tau in range(8):
        for o in range(4):
            bk = next_bank()
            l_r = MB0r[:, 4 * o:4 * o + 4, :, :].rearrange("p a b c -> p (a b c)")
            l_i = MB0i[:, 4 * o:4 * o + 4, :, :].rearrange("p a b c -> p (a b c)")
            r_r = CLf[:, tau, 0, 128 * o:128 * o + 128]
            r_i = CLf[:, tau, 1, 128 * o:128 * o + 128]
            pe([lambda bk=bk, l_r=l_r, r_r=r_r: T.matmul(ps[:, bk, 0:128], lhsT=l_r, rhs=r_r, start=True, stop=False),
                lambda bk=bk, l_i=l_i, r_i=r_i: T.matmul(ps[:, bk, 0:128], lhsT=l_i, rhs=r_i, start=False, stop=True)],
               R=[BM, BCLf], W=[pb[bk]])
            if tau == 0:
                dve(lambda bk=bk: V.tensor_tensor(out=ktmp, in0=ps[:, bk, 0:128], in1=bd32, op=ALU.mult), R=[pb[bk], Bbd], W=[Bkt])
                dve(lambda o=o: V.scalar_tensor_tensor(out=KT[:, o, 0, :], in0=ident, scalar=dcol[:, o:o + 1], in1=ktmp,
                                                       op0=ALU.mult, op1=ALU.add), R=[Bkt, consts, Bpar], W=[BKT])
            else:
                dve(lambda bk=bk, o=o, tau=tau: V.tensor_tensor(out=KT[:, o, tau, :], in0=ps[:, bk, 0:128], in1=bd32, op=ALU.mult),
                    R=[pb[bk], Bbd], W=[BKT])
    Bglu = bb("glub")
    dve(lambda: V.tensor_copy(out=glub, in_=gluf), R=[Bgl], W=[Bglu])
    for k in range(8):
        act(lambda k=k: A.activation(out=winb[:, k, :], in_=winb[:, k, :], func=AF.Copy, scale=gmix_c[:, k:k + 1]),
            R=[Bwin, Bpar], W=[Bwin])
    Bhl = bb("hlast")
    dve(lambda: V.memset(hl_r, 0.0), W=[Bhl])
    dve(lambda: V.memset(hl_i, 0.0), W=[Bhl])
    K.barrier()
    dump("cs", cs, [Btab]); dump("sn", sn, [Btab]); dump("rho8", rho8, [consts]); dump("esk", esk, [Bpar])
    dump("KT", KT, [BKT]); dump("BL", BL, [BBL]); dump("CL", CL, [BCL]); dump("glub", glub, [Bglu])
    dump("Lr", Lr, [Bs]); dump("Li", Li, [Bs]); dump("fr", fr, [Bs]); dump("fi", fi, [Bs])
    dump("mcur", mcur, [consts]); dump("mprev", mprev, [consts]); dump("m0cur", m0cur, [consts]); dump("msamp", msamp, [consts])
    dump("mcache", mcache, [consts]); dump("m1prev", m1prev, [consts]); dump("bd32", bd32, [Bbd]); dump("ident", ident, [consts])
    dump("winb", winb, [Bwin])
    SSMW = [BKT, BBL, BCL, Bglu, Btab, consts, Bpar]

    ar.top = PERS_TOP
    xst = [ar.f32(1024), ar.f32(1024)]
    junk = ar.bf(1024)
    aT = ar.bf(11 * 512, [11, 512])
    R0_END = ar.top
    xnT = ar.bf(8 * 512, [8, 512])
    ossm = ar.bf(4 * 1024, [4, 1024])
    rbc = ar.f32(512)
    diag = ar.f32(128)
    ssq = ar.f32(4); rst = ar.f32(4)
    PH0 = ar.top
    useg = ar.bf(4 * 8 * 128, [4, 8, 128])
    rt1 = ar.f32(1024); rt2 = ar.f32(1024)
    zr = ar.f32(1024); zi = ar.f32(1024); gr = ar.f32(1024); gi = ar.f32(1024)
    hb_r = ar.f32(8 * 129, [8, 129]); hb_i = ar.f32(8 * 129, [8, 129])
    hprev = ar.bf(16 * 2 * 128, [16, 2, 128])
    Gt = ar.bf(1024); sg_ = ar.bf(1024)
    h0r = ar.f32(256, [16, 16]); h0i = ar.f32(256, [16, 16])
    hsr = ar.f32(256, [16, 16]); hsi = ar.f32(256, [16, 16])
    hs_st = ar.f32(256)
    A_TOP = ar.top
    ar.top = PH0
    xT = ar.f32(8 * 512, [8, 512])
    qn = ar.bf(4 * 512, [4, 512])
    sqt = ar.bf(512)
    rq = ar.f32(512)
    et = ar.bf(16 * 128, [16, 128]); pt = ar.bf(16 * 128, [16, 128])
    rden = ar.f32(512, [4, 128])
    oatt = ar.f32(4 * 512, [4, 512])
    NGU = 3
    NDN = 3
    WG_OFF = ar.top
    wgu = [ar.bf(2 * 8 * 128, [2, 8, 128]) for _ in range(NGU)]
    wdn = [ar.bf(11 * 128, [11, 128]) for _ in range(NDN)]
    sgb = ar.bf(512)
    kvo = ar.f32(256, [2, 128])
    kTf = ar.f32(256, [2, 128])
    B_TOP = ar.top
    sq8 = aT[:, 0:8, :]
    _aTtail = aT[:, 8:11, :].rearrange("p a b -> p (a b)")
    sqtB = _aTtail[:, 0:512]
    rqB = _aTtail[:, 512:1536].bitcast(F32)
    cbase = WG_OFF
    ckst = ar.raw_f32(cbase, 2048).rearrange("p (a b c) -> p a b c", a=8, b=2)
    KcT = ar.raw_bf(cbase + 2048, 4096).rearrange("p (a b c) -> p a b c", a=16, b=2)
    Vc = ar.raw_bf(cbase + 4096, 2048).rearrange("p (a b) -> p a b", a=16)
    assert NGU * 1024 + NDN * 704 >= 5120
    assert max(A_TOP, B_TOP) < ARENA_WORDS - 1, (A_TOP, B_TOP)

    Bx = [bb("xst0"), bb("xst1")]
    Bjunk = bb("junk"); BaT = bb("aT"); BxnTk = [bb("xnT%d" % i) for i in range(8)]; Bossm = bb("ossm"); Brbc = bb("rbc"); Bdiag = bb("diag")
    Bssq = bb("ssq"); Brst = bb("rst")
    Buseg = bb("useg"); Brt = bb("rt"); Bz = bb("z"); Bg = bb("g"); Bhb = bb("hb"); Bhp = bb("hprev")
    BGt = bb("Gt"); Bsg = bb("sg"); Bh0 = bb("h0"); Bhs = bb("hs")
    BxT = bb("xT"); Bqn = bb("qn"); Bsqt = bb("sqt"); Brq = bb("rq"); Bet = [bb("et0"), bb("et1")]; Bpt = [bb("pt0"), bb("pt1")]
    Brden = bb("rden"); Boatt = bb("oatt"); Bwgu = [bb("wgu%d" % i) for i in range(NGU)]; Bwdn = [bb("wdn%d" % i) for i in range(NDN)]
    Bsgb = bb("sgb"); Bkvo = bb("kvo"); BkTf = bb("kTf"); BsqtB = bb("sqtB"); BrqB = bb("rqB"); Bsq8 = [bb("sq8a"), bb("sq8b")]
    CACHE_B = Bwgu + Bwdn
    xslot = [K.slot("x0"), K.slot("x1")]
    yslot = [K.slot("y0"), K.slot("y1")]
    Bost = [bb("ost0"), bb("ost1")]
    wgslot = [K.slot("wg%d" % i) for i in range(NGU)]
    wdslot = [K.slot("wd%d" % i) for i in range(NDN)]
    oslot = K.slot("out")
    cslot = K.slot("cache")
    stslot = K.slot("stout")
    out_slots.append(oslot)

    def tile_src(gt):
        if gt == 33:
            return xs[:, :]
        return xp[(gt - 1) * 128:gt * 128, :]

    def load_x(gt, par):
        if gt == 0:
            dve(lambda: V.memset(xst[par], 0.0), W=[Bx[par]])
            K.dma(SP, lambda: S.dma_start(out=xst[par][112:128, :], in_=meta[:, :]), xslot[par], W=[Bx[par]])
        else:
            K.dma(SP, lambda: S.dma_start(out=xst[par], in_=tile_src(gt)), xslot[par], W=[Bx[par]])

    XB = [(5, 6, 7), (3, 4, 2)]

    def x_stage1(gt, par, tl):
        ba, bb_, br = XB[tl % 2]
        act(lambda: A.activation(out=junk, in_=xst[par], func=AF.Square, accum_out=ssq[:, tl:tl + 1]), R=[Bx[par]], W=[Bjunk, Bssq])
        act(lambda: A.activation(out=rst[:, tl:tl + 1], in_=ssq[:, tl:tl + 1], func=AF.Ln, scale=1.0 / D, bias=EPS_c[:, 0:1]), R=[Bssq, consts], W=[Brst])
        act(lambda: A.activation(out=rst[:, tl:tl + 1], in_=rst[:, tl:tl + 1], func=AF.Exp, scale=-0.5), R=[Brst], W=[Brst])
        dve(lambda: V.tensor_scalar(out=diag, in0=ident, scalar1=rst[:, tl:tl + 1], scalar2=None, op0=ALU.mult), R=[Brst, consts], W=[Bdiag])
        for half, bk2 in enumerate((ba, bb_)):
            pe([lambda k=k, bk2=bk2: T.matmul(ps[:, bk2, (k % 4) * 128:(k % 4 + 1) * 128], lhsT=xst[par][:, k * 128:(k + 1) * 128], rhs=ident,
                                              is_transpose=True, start=True, stop=True)
                for k in range(4 * half, 4 * half + 4)], R=[Bx[par], consts], W=[pb[bk2]])
        pe(lambda: T.matmul(ps[:, br, 0:128], lhsT=ones_f, rhs=diag, start=True, stop=True), R=[Bdiag, consts], W=[pb[br]])

    def x_stage2(gt, par, tl, want_xT):
        ba, bb_, br = XB[tl % 2]
        act(lambda: A.copy(out=rbc[:, tl * 128:(tl + 1) * 128], in_=ps[:, br, 0:128]), R=[pb[br]], W=[Brbc])
        for half, bk2 in enumerate((ba, bb_)):
            if want_xT:
                dve(lambda half=half, bk2=bk2: V.tensor_copy(out=xT[:, 4 * half:4 * half + 4, tl * 128:(tl + 1) * 128],
                                                             in_=ps[:, bk2, :].rearrange("p (a b) -> p a b", a=4)), R=[pb[bk2]], W=[BxT])
            dve(lambda half=half, bk2=bk2: V.tensor_tensor(
                out=xnT[:, 4 * half:4 * half + 4, tl * 128:(tl + 1) * 128], in0=ps[:, bk2, :].rearrange("p (a b) -> p a b", a=4),
                in1=rbc[:, tl * 128:(tl + 1) * 128].unsqueeze(1).to_broadcast([128, 4, 128]), op=ALU.mult),
                R=[pb[bk2], Brbc], W=BxnTk[4 * half:4 * half + 4])

    XSEQ = []
    for s_i in range(5):
        seg_t = list(range(8 * s_i, 8 * s_i + 8)) if s_i < 4 else [32, 33]
        XSEQ += seg_t
        XSEQ += seg_t
    xq = {"issued": 0, "consumed": 0}

    def _x_fetch():
        while xq["issued"] < min(len(XSEQ), xq["consumed"] + 2):
            i = xq["issued"]
            load_x(XSEQ[i], i % 2)
            xq["issued"] = i + 1

    pend = {"barrier": False}

    def barrier_if_pending():
        if pend["barrier"]:
            K.barrier()
            pend["barrier"] = False

    def run_x_tiles(tiles, want_xT, pre_stage2=None):
        pars = []
        hook = [pre_stage2]

        def st2(j):
            if hook[0] is not None:
                hook[0]()
                hook[0] = None
            x_stage2(tiles[j], pars[j], j, want_xT)

        for i, gt in enumerate(tiles):
            c = xq["consumed"]
            assert XSEQ[c] == gt, (c, XSEQ[c], gt)
            _x_fetch()
            xq["consumed"] = c + 1
            pars.append(c % 2)
            x_stage1(gt, c % 2, i)
            _x_fetch()
            if i >= 1:
                st2(i - 1)
        st2(len(tiles) - 1)

    def phase_A(seg_tiles, is_last, post_on_dve=False):
        ntile = len(seg_tiles)
        Nc = 16 * ntile
        for h0_ in range(0, ntile, 4):
            tl_tiles = seg_tiles[h0_:h0_ + 4]
            TT = 128 * len(tl_tiles)
            run_x_tiles(tl_tiles, False)
            for o in range(4):
                bk = o % 4
                pe([lambda k=k, o=o, bk=bk, TT=TT: T.matmul(ps[:, bk, 0:TT], lhsT=winb[:, k, 896 + o * 128:896 + (o + 1) * 128],
                                                            rhs=xnT[:, k, 0:TT], start=(k == 0), stop=(k == 7)) for k in range(8)],
                   R=BxnTk + [Bwin], W=[pb[bk]])
                c0 = h0_ * 16
                ncl = TT // 8
                barrier_if_pending()
                act(lambda o=o, bk=bk, TT=TT, c0=c0, ncl=ncl: A.copy(out=useg[:, o, :, c0:c0 + ncl],
                                                                   in_=ps[:, bk, 0:TT].rearrange("p (c j) -> p j c", j=8)),
                    R=[pb[bk]], W=[Buseg])
        npr = 16 if is_last else Nc
        if is_last:
            for ri, (src, dst) in enumerate(((sre, h0r), (sim, h0i))):
                for hh in range(2):
                    K.dma(SP, lambda src=src, hh=hh: S.dma_start(out=hs_st[:, 0:128], in_=src[hh * 128:(hh + 1) * 128, :]), cslot, W=[Bhs])
                    bk = 4 + hh
                    pe(lambda bk=bk: T.matmul(ps[:, bk, 0:128], lhsT=hs_st[:, 0:128], rhs=ident, is_transpose=True, start=True, stop=True), R=[Bhs, consts], W=[pb[bk]])
                    act(lambda bk=bk, dst=dst, hh=hh: A.copy(out=dst[:, :, hh * 8:(hh + 1) * 8],
                                                           in_=ps[:, bk, 0:128].rearrange("p (s P) -> p P s", s=8)), R=[pb[bk]], W=[Bh0])
        for hf in range(2):
            for o in (2 * hf, 2 * hf + 1):
                o2 = o % 2
                for part in range(2):
                    sl = part * 2 + o2
                    grp = []
                    for i in range(8):
                        for pl in range(4):
                            grp.append(lambda i=i, o=o, pl=pl, part=part, sl=sl: T.matmul(
                                ps[:, pl, sl * 128:sl * 128 + Nc], lhsT=BL[32 * pl:32 * pl + 32, o, i, part, :],
                                rhs=useg[32 * pl:32 * pl + 32, o, i, 0:Nc], start=(i == 0), stop=(i == 7), tile_position=(32 * pl, 0)))
                    pe(grp, R=[Buseg] + SSMW, W=[pb[0], pb[1], pb[2], pb[3]])
            def sview(part):
                return ps[:, 0:4, :].rearrange("p b (s m) -> p s b m", s=4)[:, part * 2:part * 2 + 2, :, 0:Nc]
            def tabv(tb_, lo, n_):
                return tb_[:, 8 * hf:8 * hf + 8, lo:lo + n_].rearrange("p (a b) m -> p a b m", a=2)
            def v8(x, n_):
                return x[:, 0:8 * n_].rearrange("p (a b m) -> p a b m", a=2, b=4)
            PSB = [pb[0], pb[1], pb[2], pb[3]]
            def rot_pre(n_, lo, col0, bc=False):
                def tv(tb_):
                    if bc:
                        return tb_[:, 8 * hf:8 * hf + 8, lo:lo + 1].rearrange("p (a b) m -> p a b m", a=2).to_broadcast([128, 2, 4, n_])
                    return tabv(tb_, lo, n_)
                Sr = sview(0)[:, :, :, col0:col0 + n_]; Si = sview(1)[:, :, :, col0:col0 + n_]
                a1 = ps[:, 4:6, :].rearrange("p b c -> p (b c)")[:, 0:8 * n_].rearrange("p (a b m) -> p a b m", a=2, b=4)
                a2 = v8(rt2, n_)
                zrv = v8(zr, Nc)[:, :, :, col0:col0 + n_] if False else zr[:, 0:8 * Nc].rearrange("p (a b m) -> p a b m", a=2, b=4)[:, :, :, col0:col0 + n_]
                ziv = zi[:, 0:8 * Nc].rearrange("p (a b m) -> p a b m", a=2, b=4)[:, :, :, col0:col0 + n_]
                PT = [pb[4], pb[5]]
                dve(lambda: V.tensor_tensor(out=a1, in0=Sr, in1=tv(cs), op=ALU.mult), R=PSB + [Btab], W=PT)
                dve(lambda: V.tensor_tensor(out=a2, in0=Si, in1=tv(sn), op=ALU.mult), R=PSB + [Btab], W=[Brt])
                dve(lambda: V.tensor_tensor(out=zrv, in0=a1, in1=a2, op=ALU.add), R=[Brt] + PT, W=[Bz])
                dve(lambda: V.tensor_tensor(out=a1, in0=Si, in1=tv(cs), op=ALU.mult), R=PSB + [Btab], W=PT)
                dve(lambda: V.tensor_tensor(out=a2, in0=Sr, in1=tv(sn), op=ALU.mult), R=PSB + [Btab], W=[Brt])
                dve(lambda: V.tensor_tensor(out=ziv, in0=a1, in1=a2, op=ALU.subtract), R=[Brt] + PT, W=[Bz])
            rot_pre(npr, 0, 0)
            if is_last:
                rot_pre(16, 0, 16, bc=True)
            def zv(x):
                return x[:, 0:8 * Nc].rearrange("p (l m) -> p l m", l=8)
            for lp in range(8):
                P_ = 8 * hf + lp
                for (zsrc, gdst, hl) in ((zr, gr, hl_r), (zi, gi, hl_i)):
                    dve(lambda lp=lp, P_=P_, zsrc=zsrc, gdst=gdst, hl=hl: V.tensor_tensor_scan(
                        out=zv(gdst)[:, lp, 0:npr], data0=rho8[:, P_:P_ + 1].to_broadcast([128, npr]), data1=zv(zsrc)[:, lp, 0:npr],
                        initial=hl[:, P_:P_ + 1], op0=ALU.mult, op1=ALU.add), R=[Bz, Bhl, consts], W=[Bg])
                    if is_last:
                        h0 = h0r if hl is hl_r else h0i
                        dve(lambda lp=lp, P_=P_, zsrc=zsrc, gdst=gdst, h0=h0: V.scalar_tensor_tensor(
                            out=zv(gdst)[:, lp, 16:32], in0=h0[:, P_, :], scalar=rho8[:, P_:P_ + 1], in1=zv(zsrc)[:, lp, 16:32],
                            op0=ALU.mult, op1=ALU.add), R=[Bz, Bh0, consts], W=[Bg])
            def rot_post(n_, lo, col0, bc=False):
                def tv(tb_):
                    if bc:
                        return tb_[:, 8 * hf:8 * hf + 8, lo:lo + 1].to_broadcast([128, 8, n_])
                    return tb_[:, 8 * hf:8 * hf + 8, lo:lo + n_]
                grv = zv(gr)[:, :, col0:col0 + n_]; giv = zv(gi)[:, :, col0:col0 + n_]
                hr_ = hb_r[:, :, 1 + col0:1 + col0 + n_]; hi_ = hb_i[:, :, 1 + col0:1 + col0 + n_]
                b1 = ps[:, 6:8, :].rearrange("p b c -> p (b c)")[:, 0:8 * n_].rearrange("p (l m) -> p l m", l=8)
                b2 = zi[:, 0:8 * n_].rearrange("p (l m) -> p l m", l=8)
                PT2 = [pb[6], pb[7]]
                dve(lambda: V.tensor_tensor(out=b1, in0=grv, in1=tv(cs), op=ALU.mult), R=[Bg, Btab], W=PT2)
                dve(lambda: V.tensor_tensor(out=b2, in0=giv, in1=tv(sn), op=ALU.mult), R=[Bg, Btab], W=[Bz])
                dve(lambda: V.tensor_tensor(out=hr_, in0=b1, in1=b2, op=ALU.subtract), R=[Bz] + PT2, W=[Bhb])
                dve(lambda: V.tensor_tensor(out=b1, in0=giv, in1=tv(cs), op=ALU.mult), R=[Bg, Btab], W=PT2)
                dve(lambda: V.tensor_tensor(out=b2, in0=grv, in1=tv(sn), op=ALU.mult), R=[Bg, Btab], W=[Bz])
                dve(lambda: V.tensor_tensor(out=hi_, in0=b1, in1=b2, op=ALU.add), R=[Bz] + PT2, W=[Bhb])
            dve(lambda: V.tensor_copy(out=hb_r[:, :, 0], in_=hl_r[:, 8 * hf:8 * hf + 8]), R=[Bhl], W=[Bhb])
            dve(lambda: V.tensor_copy(out=hb_i[:, :, 0], in_=hl_i[:, 8 * hf:8 * hf + 8]), R=[Bhl], W=[Bhb])
            rot_post(npr, 0, 0)
            if is_last:
                rot_post(16, 0, 16, bc=True)
            dve(lambda: V.tensor_copy(out=hl_r[:, 8 * hf:8 * hf + 8], in_=hb_r[:, :, npr]), R=[Bhb], W=[Bhl])
            dve(lambda: V.tensor_copy(out=hl_i[:, 8 * hf:8 * hf + 8], in_=hb_i[:, :, npr]), R=[Bhb], W=[Bhl])
            act(lambda: A.copy(out=hprev[:, 8 * hf:8 * hf + 8, 0, 0:npr], in_=hb_r[:, :, 0:npr]), R=[Bhb], W=[Bhp])
            act(lambda: A.copy(out=hprev[:, 8 * hf:8 * hf + 8, 1, 0:npr], in_=hb_i[:, :, 0:npr]), R=[Bhb], W=[Bhp])
            if is_last:
                act(lambda: A.copy(out=hprev[:, 8 * hf:8 * hf + 8, 0, 16:32], in_=h0r[:, 8 * hf:8 * hf + 8, :]), R=[Bh0], W=[Bhp])
                act(lambda: A.copy(out=hprev[:, 8 * hf:8 * hf + 8, 1, 16:32], in_=h0i[:, 8 * hf:8 * hf + 8, :]), R=[Bh0], W=[Bhp])
                dve(lambda: V.tensor_copy(out=hsr[:, 8 * hf:8 * hf + 8, :], in_=hb_r[:, :, 17:33]), R=[Bhb], W=[Bhs])
                dve(lambda: V.tensor_copy(out=hsi[:, 8 * hf:8 * hf + 8, :], in_=hb_i[:, :, 17:33]), R=[Bhb], W=[Bhs])
        for o in range(4):
            banks = [pb[4], pb[5]]
            grp = []
            merged = (Nc == 128)
            if merged:
                for tau in range(8):
                    for jb in range(2):
                        jlo = max(tau, 4 * jb)
                        jhi = 4 * jb + 4
                        if jlo >= jhi:
                            continue
                        grp.append(lambda tau=tau, jb=jb, jlo=jlo, jhi=jhi: T.matmul(
                            ps[:, 4 + jb, (jlo - 4 * jb) * 128:512], lhsT=KT[:, o, tau, :],
                            rhs=useg[:, o, jlo - tau:jhi - tau, :].rearrange("p a b -> p (a b)"), start=(tau == 0), stop=False))
            for j in range(8):
                bk = 4 + j // 4
                sl = j % 4
                dst = ps[:, bk, sl * 128:sl * 128 + Nc]
                for tau in range(j + 1):
                    if merged:
                        break
                    grp.append(lambda dst=dst, tau=tau, j=j: T.matmul(dst, lhsT=KT[:, o, tau, :], rhs=useg[:, o, j - tau, 0:Nc],
                                                                      start=(tau == 0), stop=False))
                for pl in range(4):
                    P_ = 4 * o + pl
                    for part in range(2):
                        last = (pl == 3 and part == 1) and ((not merged) or j in (3, 7))
                        grp.append(lambda bk=bk, sl=sl, pl=pl, P_=P_, part=part, j=j, last=last: T.matmul(
                            ps[32 * pl:32 * pl + 32, bk, sl * 128:sl * 128 + Nc], lhsT=CL[:, j, part, 32 * P_:32 * P_ + 32],
                            rhs=hprev[:, P_, part, 0:Nc], start=False, stop=last, tile_position=(0, 32 * pl)))
            pe(grp, R=[Buseg, Bhp] + SSMW, W=banks)
            yv = ps[:, 4:6, :].rearrange("p b (s m) -> p (b s) m", s=4)[:, :, 0:Nc]
            Gv = Gt[:, 0:8 * Nc].rearrange("p (j m) -> p j m", j=8)
            act(lambda yv=yv, Gv=Gv: A.activation(out=Gv, in_=yv, func=AF.Gelu_apprx_tanh), R=banks, W=[BGt])
            tot = 8 * Nc
            pieces = [(0, min(512, tot))] + ([(512, tot)] if tot > 512 else [])
            gb = [pb[6], pb[7]]
            for pi, (lo, hi) in enumerate(pieces):
                pe(lambda lo=lo, hi=hi, pi=pi: T.matmul(ps[:, 6 + pi, 0:hi - lo], lhsT=glub[:, o, :], rhs=Gt[:, lo:hi], start=True, stop=True),
                   R=[BGt, Bglu], W=[gb[pi]])
                act(lambda lo=lo, hi=hi, pi=pi: A.activation(out=sg_[:, lo:hi], in_=ps[:, 6 + pi, 0:hi - lo], func=AF.Sigmoid,
                                                             bias=bglu_c[:, o:o + 1]), R=[gb[pi], Bpar], W=[Bsg])
            dve(lambda o=o, Gv=Gv: V.tensor_tensor(out=ossm[:, o, 0:8 * Nc].rearrange("p (m j) -> p j m", j=8), in0=Gv,
                                                   in1=sg_[:, 0:8 * Nc].rearrange("p (j m) -> p j m", j=8), op=ALU.mult),
                R=[BGt, Bsg], W=[Bossm])

    NMT = 9
    GU_TOTAL = NF * NMT
    DN_TOTAL = 16 * NMT
    wq = {"gu_i": 0, "gu_c": 0, "dn_i": 0, "dn_c": 0, "gu_lim": 0, "dn_lim": 0, "mt": 0}

    def _issue_gu():
        i = wq["gu_i"]
        f_ = i % NF
        par = i % NGU
        wq["gu_i"] = i + 1
        for gi_, wsrc in enumerate((sgate, sup)):
            K.dma(SP, lambda gi_=gi_, wsrc=wsrc, par=par, f_=f_: S.dma_start(
                out=wgu[par][:, gi_, :, :].rearrange("p k c -> p (k c)"), in_=wsrc[f_]), wgslot[par], R=[Bscr], W=[Bwgu[par]])

    def _issue_dn():
        i = wq["dn_i"]
        par = i % NDN
        wq["dn_i"] = i + 1
        K.dma(SP, lambda par=par, i=i: S.dma_start(out=wdn[par].rearrange("p f c -> p (f c)"), in_=sdown[i % 16]),
              wdslot[par], R=[Bscr], W=[Bwdn[par]])

    def use_gu():
        conv_issue(100)
        if Bscr.w is not None and Bscr.w[0] is cvslot:
            Bscr.w = (cvslot, cvslot.n)
        c = wq["gu_c"]
        while wq["gu_i"] < min(wq["gu_lim"], c + NGU):
            _issue_gu()
        wq["gu_c"] = c + 1
        return c % NGU

    def use_dn():
        c = wq["dn_c"]
        while wq["dn_i"] < min(wq["dn_lim"], c + NDN):
            _issue_dn()
        wq["dn_c"] = c + 1
        return c % NDN

    def prefetch_w():
        while wq["gu_i"] < min(wq["gu_lim"], wq["gu_c"] + NGU):
            _issue_gu()
        while wq["dn_i"] < min(wq["dn_lim"], wq["dn_c"] + NDN):
            _issue_dn()

    def rstd_from_ms(bk, TT, dst, Rb, Wb):
        act(lambda: A.activation(out=dst[:, 0:TT], in_=ps[:, bk, 0:TT], func=AF.Ln, bias=EPS_c[:, 0:1]), R=[pb[bk]] + Rb, W=Wb)
        act(lambda: A.activation(out=dst[:, 0:TT], in_=dst[:, 0:TT], func=AF.Exp, scale=-0.5), R=Wb, W=Wb)

    def rstd_in_psum(bk, TT):
        act(lambda: A.activation(out=rq[:, 0:TT], in_=ps[:, bk, 0:TT], func=AF.Ln, bias=EPS_c[:, 0:1]), R=[pb[bk]], W=[Brq])
        act(lambda: A.activation(out=ps[:, bk, 0:TT], in_=rq[:, 0:TT], func=AF.Exp, scale=-0.5), R=[Brq], W=[pb[bk]])

    def phase_B(tiles, seg_off, last_in_seg):
        nt = len(tiles)
        TT = 128 * nt
        wq["mt"] += 1
        wq["gu_lim"] = wq["mt"] * NF if last_in_seg else min(GU_TOTAL, (wq["mt"] + 1) * NF)
        wq["dn_lim"] = wq["mt"] * 16 if last_in_seg else min(DN_TOTAL, (wq["mt"] + 1) * 16)
        has_sample = (33 in tiles)

        def _pre():
            barrier_if_pending()
            if not has_sample:
                prefetch_w()

        run_x_tiles(tiles, True, pre_stage2=_pre)
        milestone('B.x')
        def proj(mt):
            pe([lambda k=k: T.matmul(ps[:, mt, 0:TT], lhsT=winb[:, k, mt * 128:(mt + 1) * 128], rhs=xnT[:, k, 0:TT],
                                     start=(k == 0), stop=(k == 7)) for k in range(8)], R=BxnTk + [Bwin], W=[pb[mt]])
        grp = []
        for tl in range(nt):
            for k in range(8):
                grp.append(lambda tl=tl, k=k: T.matmul(ps[:, 6, tl * 128:(tl + 1) * 128], lhsT=xnT[:, k, tl * 128:(tl + 1) * 128],
                                                       rhs=winb[:, k, 768:896], start=(k == 0), stop=(k == 7)))
        pe(grp, R=BxnTk + [Bwin], W=[pb[6]])
        act(lambda: A.copy(out=vtok[:, 1:1 + nt, :], in_=ps[:, 6, 0:TT].rearrange("p (t c) -> p t c", t=nt)), R=[pb[6]], W=[Bvt])
        for tl, gt in enumerate(tiles):
            if gt in (32, 33):
                act(lambda tl=tl: A.copy(out=kvo[:, 1, :], in_=ps[:, 6, tl * 128:(tl + 1) * 128]), R=[pb[6]], W=[Bkvo])
                if gt == 32:
                    K.dma(SP, lambda: S.dma_start(out=vwp[:, :], in_=kvo[:, 1, :]), oslot, R=[Bkvo])
                else:
                    for s_ in range(16):
                        K.dma(SP, lambda s_=s_: S.dma_start(out=vws[s_, 120:128, :], in_=kvo[8 * s_:8 * s_ + 8, 1, :]), oslot, R=[Bkvo])
        milestone('B.proj')
        sq_b = [sqt, sqtB]; rq_b = [rq, rqB]; Bsq_b = [[Bsqt], [BsqtB]]; Brq_b = [[Brq], [BrqB]]

        def qk_A(mt):
            pp = mt % 2
            act(lambda: A.activation(out=sq_b[pp][:, 0:TT], in_=ps[:, mt, 0:TT], func=AF.Square), R=[pb[mt]], W=Bsq_b[pp])
            pe(lambda: T.matmul(ps[:, 6 + pp, 0:TT], lhsT=blk64, rhs=sq_b[pp][:, 0:TT], start=True, stop=True), R=Bsq_b[pp] + [consts], W=[pb[6 + pp]])

        def qk_B(mt):
            pp = mt % 2
            rqc = rq_b[pp]
            rstd_from_ms(6 + pp, TT, rqc, [], Brq_b[pp])
            if mt < 4:
                dve(lambda: V.scalar_tensor_tensor(out=qn[:, mt, 0:TT], in0=ps[:, mt, 0:TT], scalar=gq_c[:, 0:1], in1=rqc[:, 0:TT],
                                                   op0=ALU.mult, op1=ALU.mult), R=[pb[mt], Bpar] + Brq_b[pp], W=[Bqn])
                return
            kv = mt - 4
            dve(lambda: V.scalar_tensor_tensor(out=kdupT[:, kv, 128:128 + TT], in0=ps[:, mt, 0:TT], scalar=gk_c[:, 0:1],
                                               in1=rqc[:, 0:TT], op0=ALU.mult, op1=ALU.mult), R=[pb[mt], Bpar] + Brq_b[pp], W=[Bkd])
            for tl, gt in enumerate(tiles):
                if gt in (32, 33):
                    lo, hi = kv * 64, kv * 64 + 64
                    dve(lambda tl=tl, lo=lo, hi=hi: V.scalar_tensor_tensor(
                        out=kTf[lo:hi, tl % 2, :], in0=ps[lo:hi, mt, tl * 128:(tl + 1) * 128], scalar=gk_c[lo:hi, 0:1],
                        in1=rqc[lo:hi, tl * 128:(tl + 1) * 128], op0=ALU.mult, op1=ALU.mult), R=[pb[mt], Bpar] + Brq_b[pp], W=[BkTf])
                    if kv == 1:
                        pe(lambda tl=tl: T.matmul(ps[:, 0, 0:128], lhsT=kTf[:, tl % 2, :], rhs=ident, is_transpose=True, start=True, stop=True),
                           R=[BkTf, consts], W=[pb[0]])
                        act(lambda: A.copy(out=kvo[:, 0, :], in_=ps[:, 0, 0:128]), R=[pb[0]], W=[Bkvo])
                        if gt == 32:
                            K.dma(SP, lambda: S.dma_start(out=kwp[:, :], in_=kvo[:, 0, :]), oslot, R=[Bkvo])
                        else:
                            for s_ in range(16):
                                K.dma(SP, lambda s_=s_: S.dma_start(out=kws[s_, 120:128, :], in_=kvo[8 * s_:8 * s_ + 8, 0, :]), oslot, R=[Bkvo])

        act(lambda: A.activation(out=sq8[:, 4:8, 0:TT], in_=ossm[:, :, seg_off:seg_off + TT], func=AF.Square), R=[Bossm], W=[Bsq8[1], BaT])
        proj(0); proj(1); qk_A(0); proj(2); qk_A(1)
        for mt in range(6):
            qk_B(mt)
            if mt + 3 < 6:
                proj(mt + 3)
            if mt + 2 < 6:
                qk_A(mt + 2)
        pe([lambda k=k: T.matmul(ps[:, 6, 0:TT], lhsT=onesA, rhs=sq8[:, 4 + k, 0:TT], start=(k == 0), stop=(k == 3)) for k in range(4)],
           R=[Bsq8[1], consts], W=[pb[6]])
        rstd_in_psum(6, TT)
        for k in range(4):
            dve(lambda k=k: V.scalar_tensor_tensor(out=xnT[:, 4 + k, 0:TT], in0=ossm[:, k, seg_off:seg_off + TT], scalar=gssm_c[:, k:k + 1],
                                                   in1=ps[:, 6, 0:TT], op0=ALU.mult, op1=ALU.mult), R=[Bossm, pb[6], Bpar], W=[BxnTk[4 + k]])
        if tiles[0] == 0:
            dump('qn', qn, [Bqn]); dump('kdupT', kdupT, [Bkd]); dump('vtok', vtok, [Bvt]); dump('xT0', xT, [BxT]); dump('xnT_B', xnT, BxnTk)
        milestone('B.qknorm')
        et5 = et.rearrange("p (par t b) q -> p par t b q", par=2, t=4)
        pt5 = pt.rearrange("p (par t b) q -> p par t b q", par=2, t=4)
        units = [(tl, kvg) for tl in range(nt) for kvg in range(2)]

        def masks_for(gt):
            samp = (gt == 33)
            mk_cur = m0cur if gt == 0 else (msamp if samp else mcur)
            mk_prev = mzero if gt == 0 else (mcache if samp else (m1prev if gt == 1 else mprev))
            return mk_prev, mk_cur

        def att_S(tl, kvg):
            gt = tiles[tl]
            samp = (gt == 33)
            if samp and kvg == 0:
                sample_cache_prep()
            grp = []
            for h in range(4 * kvg, 4 * kvg + 4):
                par = h % 2
                base = par * 64
                kv = kvg
                hp = par * 4 + h // 2
                for blk in range(2):
                    idx = hp * 2 + blk
                    kc0 = tl * 128 + blk * 128
                    if samp and blk == 0:
                        var = 0 if kv == par else 1
                        for s_ in range(16):
                            grp.append(lambda var=var, idx=idx, s_=s_, h=h, base=base: T.matmul(
                                ps[:, idx // 4, (idx % 4) * 128 + s_ * 8:(idx % 4) * 128 + s_ * 8 + 8], lhsT=KcT[base:base + 64, s_, var, :],
                                rhs=qn[base:base + 64, h // 2, tl * 128 + s_ * 8:tl * 128 + s_ * 8 + 8], start=True, stop=True,
                                tile_position=(base, 0)))
                        continue
                    grp.append(lambda kv=kv, idx=idx, kc0=kc0, h=h, base=base: T.matmul(
                        ps[:, idx // 4, (idx % 4) * 128:(idx % 4 + 1) * 128], lhsT=kdupT[base:base + 64, kv, kc0:kc0 + 128],
                        rhs=qn[base:base + 64, h // 2, tl * 128:(tl + 1) * 128], start=True, stop=True, tile_position=(base, 0)))
            pe(grp, R=[Bqn, Bkd] + (CACHE_B if samp else []), W=[pb[kvg], pb[2 + kvg]])

        def att_EM(tl, kvg):
            gt = tiles[tl]
            mk_prev, mk_cur = masks_for(gt)
            for bk in (kvg, 2 + kvg):
                act(lambda bk=bk: A.activation(out=et[:, 4 * bk:4 * bk + 4, :], in_=ps[:, bk, :].rearrange("p (a b) -> p a b", a=4),
                                               func=AF.Exp, scale=SCALE), R=[pb[bk]], W=[Bet[kvg]])
            ts_ = slice(2 * kvg, 2 * kvg + 2)
            for blk, mk in ((0, mk_prev), (1, mk_cur)):
                dve(lambda blk=blk, mk=mk: V.tensor_tensor(out=pt5[:, :, ts_, blk, :], in0=et5[:, :, ts_, blk, :],
                                                           in1=mk.unsqueeze(1).unsqueeze(1).to_broadcast([128, 2, 2, 128]), op=ALU.mult),
                    R=[Bet[kvg], consts], W=[Bpt[kvg]])

        def att_P(tl, kvg):
            gt = tiles[tl]
            samp = (gt == 33)
            bo = 4 + 2 * (tl % 2)
            bd_ = bo + 1
            grp = []
            grp2 = []
            for h in range(4 * kvg, 4 * kvg + 4):
                par = h % 2
                base = par * 64
                kv = kvg
                hp = par * 4 + h // 2
                t2 = h // 2
                for blk in range(2):
                    idx = hp * 2 + blk
                    if samp and blk == 0:
                        for s_ in range(16):
                            grp.append(lambda kv=kv, idx=idx, t2=t2, s_=s_, base=base: T.matmul(
                                ps[base:base + 64, bo, t2 * 128 + s_ * 8:t2 * 128 + s_ * 8 + 8], lhsT=Vc[:, s_, kv * 64:(kv + 1) * 64],
                                rhs=pt[:, idx, s_ * 8:s_ * 8 + 8], start=(s_ == 0), stop=False, tile_position=(0, base)))
                    else:
                        grp.append(lambda kv=kv, idx=idx, t2=t2, blk=blk, base=base: T.matmul(
                            ps[base:base + 64, bo, t2 * 128:(t2 + 1) * 128], lhsT=vtok[:, tl + blk, kv * 64:(kv + 1) * 64], rhs=pt[:, idx, :],
                            start=(blk == 0), stop=(blk == 1), tile_position=(0, base)))
                    grp2.append(lambda idx=idx, t2=t2, blk=blk, base=base: T.matmul(
                        ps[base:base + 64, bd_, t2 * 128:(t2 + 1) * 128], lhsT=ones_b[:, 0:64], rhs=pt[:, idx, :],
                        start=(blk == 0), stop=(blk == 1), tile_position=(0, base)))
            mix_ = []
            for a_, b_ in zip(grp, grp2):
                mix_ += [a_, b_]
            mix_ += grp[len(grp2):]
            pe(mix_ if not samp else grp + grp2, R=[Bpt[kvg], Bvt, consts] + (CACHE_B if samp else []), W=[pb[bo], pb[bd_]])

        def att_F(tl):
            bo = 4 + 2 * (tl % 2)
            bd_ = bo + 1
            for t2 in range(4):
                act(lambda t2=t2: A.activation(out=rden[:, t2, :], in_=ps[:, bd_, t2 * 128:(t2 + 1) * 128], func=AF.Ln, bias=esk[:, t2:t2 + 1]),
                    R=[pb[bd_], Bpar], W=[Brden])
            act(lambda: A.activation(out=rden, in_=rden, func=AF.Exp, scale=-1.0), R=[Brden], W=[Brden])
            dve(lambda: V.tensor_tensor(out=oatt[:, :, tl * 128:(tl + 1) * 128], in0=ps[:, bo, :].rearrange("p (a b) -> p a b", a=4),
                                        in1=rden, op=ALU.mult), R=[pb[bo], Brden], W=[Boatt] + Bost)
            pool(lambda: G.tensor_tensor(out=sq8[:, 0:4, tl * 128:(tl + 1) * 128], in0=oatt[:, :, tl * 128:(tl + 1) * 128],
                                         in1=oatt[:, :, tl * 128:(tl + 1) * 128], op=ALU.mult), R=[Boatt], W=[Bsq8[0]])

        nu = len(units)
        att_S(*units[0])
        if nu > 1:
            att_S(*units[1])
        att_EM(*units[0])
        for ui, (tl, par) in enumerate(units):
            if ui + 2 < nu:
                att_S(*units[ui + 2])
            if ui + 1 < nu:
                att_EM(*units[ui + 1])
            att_P(tl, par)
            if par == 1:
                att_F(tl)
        if tiles[0] == 0:
            dump('oatt', oatt, [Boatt]); dump('rden', rden, [Brden])
        if has_sample:
            prefetch_w()
        milestone('B.att')
        dve(lambda: V.tensor_copy(out=kdupT[:, :, 0:128], in_=kdupT[:, :, TT:TT + 128]), R=[Bkd], W=[Bkd])
        dve(lambda: V.tensor_copy(out=vtok[:, 0, :], in_=vtok[:, nt, :]), R=[Bvt], W=[Bvt])
        pe([lambda k=k: T.matmul(ps[:, 6, 0:TT], lhsT=onesA, rhs=sq8[:, k, 0:TT], start=(k == 0), stop=(k == 3)) for k in range(4)],
           R=[Bsq8[0], consts], W=[pb[6]])
        rstd_in_psum(6, TT)
        for k in range(4):
            dve(lambda k=k: V.scalar_tensor_tensor(out=xnT[:, k, 0:TT], in0=oatt[:, k, 0:TT], scalar=gatt_c[:, k:k + 1], in1=ps[:, 6, 0:TT],
                                                   op0=ALU.mult, op1=ALU.mult), R=[Boatt, pb[6], Bpar], W=[BxnTk[k]])
        if tiles[0] == 0:
            dump('mix', xnT, BxnTk)
        milestone('B.mix')
        for m in range(8):
            bk = m % 4
            if m == 0:
                for i_, k in enumerate((4, 5, 6, 7, 0, 1, 2, 3)):
                    pe(lambda k=k, m=m, bk=bk, i_=i_: T.matmul(ps[:, bk, 0:TT], lhsT=woutb[:, k, m * 128:(m + 1) * 128], rhs=xnT[:, k, 0:TT],
                                                               start=(i_ == 0), stop=(i_ == 7)), R=[BxnTk[k], Bwout], W=[pb[bk]])
            else:
                pe([lambda k=k, m=m, bk=bk: T.matmul(ps[:, bk, 0:TT], lhsT=woutb[:, k, m * 128:(m + 1) * 128], rhs=xnT[:, k, 0:TT],
                                                     start=(k == 0), stop=(k == 7)) for k in range(8)], R=BxnTk + [Bwout], W=[pb[bk]])
            dve(lambda m=m, bk=bk: V.tensor_tensor(out=xT[:, m, 0:TT], in0=ps[:, bk, 0:TT], in1=xT[:, m, 0:TT], op=ALU.add), R=[pb[bk], BxT], W=[BxT])
            act(lambda m=m: A.activation(out=sq8[:, m, 0:TT], in_=xT[:, m, 0:TT], func=AF.Square), R=[BxT], W=[Bsq8[m // 4]])
        if tiles[0] == 0:
            dump('hT', xT, [BxT])
        milestone('B.wout')
        pe([lambda k=k: T.matmul(ps[:, 6, 0:TT], lhsT=onesF, rhs=sq8[:, k, 0:TT], start=(k == 0), stop=(k == 7)) for k in range(8)],
           R=Bsq8 + [consts], W=[pb[6]])
        rstd_in_psum(6, TT)
        for k in range(8):
            dve(lambda k=k: V.scalar_tensor_tensor(out=xnT[:, k, 0:TT], in0=xT[:, k, 0:TT], scalar=gffn_c[:, k:k + 1], in1=ps[:, 6, 0:TT],
                                                   op0=ALU.mult, op1=ALU.mult), R=[BxT, pb[6], Bpar], W=[BxnTk[k]])
        if tiles[0] == 0:
            dump('fT', xnT, BxnTk)
        milestone('B.ffnnorm')
        for half in range(2):
            for fl in range(11):
                par = use_gu()
                bg, bu = (0, 1) if fl % 2 == 0 else (2, 3)
                if half == 0 and fl == 0:
                    for k in range(8):
                        pe(lambda k=k, par=par, bg=bg: T.matmul(ps[:, bg, 0:TT], lhsT=wgu[par][:, 0, k, :], rhs=xnT[:, k, 0:TT], start=(k == 0), stop=(k == 7)),
                           R=[BxnTk[k], Bwgu[par]], W=[pb[bg]])
                else:
                    pe([lambda k=k, par=par, bg=bg: T.matmul(ps[:, bg, 0:TT], lhsT=wgu[par][:, 0, k, :], rhs=xnT[:, k, 0:TT], start=(k == 0), stop=(k == 7))
                        for k in range(8)], R=BxnTk + [Bwgu[par]], W=[pb[bg]])
                pe([lambda k=k, par=par, bu=bu: T.matmul(ps[:, bu, 0:TT], lhsT=wgu[par][:, 1, k, :], rhs=xnT[:, k, 0:TT], start=(k == 0), stop=(k == 7))
                    for k in range(8)], R=BxnTk + [Bwgu[par]], W=[pb[bu]])
                act(lambda bg=bg: A.activation(out=sgb[:, 0:TT], in_=ps[:, bg, 0:TT], func=AF.Silu), R=[pb[bg]], W=[Bsgb])
                dve(lambda fl=fl, bu=bu: V.tensor_tensor(out=aT[:, fl, 0:TT], in0=ps[:, bu, 0:TT], in1=sgb[:, 0:TT], op=ALU.mult),
                    R=[pb[bu], Bsgb], W=[BaT, BsqtB, BrqB] + Bsq8)
            for m in range(8):
                par = use_dn()
                bk = 4 + (m % 2)
                pe([lambda fl=fl, par=par, bk=bk: T.matmul(ps[:, bk, 0:TT], lhsT=wdn[par][:, fl, :], rhs=aT[:, fl, 0:TT], start=(fl == 0), stop=(fl == 10))
                    for fl in range(11)], R=[BaT, Bwdn[par]], W=[pb[bk]])
                dve(lambda m=m, bk=bk: V.tensor_tensor(out=xT[:, m, 0:TT], in0=ps[:, bk, 0:TT], in1=xT[:, m, 0:TT], op=ALU.add), R=[pb[bk], BxT], W=[BxT])
        prefetch_w()
        if tiles[0] == 0:
            dump('yT', xT, [BxT])
        milestone('B.ffn')
        ost = [oatt[:, 0:2, :].rearrange("p a b -> p (a b)"), oatt[:, 2:4, :].rearrange("p a b -> p (a b)")]
        for tl, gt in enumerate(tiles):
            if gt == 0:
                continue
            par = tl % 2
            for half in range(2):
                bk = 6 + half
                pe([lambda m=m, bk=bk, tl=tl: T.matmul(ps[:, bk, (m % 4) * 128:(m % 4 + 1) * 128], lhsT=xT[:, m, tl * 128:(tl + 1) * 128], rhs=ident, is_transpose=True, start=True, stop=True)
                    for m in range(4 * half, 4 * half + 4)], R=[BxT, consts], W=[pb[bk]])
                if half == 0:
                    act(lambda half=half, bk=bk, par=par: A.copy(out=ost[par][:, 512 * half:512 * half + 512], in_=ps[:, bk, :]), R=[pb[bk]], W=[Bost[par]])
                else:
                    dve(lambda half=half, bk=bk, par=par: V.tensor_copy(out=ost[par][:, 512 * half:512 * half + 512], in_=ps[:, bk, :]), R=[pb[bk]], W=[Bost[par]])
            dst = ys[:, :] if gt == 33 else yp[(gt - 1) * 128:gt * 128, :]
            K.dma(SP, lambda dst=dst, par=par: S.dma_start(out=dst, in_=ost[par]), yslot[par], R=[Bost[par]])

    def sample_cache_prep():
        K.dma(POOL, lambda: G.dma_start(out=Vc, in_=cv.rearrange("s k c -> k s c")), cslot, W=CACHE_B)
        K.dma(SP, lambda: S.dma_start(out=kws[:, 0:120, :], in_=ck[:, 8:128, :]), oslot)
        K.dma(SP, lambda: S.dma_start(out=vws[:, 0:120, :], in_=cv[:, 8:128, :]), oslot)
        ckv = ck.rearrange("s k c -> k s c")
        for g8 in range(2):
            K.dma(SP, lambda g8=g8: S.dma_start(out=ckst[:, :, 0, :], in_=ckv[:, 8 * g8:8 * g8 + 8, :]), cslot, W=CACHE_B)
            K.dma(SP, lambda g8=g8: S.dma_start(out=ckst[:, :, 1, 0:64], in_=ckv[:, 8 * g8:8 * g8 + 8, 64:128]), cslot, W=CACHE_B)
            K.dma(SP, lambda g8=g8: S.dma_start(out=ckst[:, :, 1, 64:128], in_=ckv[:, 8 * g8:8 * g8 + 8, 0:64]), cslot, W=CACHE_B)
            for s8 in range(8):
                for var in range(2):
                    idx = s8 * 2 + var
                    bk = 6 + (idx // 4) % 2
                    pe(lambda s8=s8, var=var, bk=bk, idx=idx: T.matmul(ps[:, bk, (idx % 4) * 128:(idx % 4 + 1) * 128], lhsT=ckst[:, s8, var, :], rhs=ident, is_transpose=True, start=True, stop=True),
                       R=CACHE_B + [consts], W=[pb[bk]])
                    act(lambda s8=s8, var=var, bk=bk, idx=idx, g8=g8: A.copy(out=KcT[:, 8 * g8 + s8, var, :],
                                                                              in_=ps[:, bk, (idx % 4) * 128:(idx % 4 + 1) * 128]),
                        R=[pb[bk]], W=CACHE_B)

    EPS_c = ar.raw_f32(ARENA_WORDS - 1, 1)
    dve(lambda: V.memset(EPS_c, EPS), W=[consts])
    segs = [list(range(8 * s, 8 * s + 8)) for s in range(4)] + [[32, 33]]

    def _main():
        milestone("setup")
        for si, seg in enumerate(segs):
            is_last = (si == 4)
            phase_A(seg, is_last, post_on_dve=(si == 0))
            if is_last:
                for ri, (src_h, dsto) in enumerate(((hl_r, srp), (hl_i, sip))):
                    bk = 4 + ri
                    pe(lambda bk=bk, src_h=src_h: T.matmul(ps[0:16, bk, 0:128], lhsT=src_h, rhs=ident, is_transpose=True, start=True, stop=True), R=[Bhl, consts], W=[pb[bk]])
                    act(lambda bk=bk, ri=ri: A.copy(out=hs_st[0:16, ri * 128:(ri + 1) * 128], in_=ps[0:16, bk, 0:128]), R=[pb[bk]], W=[Bhs])
                    K.dma(SP, lambda dsto=dsto, ri=ri: S.dma_start(out=dsto[:, :], in_=hs_st[0:16, ri * 128:(ri + 1) * 128]), stslot, R=[Bhs])
                K.barrier(slots=[stslot])
                Bhs2 = bb("hs2")
                for ri, (src_h, dsto) in enumerate(((hsr, srs), (hsi, sis))):
                    for hh in range(2):
                        bk = 4 + hh
                        dve(lambda src_h=src_h, hh=hh: V.tensor_copy(out=rt1[:, 0:128].rearrange("p (s P) -> p P s", s=8),
                                                                      in_=src_h[:, :, hh * 8:(hh + 1) * 8]), R=[Bhs, Bhs2], W=[Brt])
                        pe(lambda bk=bk: T.matmul(ps[:, bk, 0:128], lhsT=rt1[:, 0:128], rhs=ident, is_transpose=True, start=True, stop=True), R=[Brt, consts], W=[pb[bk]])
                        act(lambda bk=bk: A.copy(out=rt2[:, 0:128], in_=ps[:, bk, 0:128]), R=[pb[bk]], W=[Bhs2])
                        K.dma(SP, lambda dsto=dsto, hh=hh: S.dma_start(out=dsto[hh * 128:(hh + 1) * 128, :], in_=rt2[:, 0:128]), stslot, R=[Bhs2])
                K.barrier(slots=[stslot, cslot])
                K.barrier()
            else:
                pend["barrier"] = True
            if si == 0:
                dump("useg", useg, [Buseg]); dump("ossm", ossm, [Bossm]); dump("hl_r", hl_r, [Bhl]); dump("hprev", hprev, [Bhp])
                dump("xnT_A", xnT, BxnTk); dump("hb_r", hb_r, [Bhb]); dump("zr", zr, [Bz]); dump("gr", gr, [Bg]); dump("Gt", Gt, [BGt])
            milestone("A%d" % si)
            for mt0 in range(0, len(seg), 4):
                mtl = seg[mt0:mt0 + 4]
                phase_B(mtl, mt0 * 128, mt0 + 4 >= len(seg))
                if mt0 + 4 >= len(seg):
                    pend["barrier"] = True
                milestone("B%d_%d" % (si, mt0))

    try:
        _main()
    except _StopBuild as e:
        print("build truncated at milestone", e)
    K.barrier()
    for sl in [oslot, stslot] + xslot + yslot + ([dbg_slot[0]] if dbg_slot[0] else []):
        if sl.n:
            S.wait_ge(sl.sem, sl.n)
    nc_ctx.__exit__(None, None, None)
    return nc


_NC_CACHE = {}


def kernel(x_prompt, x_sample, cache_k_win, cache_v_win, state_ssm_re, state_ssm_im,
           meta_tokens, g_mix, w_in, g_q, g_k, sinks,
           ssm_a_re, ssm_a_im, ssm_log_dt, ssm_b_re, ssm_b_im, ssm_c_re, ssm_c_im,
           ssm_d, ssm_w_glu, ssm_b_glu, g_att_out, g_ssm_out, w_out,
           g_ffn, w_gate, w_up, w_down):
    f = lambda a: np.ascontiguousarray(np.asarray(a, dtype=np.float32))
    if "nc" not in _NC_CACHE:
        _NC_CACHE["nc"] = build_nc()
    nc = _NC_CACHE["nc"]
    shared = {
        "meta": f(meta_tokens), "g_mix": f(g_mix).reshape(D), "w_in": f(w_in).reshape(D, 1280),
        "g_q": f(g_q).reshape(64), "g_k": f(g_k).reshape(64), "sinks": f(sinks).reshape(8),
        "a_re": f(ssm_a_re).reshape(2048), "a_im": f(ssm_a_im).reshape(2048), "log_dt": f(ssm_log_dt).reshape(32),
        "b_re": f(ssm_b_re).reshape(32768), "b_im": f(ssm_b_im).reshape(32768),
        "c_re": f(ssm_c_re).reshape(512, 64), "c_im": f(ssm_c_im).reshape(512, 64),
        "ssm_d": f(ssm_d).reshape(512), "w_glu": f(ssm_w_glu).reshape(512, 16), "b_glu": f(ssm_b_glu).reshape(512),
        "g_att": f(g_att_out).reshape(512), "g_ssm": f(g_ssm_out).reshape(512), "w_out": f(w_out).reshape(D, D),
        "g_ffn": f(g_ffn).reshape(D), "w_gate": f(w_gate).reshape(D, DFF), "w_up": f(w_up).reshape(D, DFF),
        "w_down": f(w_down).reshape(DFF, D),
    }
    xpf = f(x_prompt); xsf = f(x_sample)
    ckf = f(cache_k_win).reshape(128, 128, 128); cvf = f(cache_v_win).reshape(128, 128, 128)
    sref = f(state_ssm_re).reshape(128, 2048); simf = f(state_ssm_im).reshape(128, 2048)
    in_maps = []
    for b in range(NCORES):
        m = dict(shared)
        m["xp"] = xpf[b]
        m["xs"] = xsf[16 * b:16 * b + 16].reshape(128, D)
        m["ck"] = ckf[16 * b:16 * b + 16]
        m["cv"] = cvf[16 * b:16 * b + 16]
        m["sre"] = sref[16 * b:16 * b + 16].reshape(256, 128)
        m["sim"] = simf[16 * b:16 * b + 16].reshape(256, 128)
        in_maps.append(m)
    res = run_bass_kernel_spmd(nc, in_maps, core_ids=list(range(NCORES)))
    R = res.results
    y_prompt = np.stack([R[b]["yp"] for b in range(NCORES)]).astype(np.float32)
    y_sample = np.concatenate([R[b]["ys"].reshape(16, 8, D) for b in range(NCORES)]).astype(np.float32)
    kwp = np.stack([R[b]["kwp"].reshape(128, 2, 64) for b in range(NCORES)])[None].astype(np.float32)
    vwp = np.stack([R[b]["vwp"].reshape(128, 2, 64) for b in range(NCORES)])[None].astype(np.float32)
    srp = np.stack([R[b]["srp"].reshape(32, 64) for b in range(NCORES)])[None].astype(np.float32)
    sip = np.stack([R[b]["sip"].reshape(32, 64) for b in range(NCORES)])[None].astype(np.float32)
    kws = np.concatenate([R[b]["kws"].reshape(16, 128, 2, 64) for b in range(NCORES)])[None].astype(np.float32)
    vws = np.concatenate([R[b]["vws"].reshape(16, 128, 2, 64) for b in range(NCORES)])[None].astype(np.float32)
    srs = np.concatenate([R[b]["srs"].reshape(16, 32, 64) for b in range(NCORES)])[None].astype(np.float32)
    sis = np.concatenate([R[b]["sis"].reshape(16, 32, 64) for b in range(NCORES)])[None].astype(np.float32)
    return (y_prompt, y_sample, kwp, vwp, srp, sip, kws, vws, srs, sis)
```

```python
import math
import numpy as np
import concourse.bass as bass
import concourse.mybir as mybir
from concourse.bass_utils import run_bass_kernel_spmd

F32 = mybir.dt.float32
BF16 = mybir.dt.bfloat16
AF = mybir.ActivationFunctionType
ALU = mybir.AluOpType

NCORES = 8
D = 1024
DFF = 2816
NF = 22
EPS = 1e-6
NPT = 33
SCALE = 0.125
ARENA_WORDS = 53100
DEBUG = False
STOP_AT = None


class _StopBuild(Exception):
    pass


class Buf:
    __slots__ = ("name", "w", "r", "excl")

    def __init__(self, name, excl=False):
        self.name = name
        self.w = None
        self.r = {}
        self.excl = excl


class EngQ:
    def __init__(self, nc, eng, name, same=False):
        self.eng = eng
        self.name = name
        self.sem = nc.alloc_semaphore("sem_" + name)
        self.n = 0
        self.waited = {}
        self.same = same


class Slot:
    def __init__(self, nc, name):
        self.sem = nc.alloc_semaphore("dsem_" + name)
        self.n = 0
        self.name = name


class KB:
    def __init__(self, nc):
        self.nc = nc
        self.PE = EngQ(nc, nc.tensor, "pe")
        self.ACT = EngQ(nc, nc.scalar, "act", same=True)
        self.DVE = EngQ(nc, nc.vector, "dve", same=True)
        self.POOL = EngQ(nc, nc.gpsimd, "pool", same=True)
        self.SP = EngQ(nc, nc.sync, "sp")
        self.nslots = 0

    def slot(self, name):
        self.nslots += 1
        return Slot(self.nc, name)

    def _deps(self, R, W):
        deps = []
        for b in R:
            if b.w is not None:
                deps.append((b.w, True))
        for b in W:
            if b.w is not None:
                deps.append((b.w, False))
            deps.extend((t, False) for t in b.r.values())
        return deps

    def _wait(self, q, deps):
        for ((obj, val), raw) in deps:
            if obj is q and not (q.same and raw):
                continue
            if isinstance(obj, Slot):
                val = obj.n
            if q.waited.get(id(obj), 0) >= val:
                continue
            q.eng.wait_ge(obj.sem, val)
            q.waited[id(obj)] = val

    def _mark(self, tok, R, W):
        obj, val = tok
        for b in R:
            cur = b.r.get(id(obj))
            if cur is None or cur[1] < val:
                b.r[id(obj)] = tok
        for b in W:
            b.w = tok
            b.r = {}

    def op(self, q, fns, R=(), W=()):
        if not isinstance(fns, (list, tuple)):
            fns = [fns]
        if any(b.excl for b in R):
            W = list(W) + [b for b in R if b.excl]
            R = [b for b in R if not b.excl]
        self._wait(q, self._deps(R, W))
        ins = None
        for f in fns:
            ins = f()
        q.n += 1
        ins.then_inc(q.sem, 1)
        tok = (q, q.n)
        self._mark(tok, R, W)
        return tok

    def dma(self, q, fn, slot, R=(), W=()):
        self._wait(q, self._deps(R, W))
        ins = fn()
        slot.n += 16
        ins.then_inc(slot.sem, 16)
        tok = (slot, slot.n)
        self._mark(tok, R, W)
        return tok

    def barrier(self, qs=None, slots=()):
        qs = qs or [self.PE, self.ACT, self.DVE, self.POOL, self.SP]
        for q in qs:
            for sl in slots:
                if sl.n and q.waited.get(id(sl), 0) < sl.n:
                    q.eng.wait_ge(sl.sem, sl.n)
                    q.waited[id(sl)] = sl.n
            for o in qs:
                if o is q or o.n == 0:
                    continue
                if q.waited.get(id(o), 0) >= o.n:
                    continue
                q.eng.wait_ge(o.sem, o.n)
                q.waited[id(o)] = o.n


class Arena:
    def __init__(self, nc, words):
        self.nc = nc
        self.slab = nc.alloc_sbuf_tensor("arena", [128, words], F32)
        self.base = int(nc.lookup_mloc(self.slab).addr)
        self.words = words
        self.top = 0
        self.peak = 0
        self.n = 0

    def _at(self, off_words, n_elem, dtype):
        self.n += 1
        h = self.nc.alloc_sbuf_tensor_at("b%d" % self.n, [128, n_elem], dtype, offset=self.base + 4 * off_words, align_bytes=4)
        return h[:, :]

    def raw_f32(self, off_words, words):
        return self._at(off_words, words, F32)

    def raw_bf(self, off_words, elems):
        return self._at(off_words, elems, BF16)

    def f32(self, words, shape=None):
        off = self.top
        self.top += words
        self.peak = max(self.peak, self.top)
        assert self.top <= self.words, ("arena overflow", self.top, self.words)
        ap = self._at(off, words, F32)
        if shape is not None:
            ap = self._shape(ap, shape)
        return ap

    def bf(self, elems, shape=None):
        words = (elems + 1) // 2
        off = self.top
        self.top += words
        self.peak = max(self.peak, self.top)
        assert self.top <= self.words, ("arena overflow", self.top, self.words)
        ap = self._at(off, elems, BF16)
        if shape is not None:
            ap = self._shape(ap, shape)
        return ap

    @staticmethod
    def _shape(ap, shape):
        if len(shape) == 2:
            return ap.rearrange("p (a b) -> p a b", a=shape[0])
        if len(shape) == 3:
            return ap.rearrange("p (a b c) -> p a b c", a=shape[0], b=shape[1])
        if len(shape) == 4:
            return ap.rearrange("p (a b c d) -> p a b c d", a=shape[0], b=shape[1], c=shape[2])
        raise ValueError(shape)


def build_nc():
    nc = bass.Bass("TRN2", target_bir_lowering=False)

    def din(name, shape):
        return nc.dram_tensor(name, list(shape), F32, kind="ExternalInput").ap()

    def dout(name, shape):
        return nc.dram_tensor(name, list(shape), F32, kind="ExternalOutput").ap()

    xp = din("xp", [4096, D]); xs = din("xs", [128, D]); meta = din("meta", [16, D])
    ck = din("ck", [16, 128, 128]); cv = din("cv", [16, 128, 128])
    sre = din("sre", [256, 128]); sim = din("sim", [256, 128])
    g_mix = din("g_mix", [D]); w_in = din("w_in", [D, 1280]); g_q = din("g_q", [64]); g_k = din("g_k", [64])
    sinks = din("sinks", [8]); a_re = din("a_re", [2048]); a_im = din("a_im", [2048]); log_dt = din("log_dt", [32])
    b_re = din("b_re", [32768]); b_im = din("b_im", [32768]); c_re = din("c_re", [512, 64]); c_im = din("c_im", [512, 64])
    ssm_d = din("ssm_d", [512]); w_glu = din("w_glu", [512, 16]); b_glu = din("b_glu", [512])
    g_att = din("g_att", [512]); g_ssm = din("g_ssm", [512]); w_out = din("w_out", [D, D]); g_ffn = din("g_ffn", [D])
    w_gate = din("w_gate", [D, DFF]); w_up = din("w_up", [D, DFF]); w_down = din("w_down", [DFF, D])

    yp = dout("yp", [4096, D]); ys = dout("ys", [128, D])
    kwp = dout("kwp", [128, 128]); vwp = dout("vwp", [128, 128])
    srp = dout("srp", [16, 128]); sip = dout("sip", [16, 128])
    kws = dout("kws", [16, 128, 128]); vws = dout("vws", [16, 128, 128])
    srs = dout("srs", [256, 128]); sis = dout("sis", [256, 128])

    K = KB(nc)
    PE, ACT, DVE, POOL, SP = K.PE, K.ACT, K.DVE, K.POOL, K.SP
    T, A, V, G, S = nc.tensor, nc.scalar, nc.vector, nc.gpsimd, nc.sync
    ar = Arena(nc, ARENA_WORDS)
    ps = nc.alloc_psum_tensor("ps", [128, 8, 512], F32)
    pb = [Buf("psb%d" % i, excl=True) for i in range(8)]
    out_slots = []
    ms_ctr = [0]
    dbg_slot = [None]
    dbg_names = []

    def dump(name, ap, bufs):
        if not DEBUG:
            return
        if dbg_slot[0] is None:
            dbg_slot[0] = K.slot("dbg")
        shp = list(ap.shape)
        dt_ = nc.dram_tensor("dbg_" + name, shp, ap.dtype, kind="ExternalOutput").ap()
        dbg_names.append(name)
        K.dma(SP, lambda: S.dma_start(out=dt_, in_=ap), dbg_slot[0], R=bufs)

    def milestone(name):
        ms_ctr[0] += 1
        if STOP_AT is not None and ms_ctr[0] >= STOP_AT:
            raise _StopBuild(name)

    nc_ctx = nc.allow_non_contiguous_dma(reason="small parameter layouts")
    nc_ctx.__enter__()

    ident = ar.f32(128); ones_f = ar.f32(128)
    mcur = ar.bf(128); mprev = ar.bf(128); m1prev = ar.bf(128); m0cur = ar.bf(128); msamp = ar.bf(128); mcache = ar.bf(128); mzero = ar.bf(128)
    blk64 = ar.bf(128); onesA = ar.bf(128); onesF = ar.bf(128); ones_b = ar.bf(128)
    gmix_c = ar.f32(8); gffn_c = ar.f32(8); gatt_c = ar.f32(4); gssm_c = ar.f32(4)
    gq_c = ar.f32(1); gk_c = ar.f32(1); esk = ar.f32(4); dcol = ar.f32(4); bglu_c = ar.f32(4)
    mh16 = ar.f32(16); rho8 = ar.f32(16); bd32 = ar.f32(128)
    hl_r = ar.f32(16); hl_i = ar.f32(16)
    NWIN = 1408
    winb = ar.bf(8 * NWIN, [8, NWIN])
    woutb = ar.bf(8 * D, [8, D])
    KT = ar.bf(4 * 8 * 128, [4, 8, 128])
    BL = ar.bf(4 * 8 * 2 * 128, [4, 8, 2, 128])
    CL = ar.bf(8 * 2 * 512, [8, 2, 512])
    glub = ar.bf(4 * 128, [4, 128])
    cs = ar.f32(2048, [16, 128]); sn = ar.f32(2048, [16, 128])
    kdupT = ar.bf(2 * 640, [2, 640])
    vtok = ar.bf(5 * 128, [5, 128])
    PERS_TOP = ar.top

    B = {}

    def bb(name):
        if name not in B:
            B[name] = Buf(name)
        return B[name]

    consts = bb("consts")
    Bwin = bb("winb"); Bwout = bb("woutb")
    setup_slot = K.slot("setup")
    wslot = K.slot("wres")

    def pool(fn, R=(), W=()):
        return K.op(POOL, fn, R, W)

    def dve(fn, R=(), W=()):
        return K.op(DVE, fn, R, W)

    def act(fn, R=(), W=()):
        return K.op(ACT, fn, R, W)

    def pe(fns, R=(), W=()):
        return K.op(PE, fns, R, W)

    C1 = [consts]
    pool(lambda: G.memset(ident, 0.0), W=C1)
    pool(lambda: G.affine_select(out=ident, in_=ident, pattern=[[-1, 128]], compare_op=ALU.not_equal,
                                 fill=1.0, base=0, channel_multiplier=1), R=C1, W=C1)
    pool(lambda: G.memset(ones_f, 1.0), W=C1)
    pool(lambda: G.memset(mh16, -0.5), W=C1)
    SET0 = ar.top
    mtmp = ar.f32(128 * 6, [6, 128])
    Bm = bb("mtmp")
    pool(lambda: G.memset(mtmp, 1.0), W=[Bm])
    pool(lambda: G.affine_select(out=mtmp[:, 0, :], in_=mtmp[:, 0, :], pattern=[[1, 128]], compare_op=ALU.is_ge,
                                 fill=0.0, base=0, channel_multiplier=-1), R=[Bm], W=[Bm])
    pool(lambda: G.affine_select(out=mtmp[:, 1, :], in_=mtmp[:, 1, :], pattern=[[-1, 128]], compare_op=ALU.is_ge,
                                 fill=0.0, base=0, channel_multiplier=1), R=[Bm], W=[Bm])
    pool(lambda: G.affine_select(out=mtmp[:, 2, :], in_=mtmp[:, 2, :], pattern=[[1, 128]], compare_op=ALU.is_ge,
                                 fill=0.0, base=0, channel_multiplier=-1), R=[Bm], W=[Bm])
    pool(lambda: G.affine_select(out=mtmp[:, 2, :], in_=mtmp[:, 2, :], pattern=[[0, 128]], compare_op=ALU.is_ge,
                                 fill=0.0, base=-112, channel_multiplier=1), R=[Bm], W=[Bm])
    v3 = mtmp[:, 3, :].rearrange("p (s i) -> p s i", s=16)
    pool(lambda: G.affine_select(out=v3, in_=v3, pattern=[[8, 16], [1, 8]], compare_op=ALU.is_ge,
                                 fill=0.0, base=0, channel_multiplier=-1), R=[Bm], W=[Bm])
    pool(lambda: G.affine_select(out=v3, in_=v3, pattern=[[-8, 16], [0, 8]], compare_op=ALU.is_ge,
                                 fill=0.0, base=0, channel_multiplier=1), R=[Bm], W=[Bm])
    v4 = mtmp[:, 4, :].rearrange("p (s i) -> p s i", s=16)
    pool(lambda: G.affine_select(out=v4, in_=v4, pattern=[[0, 16], [-1, 8]], compare_op=ALU.is_ge,
                                 fill=0.0, base=0, channel_multiplier=1), R=[Bm], W=[Bm])
    pool(lambda: G.affine_select(out=mtmp[:, 5, :], in_=mtmp[:, 5, :], pattern=[[-1, 128]], compare_op=ALU.is_ge,
                                 fill=0.0, base=0, channel_multiplier=1), R=[Bm], W=[Bm])
    pool(lambda: G.affine_select(out=mtmp[:, 5, :], in_=mtmp[:, 5, :], pattern=[[0, 128]], compare_op=ALU.is_ge,
                                 fill=0.0, base=-112, channel_multiplier=1), R=[Bm], W=[Bm])
    Bbd = bb("bd32")
    pool(lambda: G.memset(bd32, 1.0), W=[Bbd])
    bdv = bd32.rearrange("p (a b) -> p a b", a=4)
    pool(lambda: G.affine_select(out=bdv, in_=bdv, pattern=[[32, 4], [0, 32]], compare_op=ALU.is_ge,
                                 fill=0.0, base=31, channel_multiplier=-1), R=[Bbd], W=[Bbd])
    pool(lambda: G.affine_select(out=bdv, in_=bdv, pattern=[[-32, 4], [0, 32]], compare_op=ALU.is_ge,
                                 fill=0.0, base=0, channel_multiplier=1), R=[Bbd], W=[Bbd])
    bd16 = ar.f32(128)
    Bbd16 = bb("bd16")
    pool(lambda: G.memset(bd16, 1.0), W=[Bbd16])
    bdv16 = bd16.rearrange("p (a b) -> p a b", a=8)
    pool(lambda: G.affine_select(out=bdv16, in_=bdv16, pattern=[[16, 8], [0, 16]], compare_op=ALU.is_ge,
                                 fill=0.0, base=15, channel_multiplier=-1), R=[Bbd16], W=[Bbd16])
    pool(lambda: G.affine_select(out=bdv16, in_=bdv16, pattern=[[-16, 8], [0, 16]], compare_op=ALU.is_ge,
                                 fill=0.0, base=0, channel_multiplier=1), R=[Bbd16], W=[Bbd16])
    for i, m in enumerate([mcur, mprev, m0cur, msamp, mcache, m1prev]):
        dve(lambda i=i, m=m: V.tensor_copy(out=m, in_=mtmp[:, i, :]), R=[Bm], W=C1)
    dve(lambda: V.memset(mzero, 0.0), W=C1)
    dve(lambda: V.memset(blk64, 0.0), W=C1)
    dve(lambda: V.memset(blk64[0:64, 0:64], 1.0 / 64), W=C1)
    dve(lambda: V.memset(blk64[64:128, 64:128], 1.0 / 64), W=C1)
    dve(lambda: V.memset(onesA, 1.0 / 512), W=C1)
    dve(lambda: V.memset(onesF, 1.0 / 1024), W=C1)
    dve(lambda: V.memset(ones_b, 1.0), W=C1)
    Bkd = bb("kdupT"); Bvt = bb("vtok")
    dve(lambda: V.memset(kdupT, 0.0), W=[Bkd])
    dve(lambda: V.memset(vtok, 0.0), W=[Bvt])

    def wload(dst, src):
        K.dma(POOL, lambda: G.dma_start(out=dst, in_=src), wslot, W=[Bwin, Bwout])

    sgate = nc.dram_tensor("scr_gate", [NF, 128, 8 * 128], BF16, kind="Internal").ap()
    sup = nc.dram_tensor("scr_up", [NF, 128, 8 * 128], BF16, kind="Internal").ap()
    sdown = nc.dram_tensor("scr_down", [16, 128, 11 * 128], BF16, kind="Internal").ap()
    Bscr = bb("scratchW")
    cvslot = K.slot("conv")
    conv_jobs = []
    wg_v = w_gate.rearrange("(k p) c -> p k c", p=128)
    wu_v = w_up.rearrange("(k p) c -> p k c", p=128)
    wd_v = w_down.rearrange("(f p) c -> p f c", p=128)
    for f_ in range(NF):
        conv_jobs.append((sgate[f_].rearrange("p (k c) -> p k c", k=8), wg_v[:, :, f_ * 128:(f_ + 1) * 128]))
        conv_jobs.append((sup[f_].rearrange("p (k c) -> p k c", k=8), wu_v[:, :, f_ * 128:(f_ + 1) * 128]))
        if f_ == 10 or f_ == 21:
            hf_ = 0 if f_ == 10 else 1
            for m_ in range(8):
                conv_jobs.append((sdown[hf_ * 8 + m_].rearrange("p (f c) -> p f c", f=11),
                                  wd_v[:, 11 * hf_:11 * hf_ + 11, m_ * 128:(m_ + 1) * 128]))
    conv_state = {"i": 0}

    def conv_issue(n):
        for _ in range(n):
            i = conv_state["i"]
            if i >= len(conv_jobs):
                return
            if i >= 6:
                need = 16 * (i - 5)
                if POOL.waited.get(id(cvslot), 0) < need:
                    G.wait_ge(cvslot.sem, need)
                    POOL.waited[id(cvslot)] = need
            dst, src = conv_jobs[i]
            K.dma(POOL, lambda dst=dst, src=src: G.dma_start(out=dst, in_=src), cvslot, W=[Bscr])
            conv_state["i"] = i + 1

    Bpar = bb("params")
    sload_list = []

    def sload(dst, src):
        K.dma(SP, lambda: S.dma_start(out=dst, in_=src), setup_slot, W=[Bpar])

    BparA = bb("paramsA")
    setupA_slot = K.slot("setupA")

    def sloadA(dst, src):
        K.dma(SP, lambda: S.dma_start(out=dst, in_=src), setupA_slot, W=[BparA])

    are = ar.f32(16); aim = ar.f32(16); ldt = ar.f32(16)
    sloadA(are, a_re.rearrange("(k p) -> p k", p=128))
    sloadA(aim, a_im.rearrange("(k p) -> p k", p=128))
    ldt2 = log_dt.rearrange("(P two) -> two P", two=2)
    sloadA(ldt[0:64, :], ldt2[0, :].partition_broadcast(64))
    sloadA(ldt[64:128, :], ldt2[1, :].partition_broadcast(64))
    Bre = ar.f32(256, [16, 16]); Bim = ar.f32(256, [16, 16])
    sloadA(Bre, b_re.rearrange("(P q h) -> q P h", P=16, q=128))
    sloadA(Bim, b_im.rearrange("(P q h) -> q P h", P=16, q=128))
    BparA.w = (setupA_slot, setupA_slot.n)
    sload(gmix_c, g_mix.rearrange("(k p) -> p k", p=128))
    sload(gffn_c, g_ffn.rearrange("(k p) -> p k", p=128))
    sload(gatt_c, g_att.rearrange("(k p) -> p k", p=128))
    sload(gssm_c, g_ssm.rearrange("(k p) -> p k", p=128))
    sload(dcol, ssm_d.rearrange("(k p) -> p k", p=128))
    sload(bglu_c, b_glu.rearrange("(k p) -> p k", p=128))
    gq2 = g_q.rearrange("(p o) -> p o", o=1)
    gk2 = g_k.rearrange("(p o) -> p o", o=1)
    sload(gq_c[0:64, :], gq2); sload(gq_c[64:128, :], gq2)
    sload(gk_c[0:64, :], gk2); sload(gk_c[64:128, :], gk2)
    sk2 = sinks.rearrange("(t two) -> two t", two=2)
    sload(esk[0:64, :], sk2[0:1, :].partition_broadcast(64) if False else sinks.rearrange("(t two) -> two t", two=2)[0, :].partition_broadcast(64))
    sload(esk[64:128, :], sinks.rearrange("(t two) -> two t", two=2)[1, :].partition_broadcast(64))
    Cin = ar.f32(4 * 2 * 128, [4, 2, 128])
    Cld = ar.f32(4 * 2 * 64, [4, 2, 64])
    BCin = bb("Cin"); BCld = bb("Cld")
    for ri, csrc in enumerate([c_re, c_im]):
        K.dma(SP, lambda ri=ri, csrc=csrc: S.dma_start(out=Cld[:, :, ri, :], in_=csrc.rearrange("(o r) p -> r o p", o=4)),
              setup_slot, W=[BCld])
    gluf = ar.f32(4 * 128, [4, 128])
    gld = ar.f32(4 * 16, [4, 16])
    Bgl = bb("gluf"); Bgld = bb("gld")
    K.dma(SP, lambda: S.dma_start(out=gld, in_=w_glu.rearrange("(o r) k -> r o k", o=4)), setup_slot, W=[Bgld])
    fin = (setup_slot, setup_slot.n)
    for b in (Bpar, BCld, Bgld):
        b.w = fin
    G.wait_ge(setupA_slot.sem, setupA_slot.n)
    POOL.waited[id(setupA_slot)] = setupA_slot.n
    w_in_v = w_in.rearrange("(k p) c -> p k c", p=128)
    wload(winb[:, :, 896:1408], w_in_v[:, :, 768:1280])
    wload(winb[:, :, 0:512], w_in_v[:, :, 0:512])
    wload(winb[:, :, 512:576], w_in_v[:, :, 512:576])
    wload(winb[:, :, 576:640], w_in_v[:, :, 512:576])
    wload(winb[:, :, 640:704], w_in_v[:, :, 576:640])
    wload(winb[:, :, 704:768], w_in_v[:, :, 576:640])
    wload(winb[:, :, 768:896], w_in_v[:, :, 640:768])
    w_out_v = w_out.rearrange("(k p) c -> p k c", p=128)
    for k in range(0, 8, 2):
        wload(woutb[:, k:k + 2, :], w_out_v[:, k:k + 2, :])

    conv_issue(len(conv_jobs))
    P1 = [Bpar]

    def t16():
        return ar.f32(16)

    Bs = bb("ssmtmp")
    RS = [BparA, Bs]
    WS = [Bs]
    act(lambda: A.activation(out=esk, in_=esk, func=AF.Exp), R=P1, W=[Bpar])
    dt_ = t16(); adt = t16(); mag = t16(); th = t16()
    TWO_PI = 2.0 * math.pi

    def sin_of(dst, src, shift):
        u = t16(); ki = ar.f32(16).bitcast(mybir.dt.int32); kf = t16(); r = t16(); m1 = t16(); m2 = t16(); x2 = t16(); qq = t16()
        dve(lambda: V.tensor_scalar(out=u, in0=src, scalar1=shift, scalar2=1.0 / TWO_PI, op0=ALU.add, op1=ALU.mult), R=RS, W=WS)
        dve(lambda: V.tensor_copy(out=ki, in_=u), R=RS, W=WS)
        dve(lambda: V.tensor_copy(out=kf, in_=ki), R=RS, W=WS)
        dve(lambda: V.tensor_tensor(out=r, in0=u, in1=kf, op=ALU.subtract), R=RS, W=WS)
        for (thr, op_, sgn) in ((0.5, ALU.is_gt, -1.0), (-0.5, ALU.is_lt, 1.0)):
            dve(lambda thr=thr, op_=op_: V.tensor_scalar(out=m1, in0=r, scalar1=thr, scalar2=None, op0=op_), R=RS, W=WS)
            dve(lambda sgn=sgn: V.scalar_tensor_tensor(out=r, in0=m1, scalar=sgn, in1=r, op0=ALU.mult, op1=ALU.add), R=RS, W=WS)
        for (thr, op_, c0) in ((0.25, ALU.is_gt, 0.5), (-0.25, ALU.is_lt, -0.5)):
            dve(lambda thr=thr, op_=op_: V.tensor_scalar(out=m1, in0=r, scalar1=thr, scalar2=None, op0=op_), R=RS, W=WS)
            dve(lambda c0=c0: V.tensor_scalar(out=m2, in0=r, scalar1=-2.0, scalar2=c0, op0=ALU.mult, op1=ALU.add), R=RS, W=WS)
            dve(lambda: V.tensor_tensor(out=m2, in0=m2, in1=m1, op=ALU.mult), R=RS, W=WS)
            dve(lambda: V.tensor_tensor(out=r, in0=r, in1=m2, op=ALU.add), R=RS, W=WS)
        dve(lambda: V.tensor_scalar(out=r, in0=r, scalar1=TWO_PI, scalar2=None, op0=ALU.mult), R=RS, W=WS)
        dve(lambda: V.tensor_tensor(out=x2, in0=r, in1=r, op=ALU.mult), R=RS, W=WS)
        cf = [-1.0 / 6, 1.0 / 120, -1.0 / 5040, 1.0 / 362880, -1.0 / 39916800, 1.0 / 6227020800]
        dve(lambda: V.tensor_scalar(out=qq, in0=x2, scalar1=cf[5], scalar2=None, op0=ALU.mult), R=RS, W=WS)
        for c_ in (cf[4], cf[3], cf[2], cf[1], cf[0]):
            dve(lambda c_=c_: V.scalar_tensor_tensor(out=qq, in0=qq, scalar=c_, in1=x2, op0=ALU.add, op1=ALU.mult), R=RS, W=WS)
        dve(lambda: V.scalar_tensor_tensor(out=dst, in0=qq, scalar=1.0, in1=r, op0=ALU.add, op1=ALU.mult), R=RS, W=WS)

    def exp_of(dst, src, nsq):
        y = t16(); qq = t16()
        dve(lambda: V.tensor_scalar(out=y, in0=src, scalar1=1.0 / (2 ** nsq), scalar2=None, op0=ALU.mult), R=RS, W=WS)
        dve(lambda: V.tensor_scalar(out=qq, in0=y, scalar1=1.0 / 8, scalar2=1.0, op0=ALU.mult, op1=ALU.add), R=RS, W=WS)
        for k_ in (7, 6, 5, 4, 3, 2, 1):
            dve(lambda k_=k_: V.scalar_tensor_tensor(out=qq, in0=qq, scalar=1.0 / k_, in1=y, op0=ALU.mult, op1=ALU.mult), R=RS, W=WS)
            dve(lambda: V.tensor_scalar(out=qq, in0=qq, scalar1=1.0, scalar2=None, op0=ALU.add), R=RS, W=WS)
        for _ in range(nsq):
            dve(lambda: V.tensor_tensor(out=qq, in0=qq, in1=qq, op=ALU.mult), R=RS, W=WS)
        dve(lambda: V.tensor_copy(out=dst, in_=qq), R=RS, W=WS)

    exp_of(dt_, ldt, 4)
    dve(lambda: V.tensor_tensor(out=adt, in0=are, in1=dt_, op=ALU.mult), R=RS, W=WS)
    dve(lambda: V.tensor_tensor(out=th, in0=aim, in1=dt_, op=ALU.mult), R=RS, W=WS)
    exp_of(mag, adt, 1)
    sth = t16(); cth = t16()
    sin_of(sth, th, 0.0)
    sin_of(cth, th, math.pi / 2)
    Lr = ar.f32(9 * 16, [9, 16]); Li = ar.f32(9 * 16, [9, 16])
    dve(lambda: V.memset(Lr[:, 0, :], 1.0), W=WS)
    dve(lambda: V.memset(Li[:, 0, :], 0.0), W=WS)
    dve(lambda: V.tensor_tensor(out=Lr[:, 1, :], in0=mag, in1=cth, op=ALU.mult), R=RS, W=WS)
    dve(lambda: V.tensor_tensor(out=Li[:, 1, :], in0=mag, in1=sth, op=ALU.mult), R=RS, W=WS)
    ta = t16(); tb = t16()

    def cmul(dr, di, ar_, ai_, br_, bi_, shape_bc=None):
        dve(lambda: V.tensor_tensor(out=ta, in0=ai_, in1=bi_, op=ALU.mult), R=RS, W=WS)
        dve(lambda: V.tensor_tensor(out=tb, in0=ar_, in1=bi_, op=ALU.mult), R=RS, W=WS)
        dve(lambda: V.tensor_tensor(out=dr, in0=ar_, in1=br_, op=ALU.mult), R=RS, W=WS)
        dve(lambda: V.tensor_tensor(out=di, in0=ai_, in1=br_, op=ALU.mult), R=RS, W=WS)
        dve(lambda: V.tensor_tensor(out=dr, in0=dr, in1=ta, op=ALU.subtract), R=RS, W=WS)
        dve(lambda: V.tensor_tensor(out=di, in0=di, in1=tb, op=ALU.add), R=RS, W=WS)

    for n in range(2, 9):
        cmul(Lr[:, n, :], Li[:, n, :], Lr[:, n - 1, :], Li[:, n - 1, :], Lr[:, 1, :], Li[:, 1, :])
    nr = t16(); den = t16(); fr = t16(); fi = t16(); t3 = t16()
    dve(lambda: V.tensor_scalar(out=nr, in0=Lr[:, 1, :], scalar1=-1.0, scalar2=None, op0=ALU.add), R=RS, W=WS)
    dve(lambda: V.tensor_tensor(out=den, in0=are, in1=are, op=ALU.mult), R=RS, W=WS)
    dve(lambda: V.tensor_tensor(out=t3, in0=aim, in1=aim, op=ALU.mult), R=RS, W=WS)
    dve(lambda: V.tensor_tensor(out=den, in0=den, in1=t3, op=ALU.add), R=RS, W=WS)
    dve(lambda: V.reciprocal(out=den, in_=den), R=RS, W=WS)
    ni = Li[:, 1, :]
    dve(lambda: V.tensor_tensor(out=fr, in0=nr, in1=are, op=ALU.mult), R=RS, W=WS)
    dve(lambda: V.tensor_tensor(out=t3, in0=ni, in1=aim, op=ALU.mult), R=RS, W=WS)
    dve(lambda: V.tensor_tensor(out=fr, in0=fr, in1=t3, op=ALU.add), R=RS, W=WS)
    dve(lambda: V.tensor_tensor(out=fr, in0=fr, in1=den, op=ALU.mult), R=RS, W=WS)
    dve(lambda: V.tensor_tensor(out=fi, in0=ni, in1=are, op=ALU.mult), R=RS, W=WS)
    dve(lambda: V.tensor_tensor(out=t3, in0=nr, in1=aim, op=ALU.mult), R=RS, W=WS)
    dve(lambda: V.tensor_tensor(out=fi, in0=fi, in1=t3, op=ALU.subtract), R=RS, W=WS)
    dve(lambda: V.tensor_tensor(out=fi, in0=fi, in1=den, op=ALU.mult), R=RS, W=WS)
    wr = t16(); wi = t16(); w2 = t16()
    dve(lambda: V.tensor_tensor(out=w2, in0=Lr[:, 8, :], in1=Lr[:, 8, :], op=ALU.mult), R=RS, W=WS)
    dve(lambda: V.tensor_tensor(out=t3, in0=Li[:, 8, :], in1=Li[:, 8, :], op=ALU.mult), R=RS, W=WS)
    dve(lambda: V.tensor_tensor(out=w2, in0=w2, in1=t3, op=ALU.add), R=RS, W=WS)
    w2a = t16(); w2y = t16(); w2t = t16()
    dve(lambda: V.tensor_copy(out=w2a, in_=w2), R=RS, W=WS)
    act(lambda: A.activation(out=w2y, in_=w2a, func=AF.Ln), R=RS, W=WS)
    act(lambda: A.activation(out=w2y, in_=w2y, func=AF.Exp, scale=-0.5), R=RS, W=WS)
    dve(lambda: V.tensor_tensor(out=w2t, in0=w2y, in1=w2y, op=ALU.mult), R=RS, W=WS)
    dve(lambda: V.tensor_tensor(out=w2t, in0=w2t, in1=w2a, op=ALU.mult), R=RS, W=WS)
    dve(lambda: V.tensor_scalar(out=w2t, in0=w2t, scalar1=-0.5, scalar2=1.5, op0=ALU.mult, op1=ALU.add), R=RS, W=WS)
    dve(lambda: V.tensor_tensor(out=w2, in0=w2y, in1=w2t, op=ALU.mult), R=RS, W=WS)
    dve(lambda: V.reciprocal(out=rho8, in_=w2), R=RS, W=C1 + [Bs])
    dve(lambda: V.tensor_tensor(out=wr, in0=Lr[:, 8, :], in1=w2, op=ALU.mult), R=RS, W=WS)
    dve(lambda: V.tensor_tensor(out=wi, in0=Li[:, 8, :], in1=w2, op=ALU.mult), R=RS, W=WS)
    Btab = bb("tables")
    WT = [Btab, Bs]
    RT = [Btab, Bs, Bpar]
    dve(lambda: V.tensor_copy(out=cs[:, :, 0], in_=wr), R=RT, W=WT)
    dve(lambda: V.tensor_copy(out=sn[:, :, 0], in_=wi), R=RT, W=WT)
    tq1 = ar.f32(16 * 64, [16, 64]); tq2 = ar.f32(16 * 64, [16, 64])
    n = 1
    while n < 128:
        kr = cs[:, :, n - 1:n].to_broadcast([128, 16, n]); ki_ = sn[:, :, n - 1:n].to_broadcast([128, 16, n])
        sr = cs[:, :, 0:n]; si = sn[:, :, 0:n]; dr = cs[:, :, n:2 * n]; di = sn[:, :, n:2 * n]
        a1 = tq1[:, :, 0:n]; a2 = tq2[:, :, 0:n]
        dve(lambda a1=a1, si=si, ki_=ki_: V.tensor_tensor(out=a1, in0=si, in1=ki_, op=ALU.mult), R=RT, W=WT)
        dve(lambda a2=a2, sr=sr, ki_=ki_: V.tensor_tensor(out=a2, in0=sr, in1=ki_, op=ALU.mult), R=RT, W=WT)
        dve(lambda dr=dr, sr=sr, kr=kr: V.tensor_tensor(out=dr, in0=sr, in1=kr, op=ALU.mult), R=RT, W=WT)
        dve(lambda di=di, si=si, kr=kr: V.tensor_tensor(out=di, in0=si, in1=kr, op=ALU.mult), R=RT, W=WT)
        dve(lambda dr=dr, a1=a1: V.tensor_tensor(out=dr, in0=dr, in1=a1, op=ALU.subtract), R=RT, W=WT)
        dve(lambda di=di, a2=a2: V.tensor_tensor(out=di, in0=di, in1=a2, op=ALU.add), R=RT, W=WT)
        n *= 2
    bbr = ar.f32(256, [16, 16]); bbi = ar.f32(256, [16, 16]); tB1 = ar.f32(256, [16, 16]); tB2 = ar.f32(256, [16, 16])

    def bc16(x):
        return x.unsqueeze(2).to_broadcast([128, 16, 16])

    def cmulB(dr, di, sr, si, xr_, xi_, dr_halves=None):
        dve(lambda: V.tensor_tensor(out=tB1, in0=si, in1=bc16(xi_), op=ALU.mult), R=RS, W=WS)
        dve(lambda: V.tensor_tensor(out=tB2, in0=sr, in1=bc16(xi_), op=ALU.mult), R=RS, W=WS)
        dve(lambda: V.tensor_tensor(out=dr, in0=sr, in1=bc16(xr_), op=ALU.mult), R=RS, W=WS)
        dve(lambda: V.tensor_tensor(out=di, in0=si, in1=bc16(xr_), op=ALU.mult), R=RS, W=WS)
        dve(lambda: V.tensor_tensor(out=dr, in0=dr, in1=tB1, op=ALU.subtract), R=RS, W=WS)
        dve(lambda: V.tensor_tensor(out=di, in0=di, in1=tB2, op=ALU.add), R=RS, W=WS)

    cmulB(bbr, bbi, Bre, Bim, fr, fi)
    Mr = ar.f32(512, [16, 2, 16]); Mi = ar.f32(512, [16, 2, 16])
    MB0r = ar.f32(512, [16, 2, 16]); MB0i = ar.f32(512, [16, 2, 16])
    Wr_ = ar.f32(256, [16, 16]); Wi_ = ar.f32(256, [16, 16])
    BM = bb("Mtiles")
    for t_ in (Mr, Mi, MB0r, MB0i):
        dve(lambda t_=t_: V.memset(t_, 0.0), W=[BM])
    BBL = bb("BL")
    pbank = [0]

    def next_bank():
        b = pbank[0]
        pbank[0] = (b + 1) % 8
        return b

    for n in range(8):
        if n == 0:
            srcr, srci = bbr, bbi
        else:
            cmulB(Wr_, Wi_, bbr, bbi, Lr[:, n, :], Li[:, n, :])
            srcr, srci = Wr_, Wi_
        dstr, dsti = (MB0r, MB0i) if n == 0 else (Mr, Mi)
        for (dst, src) in ((dstr, srcr), (dsti, srci)):
            dve(lambda dst=dst, src=src: V.tensor_copy(out=dst[0:64, :, 0, :], in_=src[0:64]), R=RS, W=[BM])
            dve(lambda dst=dst, src=src: V.tensor_copy(out=dst[64:128, :, 1, :], in_=src[64:128]), R=RS, W=[BM])
        for part, msrc in enumerate((dstr, dsti)):
            for o in range(4):
                bk = next_bank()
                mv = msrc[:, 4 * o:4 * o + 4, :, :].rearrange("p a b c -> p (a b c)")
                pe(lambda bk=bk, mv=mv: T.matmul(ps[:, bk, 0:128], lhsT=mv, rhs=ident, is_transpose=True, start=True, stop=True), R=[BM, consts], W=[pb[bk]])
                act(lambda bk=bk, o=o, n=n, part=part: A.copy(out=BL[:, o, 7 - n, part, :], in_=ps[:, bk, 0:128]),
                    R=[pb[bk]], W=[BBL])
    glhot = ar.f32(2)
    dve(lambda: V.tensor_reduce(out=glhot, in_=bd16.rearrange("p (pl gl k) -> p gl pl k", pl=4, gl=2), axis=mybir.AxisListType.XY, op=ALU.add),
        R=[Bbd16], W=[BCin])
    dve(lambda: V.tensor_scalar(out=glhot, in0=glhot, scalar1=1.0 / 16, scalar2=None, op0=ALU.mult), R=[BCin], W=[BCin])
    for o in range(4):
        for ri in range(2):
            for gl in range(2):
                dve(lambda o=o, ri=ri, gl=gl: V.tensor_scalar(out=Cin[:, o, ri, gl * 64:(gl + 1) * 64], in0=Cld[:, o, ri, :],
                                                              scalar1=glhot[:, gl:gl + 1], scalar2=None, op0=ALU.mult), R=[BCld, BCin], W=[BCin])
        dve(lambda o=o: V.tensor_tensor(out=gluf[:, o, :].rearrange("p (g k) -> p g k", g=8),
                                        in0=gld[:, o, :].unsqueeze(1).to_broadcast([128, 8, 16]),
                                        in1=bd16.rearrange("p (g k) -> p g k", g=8), op=ALU.mult), R=[Bgld, Bbd16], W=[Bgl])

    BCT = bb("CT")
    CTB = [pb[0], pb[1]]
    for ri in range(2):
        pe([lambda o=o, ri=ri: T.matmul(ps[:, ri, o * 128:(o + 1) * 128], lhsT=Cin[:, o, ri, :], rhs=ident, is_transpose=True, start=True, stop=True)
            for o in range(4)], R=[BCin, consts], W=[pb[ri]])
    CTr = ps[:, 0, :].rearrange("p (a b) -> p a b", a=16)
    CTi = ps[:, 1, :].rearrange("p (a b) -> p a b", a=16)
    CLf = ar.f32(9 * 2 * 512, [9, 2, 512])
    BCLf = bb("CLf"); BCL = bb("CL")
    tC1 = ps[:, 2, :].rearrange("p (a b) -> p a b", a=16)
    tC2 = ar.f32(512, [16, 32])
    T1 = [pb[2]]

    def bc32(x):
        return x.unsqueeze(2).to_broadcast([128, 16, 32])

    def v512(x):
        return x.rearrange("p (a b) -> p a b", a=16)

    dve(lambda: V.tensor_copy(out=v512(CLf[:, 0, 0, :]), in_=CTr), R=CTB, W=[BCLf])
    dve(lambda: V.tensor_scalar(out=v512(CLf[:, 0, 1, :]), in0=CTi, scalar1=-1.0, scalar2=None, op0=ALU.mult), R=CTB, W=[BCLf])
    for n in range(1, 9):
        lr_, li_ = Lr[:, n, :], Li[:, n, :]
        o_r = v512(CLf[:, n, 0, :]); o_i = v512(CLf[:, n, 1, :])
        dve(lambda lr_=lr_: V.tensor_tensor(out=tC1, in0=CTr, in1=bc32(lr_), op=ALU.mult), R=RS + CTB, W=T1)
        dve(lambda li_=li_: V.tensor_tensor(out=tC2, in0=CTi, in1=bc32(li_), op=ALU.mult), R=RS + CTB, W=WS)
        dve(lambda o_r=o_r: V.tensor_tensor(out=o_r, in0=tC1, in1=tC2, op=ALU.subtract), R=RS + T1, W=[BCLf, Bs])
        dve(lambda li_=li_: V.tensor_tensor(out=tC1, in0=CTr, in1=bc32(li_), op=ALU.mult), R=RS + CTB, W=T1)
        dve(lambda lr_=lr_: V.tensor_tensor(out=tC2, in0=CTi, in1=bc32(lr_), op=ALU.mult), R=RS + CTB, W=WS)
        dve(lambda o_i=o_i: V.scalar_tensor_tensor(out=o_i, in0=tC1, scalar=-1.0, in1=tC2, op0=ALU.mult, op1=ALU.subtract),
            R=RS + T1, W=[BCLf, Bs])
        act(lambda n=n: A.copy(out=CL[:, n - 1, :, :], in_=CLf[:, n, :, :]), R=[BCLf], W=[BCL])
    BKT = bb("KT")
    ktmp = ar.f32(128)
    Bkt = bb("ktmp")
    for tau in range(8):
        for o in range(4):
            bk = next_bank()
            l_r = MB0r[:, 4 * o:4 * o + 4, :, :].rearrange("p a b c -> p (a b c)")
            l_i = MB0i[:, 4 * o:4 * o + 4, :, :].rearrange("p a b c -> p (a b c)")
            r_r = CLf[:, tau, 0, 128 * o:128 * o + 128]
            r_i = CLf[:, tau, 1, 128 * o:128 * o + 128]
            pe([lambda bk=bk, l_r=l_r, r_r=r_r: T.matmul(ps[:, bk, 0:128], lhsT=l_r, rhs=r_r, start=True, stop=False),
                lambda bk=bk, l_i=l_i, r_i=r_i: T.matmul(ps[:, bk, 0:128], lhsT=l_i, rhs=r_i, start=False, stop=True)],
               R=[BM, BCLf], W=[pb[bk]])
            if tau == 0:
                dve(lambda bk=bk: V.tensor_tensor(out=ktmp, in0=ps[:, bk, 0:128], in1=bd32, op=ALU.mult), R=[pb[bk], Bbd], W=[Bkt])
                dve(lambda o=o: V.scalar_tensor_tensor(out=KT[:, o, 0, :], in0=ident, scalar=dcol[:, o:o + 1], in1=ktmp,
                                                       op0=ALU.mult, op1=ALU.add), R=[Bkt, consts, Bpar], W=[BKT])
            else:
                dve(lambda bk=bk, o=o, tau=tau: V.tensor_tensor(out=KT[:, o, tau, :], in0=ps[:, bk, 0:128], in1=bd32, op=ALU.mult),
                    R=[pb[bk], Bbd], W=[BKT])
    Bglu = bb("glub")
    dve(lambda: V.tensor_copy(out=glub, in_=gluf), R=[Bgl], W=[Bglu])
    for k in range(8):
        act(lambda k=k: A.activation(out=winb[:, k, :], in_=winb[:, k, :], func=AF.Copy, scale=gmix_c[:, k:k + 1]),
            R=[Bwin, Bpar], W=[Bwin])
    Bhl = bb("hlast")
    dve(lambda: V.memset(hl_r, 0.0), W=[Bhl])
    dve(lambda: V.memset(hl_i, 0.0), W=[Bhl])
    K.barrier()
    dump("cs", cs, [Btab]); dump("sn", sn, [Btab]); dump("rho8", rho8, [consts]); dump("esk", esk, [Bpar])
    dump("KT", KT, [BKT]); dump("BL", BL, [BBL]); dump("CL", CL, [BCL]); dump("glub", glub, [Bglu])
    dump("Lr", Lr, [Bs]); dump("Li", Li, [Bs]); dump("fr", fr, [Bs]); dump("fi", fi, [Bs])
    dump("mcur", mcur, [consts]); dump("mprev", mprev, [consts]); dump("m0cur", m0cur, [consts]); dump("msamp", msamp, [consts])
    dump("mcache", mcache, [consts]); dump("m1prev", m1prev, [consts]); dump("bd32", bd32, [Bbd]); dump("ident", ident, [consts])
    dump("winb", winb, [Bwin])
    SSMW = [BKT, BBL, BCL, Bglu, Btab, consts, Bpar]

    ar.top = PERS_TOP
    xst = [ar.f32(1024), ar.f32(1024)]
    junk = ar.bf(1024)
    aT = ar.bf(11 * 512, [11, 512])
    R0_END = ar.top
    xnT = ar.bf(8 * 512, [8, 512])
    ossm = ar.bf(4 * 1024, [4, 1024])
    rbc = ar.f32(512)
    diag = ar.f32(128)
    ssq = ar.f32(4); rst = ar.f32(4)
    PH0 = ar.top
    useg = ar.bf(4 * 8 * 128, [4, 8, 128])
    rt1 = ar.f32(1024); rt2 = ar.f32(1024)
    zr = ar.f32(1024); zi = ar.f32(1024); gr = ar.f32(1024); gi = ar.f32(1024)
    hb_r = ar.f32(8 * 129, [8, 129]); hb_i = ar.f32(8 * 129, [8, 129])
    hprev = ar.bf(16 * 2 * 128, [16, 2, 128])
    Gt = ar.bf(1024); sg_ = ar.bf(1024); Gt2 = ar.bf(1024); sg2 = ar.bf(1024)
    h0r = ar.f32(256, [16, 16]); h0i = ar.f32(256, [16, 16])
    hsr = ar.f32(256, [16, 16]); hsi = ar.f32(256, [16, 16])
    hs_st = ar.f32(256)
    A_TOP = ar.top
    ar.top = PH0
    xT = ar.f32(8 * 512, [8, 512])
    qn = ar.bf(4 * 512, [4, 512])
    sqt = ar.bf(512)
    rq = ar.f32(512)
    et = ar.bf(16 * 128, [16, 128]); pt = ar.bf(16 * 128, [16, 128])
    rden = ar.f32(512, [4, 128])
    oatt = ar.f32(4 * 512, [4, 512])
    NGU = 3
    NDN = 3
    WG_OFF = ar.top
    wgu = [ar.bf(2 * 8 * 128, [2, 8, 128]) for _ in range(NGU)]
    wdn = [ar.bf(11 * 128, [11, 128]) for _ in range(NDN)]
    sgb = ar.bf(512)
    kvo = ar.f32(256, [2, 128])
    kTf = ar.f32(256, [2, 128])
    B_TOP = ar.top
    sq8 = aT[:, 0:8, :]
    _aTtail = aT[:, 8:11, :].rearrange("p a b -> p (a b)")
    sqtB = _aTtail[:, 0:512]
    rqB = _aTtail[:, 512:1536].bitcast(F32)
    cbase = WG_OFF
    ckst = ar.raw_f32(cbase, 2048).rearrange("p (a b c) -> p a b c", a=8, b=2)
    KcT = ar.raw_bf(cbase + 2048, 4096).rearrange("p (a b c) -> p a b c", a=16, b=2)
    Vc = ar.raw_bf(cbase + 4096, 2048).rearrange("p (a b) -> p a b", a=16)
    assert NGU * 1024 + NDN * 704 >= 5120
    assert max(A_TOP, B_TOP) < ARENA_WORDS - 1, (A_TOP, B_TOP)

    Bx = [bb("xst0"), bb("xst1")]
    Bjunk = bb("junk"); BaT = bb("aT"); BxnTk = [bb("xnT%d" % i) for i in range(8)]; Bossm = bb("ossm"); Brbc = bb("rbc"); Bdiag = bb("diag")
    Bssq = bb("ssq"); Brst = bb("rst")
    Buseg = bb("useg"); Brt = bb("rt"); Bz = bb("z"); Bg = bb("g"); Bhb = bb("hb"); Bhp = bb("hprev")
    BGt = bb("Gt"); Bsg = bb("sg"); BGt2 = bb("Gt2"); Bsg2 = bb("sg2"); Bh0 = bb("h0"); Bhs = bb("hs")
    BxT = bb("xT"); Bqn = bb("qn"); Bsqt = bb("sqt"); Brq = bb("rq"); Bet = [bb("et0"), bb("et1")]; Bpt = [bb("pt0"), bb("pt1")]
    Brden = bb("rden"); Boatt = bb("oatt"); Bwgu = [bb("wgu%d" % i) for i in range(NGU)]; Bwdn = [bb("wdn%d" % i) for i in range(NDN)]
    Bsgb = bb("sgb"); Bkvo = bb("kvo"); BkTf = bb("kTf"); BsqtB = bb("sqtB"); BrqB = bb("rqB"); Bsq8 = [bb("sq8a"), bb("sq8b")]
    CACHE_B = Bwgu + Bwdn
    xslot = [K.slot("x0"), K.slot("x1")]
    yslot = [K.slot("y0"), K.slot("y1")]
    Bost = [bb("ost0"), bb("ost1")]
    wgslot = [K.slot("wg%d" % i) for i in range(NGU)]
    wdslot = [K.slot("wd%d" % i) for i in range(NDN)]
    oslot = K.slot("out")
    cslot = K.slot("cache")
    stslot = K.slot("stout")
    out_slots.append(oslot)

    def tile_src(gt):
        if gt == 33:
            return xs[:, :]
        return xp[(gt - 1) * 128:gt * 128, :]

    def load_x(gt, par):
        if gt == 0:
            dve(lambda: V.memset(xst[par], 0.0), W=[Bx[par]])
            K.dma(SP, lambda: S.dma_start(out=xst[par][112:128, :], in_=meta[:, :]), xslot[par], W=[Bx[par]])
        else:
            K.dma(SP, lambda: S.dma_start(out=xst[par], in_=tile_src(gt)), xslot[par], W=[Bx[par]])

    XB = [(5, 6, 7), (3, 4, 2)]

    def x_stage1(gt, par, tl):
        ba, bb_, br = XB[tl % 2]
        act(lambda: A.activation(out=junk, in_=xst[par], func=AF.Square, accum_out=ssq[:, tl:tl + 1]), R=[Bx[par]], W=[Bjunk, Bssq])
        act(lambda: A.activation(out=rst[:, tl:tl + 1], in_=ssq[:, tl:tl + 1], func=AF.Ln, scale=1.0 / D, bias=EPS_c[:, 0:1]), R=[Bssq, consts], W=[Brst])
        act(lambda: A.activation(out=rst[:, tl:tl + 1], in_=rst[:, tl:tl + 1], func=AF.Exp, scale=-0.5), R=[Brst], W=[Brst])
        dve(lambda: V.tensor_scalar(out=diag, in0=ident, scalar1=rst[:, tl:tl + 1], scalar2=None, op0=ALU.mult), R=[Brst, consts], W=[Bdiag])
        for half, bk2 in enumerate((ba, bb_)):
            pe([lambda k=k, bk2=bk2: T.matmul(ps[:, bk2, (k % 4) * 128:(k % 4 + 1) * 128], lhsT=xst[par][:, k * 128:(k + 1) * 128], rhs=ident,
                                              is_transpose=True, start=True, stop=True)
                for k in range(4 * half, 4 * half + 4)], R=[Bx[par], consts], W=[pb[bk2]])
        pe(lambda: T.matmul(ps[:, br, 0:128], lhsT=ones_f, rhs=diag, start=True, stop=True), R=[Bdiag, consts], W=[pb[br]])

    def x_stage2(gt, par, tl, want_xT):
        ba, bb_, br = XB[tl % 2]
        act(lambda: A.copy(out=rbc[:, tl * 128:(tl + 1) * 128], in_=ps[:, br, 0:128]), R=[pb[br]], W=[Brbc])
        for half, bk2 in enumerate((ba, bb_)):
            if want_xT:
                dve(lambda half=half, bk2=bk2: V.tensor_copy(out=xT[:, 4 * half:4 * half + 4, tl * 128:(tl + 1) * 128],
                                                             in_=ps[:, bk2, :].rearrange("p (a b) -> p a b", a=4)), R=[pb[bk2]], W=[BxT])
            dve(lambda half=half, bk2=bk2: V.tensor_tensor(
                out=xnT[:, 4 * half:4 * half + 4, tl * 128:(tl + 1) * 128], in0=ps[:, bk2, :].rearrange("p (a b) -> p a b", a=4),
                in1=rbc[:, tl * 128:(tl + 1) * 128].unsqueeze(1).to_broadcast([128, 4, 128]), op=ALU.mult),
                R=[pb[bk2], Brbc], W=BxnTk[4 * half:4 * half + 4])

    XSEQ = []
    for s_i in range(5):
        seg_t = list(range(8 * s_i, 8 * s_i + 8)) if s_i < 4 else [32, 33]
        XSEQ += seg_t
        XSEQ += seg_t
    xq = {"issued": 0, "consumed": 0}

    def _x_fetch():
        while xq["issued"] < min(len(XSEQ), xq["consumed"] + 2):
            i = xq["issued"]
            load_x(XSEQ[i], i % 2)
            xq["issued"] = i + 1

    pend = {"barrier": False}

    def barrier_if_pending():
        if pend["barrier"]:
            K.barrier()
            pend["barrier"] = False

    def run_x_tiles(tiles, want_xT, pre_stage2=None):
        pars = []
        hook = [pre_stage2]

        def st2(j):
            if hook[0] is not None:
                hook[0]()
                hook[0] = None
            x_stage2(tiles[j], pars[j], j, want_xT)

        for i, gt in enumerate(tiles):
            c = xq["consumed"]
            assert XSEQ[c] == gt, (c, XSEQ[c], gt)
            _x_fetch()
            xq["consumed"] = c + 1
            pars.append(c % 2)
            x_stage1(gt, c % 2, i)
            _x_fetch()
            if i >= 1:
                st2(i - 1)
        st2(len(tiles) - 1)

    def phase_A(seg_tiles, is_last, post_on_dve=False):
        ntile = len(seg_tiles)
        Nc = 16 * ntile
        for h0_ in range(0, ntile, 4):
            tl_tiles = seg_tiles[h0_:h0_ + 4]
            TT = 128 * len(tl_tiles)
            run_x_tiles(tl_tiles, False)
            for o in range(4):
                bk = o % 4
                pe([lambda k=k, o=o, bk=bk, TT=TT: T.matmul(ps[:, bk, 0:TT], lhsT=winb[:, k, 896 + o * 128:896 + (o + 1) * 128],
                                                            rhs=xnT[:, k, 0:TT], start=(k == 0), stop=(k == 7)) for k in range(8)],
                   R=BxnTk + [Bwin], W=[pb[bk]])
                c0 = h0_ * 16
                ncl = TT // 8
                barrier_if_pending()
                act(lambda o=o, bk=bk, TT=TT, c0=c0, ncl=ncl: A.copy(out=useg[:, o, :, c0:c0 + ncl],
                                                                   in_=ps[:, bk, 0:TT].rearrange("p (c j) -> p j c", j=8)),
                    R=[pb[bk]], W=[Buseg])
        npr = 16 if is_last else Nc
        if is_last:
            for ri, (src, dst) in enumerate(((sre, h0r), (sim, h0i))):
                for hh in range(2):
                    K.dma(SP, lambda src=src, hh=hh: S.dma_start(out=hs_st[:, 0:128], in_=src[hh * 128:(hh + 1) * 128, :]), cslot, W=[Bhs])
                    bk = 4 + hh
                    pe(lambda bk=bk: T.matmul(ps[:, bk, 0:128], lhsT=hs_st[:, 0:128], rhs=ident, is_transpose=True, start=True, stop=True), R=[Bhs, consts], W=[pb[bk]])
                    act(lambda bk=bk, dst=dst, hh=hh: A.copy(out=dst[:, :, hh * 8:(hh + 1) * 8],
                                                           in_=ps[:, bk, 0:128].rearrange("p (s P) -> p P s", s=8)), R=[pb[bk]], W=[Bh0])
        for hf in range(2):
            for o in (2 * hf, 2 * hf + 1):
                o2 = o % 2
                for part in range(2):
                    sl = part * 2 + o2
                    grp = []
                    for i in range(8):
                        for pl in range(4):
                            grp.append(lambda i=i, o=o, pl=pl, part=part, sl=sl: T.matmul(
                                ps[:, pl, sl * 128:sl * 128 + Nc], lhsT=BL[32 * pl:32 * pl + 32, o, i, part, :],
                                rhs=useg[32 * pl:32 * pl + 32, o, i, 0:Nc], start=(i == 0), stop=(i == 7), tile_position=(32 * pl, 0)))
                    pe(grp, R=[Buseg] + SSMW, W=[pb[0], pb[1], pb[2], pb[3]])
            def sview(part):
                return ps[:, 0:4, :].rearrange("p b (s m) -> p s b m", s=4)[:, part * 2:part * 2 + 2, :, 0:Nc]
            def tabv(tb_, lo, n_):
                return tb_[:, 8 * hf:8 * hf + 8, lo:lo + n_].rearrange("p (a b) m -> p a b m", a=2)
            def v8(x, n_):
                return x[:, 0:8 * n_].rearrange("p (a b m) -> p a b m", a=2, b=4)
            PSB = [pb[0], pb[1], pb[2], pb[3]]
            def rot_pre(n_, lo, col0, bc=False):
                def tv(tb_):
                    if bc:
                        return tb_[:, 8 * hf:8 * hf + 8, lo:lo + 1].rearrange("p (a b) m -> p a b m", a=2).to_broadcast([128, 2, 4, n_])
                    return tabv(tb_, lo, n_)
                Sr = sview(0)[:, :, :, col0:col0 + n_]; Si = sview(1)[:, :, :, col0:col0 + n_]
                a1 = ps[:, 4:6, :].rearrange("p b c -> p (b c)")[:, 0:8 * n_].rearrange("p (a b m) -> p a b m", a=2, b=4)
                a2 = v8(rt2, n_)
                zrv = v8(zr, Nc)[:, :, :, col0:col0 + n_] if False else zr[:, 0:8 * Nc].rearrange("p (a b m) -> p a b m", a=2, b=4)[:, :, :, col0:col0 + n_]
                ziv = zi[:, 0:8 * Nc].rearrange("p (a b m) -> p a b m", a=2, b=4)[:, :, :, col0:col0 + n_]
                PT = [pb[4], pb[5]]
                dve(lambda: V.tensor_tensor(out=a1, in0=Sr, in1=tv(cs), op=ALU.mult), R=PSB + [Btab], W=PT)
                dve(lambda: V.tensor_tensor(out=a2, in0=Si, in1=tv(sn), op=ALU.mult), R=PSB + [Btab], W=[Brt])
                dve(lambda: V.tensor_tensor(out=zrv, in0=a1, in1=a2, op=ALU.add), R=[Brt] + PT, W=[Bz])
                dve(lambda: V.tensor_tensor(out=a1, in0=Si, in1=tv(cs), op=ALU.mult), R=PSB + [Btab], W=PT)
                dve(lambda: V.tensor_tensor(out=a2, in0=Sr, in1=tv(sn), op=ALU.mult), R=PSB + [Btab], W=[Brt])
                dve(lambda: V.tensor_tensor(out=ziv, in0=a1, in1=a2, op=ALU.subtract), R=[Brt] + PT, W=[Bz])
            rot_pre(npr, 0, 0)
            if is_last:
                rot_pre(16, 0, 16, bc=True)
            def zv(x):
                return x[:, 0:8 * Nc].rearrange("p (l m) -> p l m", l=8)
            for lp in range(8):
                P_ = 8 * hf + lp
                for (zsrc, gdst, hl) in ((zr, gr, hl_r), (zi, gi, hl_i)):
                    dve(lambda lp=lp, P_=P_, zsrc=zsrc, gdst=gdst, hl=hl: V.tensor_tensor_scan(
                        out=zv(gdst)[:, lp, 0:npr], data0=rho8[:, P_:P_ + 1].to_broadcast([128, npr]), data1=zv(zsrc)[:, lp, 0:npr],
                        initial=hl[:, P_:P_ + 1], op0=ALU.mult, op1=ALU.add), R=[Bz, Bhl, consts], W=[Bg])
                    if is_last:
                        h0 = h0r if hl is hl_r else h0i
                        dve(lambda lp=lp, P_=P_, zsrc=zsrc, gdst=gdst, h0=h0: V.scalar_tensor_tensor(
                            out=zv(gdst)[:, lp, 16:32], in0=h0[:, P_, :], scalar=rho8[:, P_:P_ + 1], in1=zv(zsrc)[:, lp, 16:32],
                            op0=ALU.mult, op1=ALU.add), R=[Bz, Bh0, consts], W=[Bg])
            def rot_post(n_, lo, col0, bc=False):
                def tv(tb_):
                    if bc:
                        return tb_[:, 8 * hf:8 * hf + 8, lo:lo + 1].to_broadcast([128, 8, n_])
                    return tb_[:, 8 * hf:8 * hf + 8, lo:lo + n_]
                grv = zv(gr)[:, :, col0:col0 + n_]; giv = zv(gi)[:, :, col0:col0 + n_]
                hr_ = hb_r[:, :, 1 + col0:1 + col0 + n_]; hi_ = hb_i[:, :, 1 + col0:1 + col0 + n_]
                b1 = ps[:, 6:8, :].rearrange("p b c -> p (b c)")[:, 0:8 * n_].rearrange("p (l m) -> p l m", l=8)
                b2 = zi[:, 0:8 * n_].rearrange("p (l m) -> p l m", l=8)
                PT2 = [pb[6], pb[7]]
                dve(lambda: V.tensor_tensor(out=b1, in0=grv, in1=tv(cs), op=ALU.mult), R=[Bg, Btab], W=PT2)
                dve(lambda: V.tensor_tensor(out=b2, in0=giv, in1=tv(sn), op=ALU.mult), R=[Bg, Btab], W=[Bz])
                dve(lambda: V.tensor_tensor(out=hr_, in0=b1, in1=b2, op=ALU.subtract), R=[Bz] + PT2, W=[Bhb])
                dve(lambda: V.tensor_tensor(out=b1, in0=giv, in1=tv(cs), op=ALU.mult), R=[Bg, Btab], W=PT2)
                dve(lambda: V.tensor_tensor(out=b2, in0=grv, in1=tv(sn), op=ALU.mult), R=[Bg, Btab], W=[Bz])
                dve(lambda: V.tensor_tensor(out=hi_, in0=b1, in1=b2, op=ALU.add), R=[Bz] + PT2, W=[Bhb])
            dve(lambda: V.tensor_copy(out=hb_r[:, :, 0], in_=hl_r[:, 8 * hf:8 * hf + 8]), R=[Bhl], W=[Bhb])
            dve(lambda: V.tensor_copy(out=hb_i[:, :, 0], in_=hl_i[:, 8 * hf:8 * hf + 8]), R=[Bhl], W=[Bhb])
            rot_post(npr, 0, 0)
            if is_last:
                rot_post(16, 0, 16, bc=True)
            dve(lambda: V.tensor_copy(out=hl_r[:, 8 * hf:8 * hf + 8], in_=hb_r[:, :, npr]), R=[Bhb], W=[Bhl])
            dve(lambda: V.tensor_copy(out=hl_i[:, 8 * hf:8 * hf + 8], in_=hb_i[:, :, npr]), R=[Bhb], W=[Bhl])
            act(lambda: A.copy(out=hprev[:, 8 * hf:8 * hf + 8, 0, 0:npr], in_=hb_r[:, :, 0:npr]), R=[Bhb], W=[Bhp])
            act(lambda: A.copy(out=hprev[:, 8 * hf:8 * hf + 8, 1, 0:npr], in_=hb_i[:, :, 0:npr]), R=[Bhb], W=[Bhp])
            if is_last:
                act(lambda: A.copy(out=hprev[:, 8 * hf:8 * hf + 8, 0, 16:32], in_=h0r[:, 8 * hf:8 * hf + 8, :]), R=[Bh0], W=[Bhp])
                act(lambda: A.copy(out=hprev[:, 8 * hf:8 * hf + 8, 1, 16:32], in_=h0i[:, 8 * hf:8 * hf + 8, :]), R=[Bh0], W=[Bhp])
                dve(lambda: V.tensor_copy(out=hsr[:, 8 * hf:8 * hf + 8, :], in_=hb_r[:, :, 17:33]), R=[Bhb], W=[Bhs])
                dve(lambda: V.tensor_copy(out=hsi[:, 8 * hf:8 * hf + 8, :], in_=hb_i[:, :, 17:33]), R=[Bhb], W=[Bhs])
        for o in range(4):
            yb = 4 if o % 2 == 0 else 0
            gb0 = 6 if o % 2 == 0 else 2
            Gt_o = (Gt, Gt2)[o % 2]; sg_o = (sg_, sg2)[o % 2]; BGt_o = (BGt, BGt2)[o % 2]; Bsg_o = (Bsg, Bsg2)[o % 2]
            banks = [pb[yb], pb[yb + 1]]
            grp = []
            merged = (Nc == 128)
            if merged:
                for tau in range(8):
                    for jb in range(2):
                        jlo = max(tau, 4 * jb)
                        jhi = 4 * jb + 4
                        if jlo >= jhi:
                            continue
                        grp.append(lambda tau=tau, jb=jb, jlo=jlo, jhi=jhi: T.matmul(
                            ps[:, yb + jb, (jlo - 4 * jb) * 128:512], lhsT=KT[:, o, tau, :],
                            rhs=useg[:, o, jlo - tau:jhi - tau, :].rearrange("p a b -> p (a b)"), start=(tau == 0), stop=False))
            for j in range(8):
                bk = yb + j // 4
                sl = j % 4
                dst = ps[:, bk, sl * 128:sl * 128 + Nc]
                for tau in range(j + 1):
                    if merged:
                        break
                    grp.append(lambda dst=dst, tau=tau, j=j: T.matmul(dst, lhsT=KT[:, o, tau, :], rhs=useg[:, o, j - tau, 0:Nc],
                                                                      start=(tau == 0), stop=False))
                for pl in range(4):
                    P_ = 4 * o + pl
                    for part in range(2):
                        last = (pl == 3 and part == 1) and ((not merged) or j in (3, 7))
                        grp.append(lambda bk=bk, sl=sl, pl=pl, P_=P_, part=part, j=j, last=last: T.matmul(
                            ps[32 * pl:32 * pl + 32, bk, sl * 128:sl * 128 + Nc], lhsT=CL[:, j, part, 32 * P_:32 * P_ + 32],
                            rhs=hprev[:, P_, part, 0:Nc], start=False, stop=last, tile_position=(0, 32 * pl)))
            pe(grp, R=[Buseg, Bhp] + SSMW, W=banks)
            yv = ps[:, yb:yb + 2, :].rearrange("p b (s m) -> p (b s) m", s=4)[:, :, 0:Nc]
            Gv = Gt_o[:, 0:8 * Nc].rearrange("p (j m) -> p j m", j=8)
            act(lambda yv=yv, Gv=Gv: A.activation(out=Gv, in_=yv, func=AF.Gelu_apprx_tanh), R=banks, W=[BGt_o])
            tot = 8 * Nc
            pieces = [(0, min(512, tot))] + ([(512, tot)] if tot > 512 else [])
            gb = [pb[gb0], pb[gb0 + 1]]
            for pi, (lo, hi) in enumerate(pieces):
                pe(lambda lo=lo, hi=hi, pi=pi: T.matmul(ps[:, gb0 + pi, 0:hi - lo], lhsT=glub[:, o, :], rhs=Gt_o[:, lo:hi], start=True, stop=True),
                   R=[BGt_o, Bglu], W=[gb[pi]])
                act(lambda lo=lo, hi=hi, pi=pi: A.activation(out=sg_o[:, lo:hi], in_=ps[:, gb0 + pi, 0:hi - lo], func=AF.Sigmoid,
                                                             bias=bglu_c[:, o:o + 1]), R=[gb[pi], Bpar], W=[Bsg_o])
            dve(lambda o=o, Gv=Gv: V.tensor_tensor(out=ossm[:, o, 0:8 * Nc].rearrange("p (m j) -> p j m", j=8), in0=Gv,
                                                   in1=sg_o[:, 0:8 * Nc].rearrange("p (j m) -> p j m", j=8), op=ALU.mult),
                R=[BGt_o, Bsg_o], W=[Bossm])

    NMT = 9
    GU_TOTAL = NF * NMT
    DN_TOTAL = 16 * NMT
    wq = {"gu_i": 0, "gu_c": 0, "dn_i": 0, "dn_c": 0, "gu_lim": 0, "dn_lim": 0, "mt": 0}

    def _issue_gu():
        i = wq["gu_i"]
        f_ = i % NF
        par = i % NGU
        wq["gu_i"] = i + 1
        for gi_, wsrc in enumerate((sgate, sup)):
            K.dma(SP, lambda gi_=gi_, wsrc=wsrc, par=par, f_=f_: S.dma_start(
                out=wgu[par][:, gi_, :, :].rearrange("p k c -> p (k c)"), in_=wsrc[f_]), wgslot[par], R=[Bscr], W=[Bwgu[par]])

    def _issue_dn():
        i = wq["dn_i"]
        par = i % NDN
        wq["dn_i"] = i + 1
        K.dma(SP, lambda par=par, i=i: S.dma_start(out=wdn[par].rearrange("p f c -> p (f c)"), in_=sdown[i % 16]),
              wdslot[par], R=[Bscr], W=[Bwdn[par]])

    def use_gu():
        conv_issue(100)
        if Bscr.w is not None and Bscr.w[0] is cvslot:
            Bscr.w = (cvslot, cvslot.n)
        c = wq["gu_c"]
        while wq["gu_i"] < min(wq["gu_lim"], c + NGU):
            _issue_gu()
        wq["gu_c"] = c + 1
        return c % NGU

    def use_dn():
        c = wq["dn_c"]
        while wq["dn_i"] < min(wq["dn_lim"], c + NDN):
            _issue_dn()
        wq["dn_c"] = c + 1
        return c % NDN

    def prefetch_w():
        while wq["gu_i"] < min(wq["gu_lim"], wq["gu_c"] + NGU):
            _issue_gu()
        while wq["dn_i"] < min(wq["dn_lim"], wq["dn_c"] + NDN):
            _issue_dn()

    def rstd_from_ms(bk, TT, dst, Rb, Wb):
        act(lambda: A.activation(out=dst[:, 0:TT], in_=ps[:, bk, 0:TT], func=AF.Ln, bias=EPS_c[:, 0:1]), R=[pb[bk]] + Rb, W=Wb)
        act(lambda: A.activation(out=dst[:, 0:TT], in_=dst[:, 0:TT], func=AF.Exp, scale=-0.5), R=Wb, W=Wb)

    def rstd_in_psum(bk, TT):
        act(lambda: A.activation(out=rq[:, 0:TT], in_=ps[:, bk, 0:TT], func=AF.Ln, bias=EPS_c[:, 0:1]), R=[pb[bk]], W=[Brq])
        act(lambda: A.activation(out=ps[:, bk, 0:TT], in_=rq[:, 0:TT], func=AF.Exp, scale=-0.5), R=[Brq], W=[pb[bk]])

    def phase_B(tiles, seg_off, last_in_seg):
        nt = len(tiles)
        TT = 128 * nt
        wq["mt"] += 1
        wq["gu_lim"] = wq["mt"] * NF if last_in_seg else min(GU_TOTAL, (wq["mt"] + 1) * NF)
        wq["dn_lim"] = wq["mt"] * 16 if last_in_seg else min(DN_TOTAL, (wq["mt"] + 1) * 16)
        has_sample = (33 in tiles)

        def _pre():
            barrier_if_pending()
            if not has_sample:
                prefetch_w()

        run_x_tiles(tiles, True, pre_stage2=_pre)
        milestone('B.x')
        def proj(mt):
            pe([lambda k=k: T.matmul(ps[:, mt, 0:TT], lhsT=winb[:, k, mt * 128:(mt + 1) * 128], rhs=xnT[:, k, 0:TT],
                                     start=(k == 0), stop=(k == 7)) for k in range(8)], R=BxnTk + [Bwin], W=[pb[mt]])
        grp = []
        for tl in range(nt):
            for k in range(8):
                grp.append(lambda tl=tl, k=k: T.matmul(ps[:, 6, tl * 128:(tl + 1) * 128], lhsT=xnT[:, k, tl * 128:(tl + 1) * 128],
                                                       rhs=winb[:, k, 768:896], start=(k == 0), stop=(k == 7)))
        pe(grp, R=BxnTk + [Bwin], W=[pb[6]])
        act(lambda: A.copy(out=vtok[:, 1:1 + nt, :], in_=ps[:, 6, 0:TT].rearrange("p (t c) -> p t c", t=nt)), R=[pb[6]], W=[Bvt])
        for tl, gt in enumerate(tiles):
            if gt in (32, 33):
                act(lambda tl=tl: A.copy(out=kvo[:, 1, :], in_=ps[:, 6, tl * 128:(tl + 1) * 128]), R=[pb[6]], W=[Bkvo])
                if gt == 32:
                    K.dma(SP, lambda: S.dma_start(out=vwp[:, :], in_=kvo[:, 1, :]), oslot, R=[Bkvo])
                else:
                    for s_ in range(16):
                        K.dma(SP, lambda s_=s_: S.dma_start(out=vws[s_, 120:128, :], in_=kvo[8 * s_:8 * s_ + 8, 1, :]), oslot, R=[Bkvo])
        milestone('B.proj')
        sq_b = [sqt, sqtB]; rq_b = [rq, rqB]; Bsq_b = [[Bsqt], [BsqtB]]; Brq_b = [[Brq], [BrqB]]

        def qk_A(mt):
            pp = mt % 2
            act(lambda: A.activation(out=sq_b[pp][:, 0:TT], in_=ps[:, mt, 0:TT], func=AF.Square), R=[pb[mt]], W=Bsq_b[pp])
            pe(lambda: T.matmul(ps[:, 6 + pp, 0:TT], lhsT=blk64, rhs=sq_b[pp][:, 0:TT], start=True, stop=True), R=Bsq_b[pp] + [consts], W=[pb[6 + pp]])

        def qk_B(mt):
            pp = mt % 2
            rqc = rq_b[pp]
            rstd_from_ms(6 + pp, TT, rqc, [], Brq_b[pp])
            if mt < 4:
                dve(lambda: V.scalar_tensor_tensor(out=qn[:, mt, 0:TT], in0=ps[:, mt, 0:TT], scalar=gq_c[:, 0:1], in1=rqc[:, 0:TT],
                                                   op0=ALU.mult, op1=ALU.mult), R=[pb[mt], Bpar] + Brq_b[pp], W=[Bqn])
                return
            kv = mt - 4
            dve(lambda: V.scalar_tensor_tensor(out=kdupT[:, kv, 128:128 + TT], in0=ps[:, mt, 0:TT], scalar=gk_c[:, 0:1],
                                               in1=rqc[:, 0:TT], op0=ALU.mult, op1=ALU.mult), R=[pb[mt], Bpar] + Brq_b[pp], W=[Bkd])
            for tl, gt in enumerate(tiles):
                if gt in (32, 33):
                    lo, hi = kv * 64, kv * 64 + 64
                    dve(lambda tl=tl, lo=lo, hi=hi: V.scalar_tensor_tensor(
                        out=kTf[lo:hi, tl % 2, :], in0=ps[lo:hi, mt, tl * 128:(tl + 1) * 128], scalar=gk_c[lo:hi, 0:1],
                        in1=rqc[lo:hi, tl * 128:(tl + 1) * 128], op0=ALU.mult, op1=ALU.mult), R=[pb[mt], Bpar] + Brq_b[pp], W=[BkTf])
                    if kv == 1:
                        pe(lambda tl=tl: T.matmul(ps[:, 0, 0:128], lhsT=kTf[:, tl % 2, :], rhs=ident, is_transpose=True, start=True, stop=True),
                           R=[BkTf, consts], W=[pb[0]])
                        act(lambda: A.copy(out=kvo[:, 0, :], in_=ps[:, 0, 0:128]), R=[pb[0]], W=[Bkvo])
                        if gt == 32:
                            K.dma(SP, lambda: S.dma_start(out=kwp[:, :], in_=kvo[:, 0, :]), oslot, R=[Bkvo])
                        else:
                            for s_ in range(16):
                                K.dma(SP, lambda s_=s_: S.dma_start(out=kws[s_, 120:128, :], in_=kvo[8 * s_:8 * s_ + 8, 0, :]), oslot, R=[Bkvo])

        act(lambda: A.activation(out=sq8[:, 4:8, 0:TT], in_=ossm[:, :, seg_off:seg_off + TT], func=AF.Square), R=[Bossm], W=[Bsq8[1], BaT])
        proj(0); proj(1); qk_A(0); proj(2); qk_A(1)
        for mt in range(6):
            qk_B(mt)
            if mt + 3 < 6:
                proj(mt + 3)
            if mt + 2 < 6:
                qk_A(mt + 2)
        pe([lambda k=k: T.matmul(ps[:, 6, 0:TT], lhsT=onesA, rhs=sq8[:, 4 + k, 0:TT], start=(k == 0), stop=(k == 3)) for k in range(4)],
           R=[Bsq8[1], consts], W=[pb[6]])
        rstd_in_psum(6, TT)
        for k in range(4):
            dve(lambda k=k: V.scalar_tensor_tensor(out=xnT[:, 4 + k, 0:TT], in0=ossm[:, k, seg_off:seg_off + TT], scalar=gssm_c[:, k:k + 1],
                                                   in1=ps[:, 6, 0:TT], op0=ALU.mult, op1=ALU.mult), R=[Bossm, pb[6], Bpar], W=[BxnTk[4 + k]])
        if tiles[0] == 0:
            dump('qn', qn, [Bqn]); dump('kdupT', kdupT, [Bkd]); dump('vtok', vtok, [Bvt]); dump('xT0', xT, [BxT]); dump('xnT_B', xnT, BxnTk)
        milestone('B.qknorm')
        et5 = et.rearrange("p (par t b) q -> p par t b q", par=2, t=4)
        pt5 = pt.rearrange("p (par t b) q -> p par t b q", par=2, t=4)
        units = [(tl, kvg) for tl in range(nt) for kvg in range(2)]

        def masks_for(gt):
            samp = (gt == 33)
            mk_cur = m0cur if gt == 0 else (msamp if samp else mcur)
            mk_prev = mzero if gt == 0 else (mcache if samp else (m1prev if gt == 1 else mprev))
            return mk_prev, mk_cur

        def att_S(tl, kvg):
            gt = tiles[tl]
            samp = (gt == 33)
            if samp and kvg == 0:
                sample_cache_prep()
            grp = []
            for h in range(4 * kvg, 4 * kvg + 4):
                par = h % 2
                base = par * 64
                kv = kvg
                hp = par * 4 + h // 2
                for blk in range(2):
                    idx = hp * 2 + blk
                    kc0 = tl * 128 + blk * 128
                    if samp and blk == 0:
                        var = 0 if kv == par else 1
                        for s_ in range(16):
                            grp.append(lambda var=var, idx=idx, s_=s_, h=h, base=base: T.matmul(
                                ps[:, idx // 4, (idx % 4) * 128 + s_ * 8:(idx % 4) * 128 + s_ * 8 + 8], lhsT=KcT[base:base + 64, s_, var, :],
                                rhs=qn[base:base + 64, h // 2, tl * 128 + s_ * 8:tl * 128 + s_ * 8 + 8], start=True, stop=True,
                                tile_position=(base, 0)))
                        continue
                    grp.append(lambda kv=kv, idx=idx, kc0=kc0, h=h, base=base: T.matmul(
                        ps[:, idx // 4, (idx % 4) * 128:(idx % 4 + 1) * 128], lhsT=kdupT[base:base + 64, kv, kc0:kc0 + 128],
                        rhs=qn[base:base + 64, h // 2, tl * 128:(tl + 1) * 128], start=True, stop=True, tile_position=(base, 0)))
            pe(grp, R=[Bqn, Bkd] + (CACHE_B if samp else []), W=[pb[kvg], pb[2 + kvg]])

        def att_EM(tl, kvg):
            gt = tiles[tl]
            mk_prev, mk_cur = masks_for(gt)
            for bk in (kvg, 2 + kvg):
                act(lambda bk=bk: A.activation(out=et[:, 4 * bk:4 * bk + 4, :], in_=ps[:, bk, :].rearrange("p (a b) -> p a b", a=4),
                                               func=AF.Exp, scale=SCALE), R=[pb[bk]], W=[Bet[kvg]])
            ts_ = slice(2 * kvg, 2 * kvg + 2)
            for blk, mk in ((0, mk_prev), (1, mk_cur)):
                dve(lambda blk=blk, mk=mk: V.tensor_tensor(out=pt5[:, :, ts_, blk, :], in0=et5[:, :, ts_, blk, :],
                                                           in1=mk.unsqueeze(1).unsqueeze(1).to_broadcast([128, 2, 2, 128]), op=ALU.mult),
                    R=[Bet[kvg], consts], W=[Bpt[kvg]])

        def att_P(tl, kvg):
            gt = tiles[tl]
            samp = (gt == 33)
            bo = 4 + 2 * (tl % 2)
            bd_ = bo + 1
            grp = []
            grp2 = []
            for h in range(4 * kvg, 4 * kvg + 4):
                par = h % 2
                base = par * 64
                kv = kvg
                hp = par * 4 + h // 2
                t2 = h // 2
                for blk in range(2):
                    idx = hp * 2 + blk
                    if samp and blk == 0:
                        for s_ in range(16):
                            grp.append(lambda kv=kv, idx=idx, t2=t2, s_=s_, base=base: T.matmul(
                                ps[base:base + 64, bo, t2 * 128 + s_ * 8:t2 * 128 + s_ * 8 + 8], lhsT=Vc[:, s_, kv * 64:(kv + 1) * 64],
                                rhs=pt[:, idx, s_ * 8:s_ * 8 + 8], start=(s_ == 0), stop=False, tile_position=(0, base)))
                    else:
                        grp.append(lambda kv=kv, idx=idx, t2=t2, blk=blk, base=base: T.matmul(
                            ps[base:base + 64, bo, t2 * 128:(t2 + 1) * 128], lhsT=vtok[:, tl + blk, kv * 64:(kv + 1) * 64], rhs=pt[:, idx, :],
                            start=(blk == 0), stop=(blk == 1), tile_position=(0, base)))
                    grp2.append(lambda idx=idx, t2=t2, blk=blk, base=base: T.matmul(
                        ps[base:base + 64, bd_, t2 * 128:(t2 + 1) * 128], lhsT=ones_b[:, 0:64], rhs=pt[:, idx, :],
                        start=(blk == 0), stop=(blk == 1), tile_position=(0, base)))
            mix_ = []
            for a_, b_ in zip(grp, grp2):
                mix_ += [a_, b_]
            mix_ += grp[len(grp2):]
            pe(mix_ if not samp else grp + grp2, R=[Bpt[kvg], Bvt, consts] + (CACHE_B if samp else []), W=[pb[bo], pb[bd_]])

        def att_F(tl):
            bo = 4 + 2 * (tl % 2)
            bd_ = bo + 1
            for t2 in range(4):
                act(lambda t2=t2: A.activation(out=rden[:, t2, :], in_=ps[:, bd_, t2 * 128:(t2 + 1) * 128], func=AF.Ln, bias=esk[:, t2:t2 + 1]),
                    R=[pb[bd_], Bpar], W=[Brden])
            act(lambda: A.activation(out=rden, in_=rden, func=AF.Exp, scale=-1.0), R=[Brden], W=[Brden])
            dve(lambda: V.tensor_tensor(out=oatt[:, :, tl * 128:(tl + 1) * 128], in0=ps[:, bo, :].rearrange("p (a b) -> p a b", a=4),
                                        in1=rden, op=ALU.mult), R=[pb[bo], Brden], W=[Boatt] + Bost)
            pool(lambda: G.tensor_tensor(out=sq8[:, 0:4, tl * 128:(tl + 1) * 128], in0=oatt[:, :, tl * 128:(tl + 1) * 128],
                                         in1=oatt[:, :, tl * 128:(tl + 1) * 128], op=ALU.mult), R=[Boatt], W=[Bsq8[0]])

        nu = len(units)
        att_S(*units[0])
        if nu > 1:
            att_S(*units[1])
        att_EM(*units[0])
        for ui, (tl, par) in enumerate(units):
            if ui + 2 < nu:
                att_S(*units[ui + 2])
            if ui + 1 < nu:
                att_EM(*units[ui + 1])
            att_P(tl, par)
            if par == 1:
                att_F(tl)
        if tiles[0] == 0:
            dump('oatt', oatt, [Boatt]); dump('rden', rden, [Brden])
        if has_sample:
            prefetch_w()
        milestone('B.att')
        dve(lambda: V.tensor_copy(out=kdupT[:, :, 0:128], in_=kdupT[:, :, TT:TT + 128]), R=[Bkd], W=[Bkd])
        dve(lambda: V.tensor_copy(out=vtok[:, 0, :], in_=vtok[:, nt, :]), R=[Bvt], W=[Bvt])
        pe([lambda k=k: T.matmul(ps[:, 6, 0:TT], lhsT=onesA, rhs=sq8[:, k, 0:TT], start=(k == 0), stop=(k == 3)) for k in range(4)],
           R=[Bsq8[0], consts], W=[pb[6]])
        rstd_in_psum(6, TT)
        for k in range(4):
            dve(lambda k=k: V.scalar_tensor_tensor(out=xnT[:, k, 0:TT], in0=oatt[:, k, 0:TT], scalar=gatt_c[:, k:k + 1], in1=ps[:, 6, 0:TT],
                                                   op0=ALU.mult, op1=ALU.mult), R=[Boatt, pb[6], Bpar], W=[BxnTk[k]])
        if tiles[0] == 0:
            dump('mix', xnT, BxnTk)
        milestone('B.mix')
        for m in range(8):
            bk = m % 4
            if m == 0:
                for i_, k in enumerate((4, 5, 6, 7, 0, 1, 2, 3)):
                    pe(lambda k=k, m=m, bk=bk, i_=i_: T.matmul(ps[:, bk, 0:TT], lhsT=woutb[:, k, m * 128:(m + 1) * 128], rhs=xnT[:, k, 0:TT],
                                                               start=(i_ == 0), stop=(i_ == 7)), R=[BxnTk[k], Bwout], W=[pb[bk]])
            else:
                pe([lambda k=k, m=m, bk=bk: T.matmul(ps[:, bk, 0:TT], lhsT=woutb[:, k, m * 128:(m + 1) * 128], rhs=xnT[:, k, 0:TT],
                                                     start=(k == 0), stop=(k == 7)) for k in range(8)], R=BxnTk + [Bwout], W=[pb[bk]])
            dve(lambda m=m, bk=bk: V.tensor_tensor(out=xT[:, m, 0:TT], in0=ps[:, bk, 0:TT], in1=xT[:, m, 0:TT], op=ALU.add), R=[pb[bk], BxT], W=[BxT])
            act(lambda m=m: A.activation(out=sq8[:, m, 0:TT], in_=xT[:, m, 0:TT], func=AF.Square), R=[BxT], W=[Bsq8[m // 4]])
        if tiles[0] == 0:
            dump('hT', xT, [BxT])
        milestone('B.wout')
        pe([lambda k=k: T.matmul(ps[:, 6, 0:TT], lhsT=onesF, rhs=sq8[:, k, 0:TT], start=(k == 0), stop=(k == 7)) for k in range(8)],
           R=Bsq8 + [consts], W=[pb[6]])
        rstd_in_psum(6, TT)
        for k in range(8):
            dve(lambda k=k: V.scalar_tensor_tensor(out=xnT[:, k, 0:TT], in0=xT[:, k, 0:TT], scalar=gffn_c[:, k:k + 1], in1=ps[:, 6, 0:TT],
                                                   op0=ALU.mult, op1=ALU.mult), R=[BxT, pb[6], Bpar], W=[BxnTk[k]])
        if tiles[0] == 0:
            dump('fT', xnT, BxnTk)
        milestone('B.ffnnorm')
        for half in range(2):
            for fl in range(11):
                par = use_gu()
                bg, bu = (0, 1) if fl % 2 == 0 else (2, 3)
                if half == 0 and fl == 0:
                    for k in range(8):
                        pe(lambda k=k, par=par, bg=bg: T.matmul(ps[:, bg, 0:TT], lhsT=wgu[par][:, 0, k, :], rhs=xnT[:, k, 0:TT], start=(k == 0), stop=(k == 7)),
                           R=[BxnTk[k], Bwgu[par]], W=[pb[bg]])
                else:
                    pe([lambda k=k, par=par, bg=bg: T.matmul(ps[:, bg, 0:TT], lhsT=wgu[par][:, 0, k, :], rhs=xnT[:, k, 0:TT], start=(k == 0), stop=(k == 7))
                        for k in range(8)], R=BxnTk + [Bwgu[par]], W=[pb[bg]])
                pe([lambda k=k, par=par, bu=bu: T.matmul(ps[:, bu, 0:TT], lhsT=wgu[par][:, 1, k, :], rhs=xnT[:, k, 0:TT], start=(k == 0), stop=(k == 7))
                    for k in range(8)], R=BxnTk + [Bwgu[par]], W=[pb[bu]])
                act(lambda bg=bg: A.activation(out=sgb[:, 0:TT], in_=ps[:, bg, 0:TT], func=AF.Silu), R=[pb[bg]], W=[Bsgb])
                dve(lambda fl=fl, bu=bu: V.tensor_tensor(out=aT[:, fl, 0:TT], in0=ps[:, bu, 0:TT], in1=sgb[:, 0:TT], op=ALU.mult),
                    R=[pb[bu], Bsgb], W=[BaT, BsqtB, BrqB] + Bsq8)
            for m in range(8):
                par = use_dn()
                bk = 4 + (m % 2)
                pe([lambda fl=fl, par=par, bk=bk: T.matmul(ps[:, bk, 0:TT], lhsT=wdn[par][:, fl, :], rhs=aT[:, fl, 0:TT], start=(fl == 0), stop=(fl == 10))
                    for fl in range(11)], R=[BaT, Bwdn[par]], W=[pb[bk]])
                dve(lambda m=m, bk=bk: V.tensor_tensor(out=xT[:, m, 0:TT], in0=ps[:, bk, 0:TT], in1=xT[:, m, 0:TT], op=ALU.add), R=[pb[bk], BxT], W=[BxT])
        prefetch_w()
        if tiles[0] == 0:
            dump('yT', xT, [BxT])
        milestone('B.ffn')
        ost = [oatt[:, 0:2, :].rearrange("p a b -> p (a b)"), oatt[:, 2:4, :].rearrange("p a b -> p (a b)")]
        for tl, gt in enumerate(tiles):
            if gt == 0:
                continue
            par = tl % 2
            for half in range(2):
                bk = 6 + half
                pe([lambda m=m, bk=bk, tl=tl: T.matmul(ps[:, bk, (m % 4) * 128:(m % 4 + 1) * 128], lhsT=xT[:, m, tl * 128:(tl + 1) * 128], rhs=ident, is_transpose=True, start=True, stop=True)
                    for m in range(4 * half, 4 * half + 4)], R=[BxT, consts], W=[pb[bk]])
                if half == 0:
                    act(lambda half=half, bk=bk, par=par: A.copy(out=ost[par][:, 512 * half:512 * half + 512], in_=ps[:, bk, :]), R=[pb[bk]], W=[Bost[par]])
                else:
                    dve(lambda half=half, bk=bk, par=par: V.tensor_copy(out=ost[par][:, 512 * half:512 * half + 512], in_=ps[:, bk, :]), R=[pb[bk]], W=[Bost[par]])
            dst = ys[:, :] if gt == 33 else yp[(gt - 1) * 128:gt * 128, :]
            K.dma(SP, lambda dst=dst, par=par: S.dma_start(out=dst, in_=ost[par]), yslot[par], R=[Bost[par]])

    def sample_cache_prep():
        K.dma(POOL, lambda: G.dma_start(out=Vc, in_=cv.rearrange("s k c -> k s c")), cslot, W=CACHE_B)
        K.dma(SP, lambda: S.dma_start(out=kws[:, 0:120, :], in_=ck[:, 8:128, :]), oslot)
        K.dma(SP, lambda: S.dma_start(out=vws[:, 0:120, :], in_=cv[:, 8:128, :]), oslot)
        ckv = ck.rearrange("s k c -> k s c")
        for g8 in range(2):
            K.dma(SP, lambda g8=g8: S.dma_start(out=ckst[:, :, 0, :], in_=ckv[:, 8 * g8:8 * g8 + 8, :]), cslot, W=CACHE_B)
            K.dma(SP, lambda g8=g8: S.dma_start(out=ckst[:, :, 1, 0:64], in_=ckv[:, 8 * g8:8 * g8 + 8, 64:128]), cslot, W=CACHE_B)
            K.dma(SP, lambda g8=g8: S.dma_start(out=ckst[:, :, 1, 64:128], in_=ckv[:, 8 * g8:8 * g8 + 8, 0:64]), cslot, W=CACHE_B)
            for s8 in range(8):
                for var in range(2):
                    idx = s8 * 2 + var
                    bk = 6 + (idx // 4) % 2
                    pe(lambda s8=s8, var=var, bk=bk, idx=idx: T.matmul(ps[:, bk, (idx % 4) * 128:(idx % 4 + 1) * 128], lhsT=ckst[:, s8, var, :], rhs=ident, is_transpose=True, start=True, stop=True),
                       R=CACHE_B + [consts], W=[pb[bk]])
                    act(lambda s8=s8, var=var, bk=bk, idx=idx, g8=g8: A.copy(out=KcT[:, 8 * g8 + s8, var, :],
                                                                              in_=ps[:, bk, (idx % 4) * 128:(idx % 4 + 1) * 128]),
                        R=[pb[bk]], W=CACHE_B)

    EPS_c = ar.raw_f32(ARENA_WORDS - 1, 1)
    dve(lambda: V.memset(EPS_c, EPS), W=[consts])
    segs = [list(range(8 * s, 8 * s + 8)) for s in range(4)] + [[32, 33]]

    def _main():
        milestone("setup")
        for si, seg in enumerate(segs):
            is_last = (si == 4)
            phase_A(seg, is_last, post_on_dve=(si == 0))
            if is_last:
                for ri, (src_h, dsto) in enumerate(((hl_r, srp), (hl_i, sip))):
                    bk = 4 + ri
                    pe(lambda bk=bk, src_h=src_h: T.matmul(ps[0:16, bk, 0:128], lhsT=src_h, rhs=ident, is_transpose=True, start=True, stop=True), R=[Bhl, consts], W=[pb[bk]])
                    act(lambda bk=bk, ri=ri: A.copy(out=hs_st[0:16, ri * 128:(ri + 1) * 128], in_=ps[0:16, bk, 0:128]), R=[pb[bk]], W=[Bhs])
                    K.dma(SP, lambda dsto=dsto, ri=ri: S.dma_start(out=dsto[:, :], in_=hs_st[0:16, ri * 128:(ri + 1) * 128]), stslot, R=[Bhs])
                K.barrier(slots=[stslot])
                Bhs2 = bb("hs2")
                for ri, (src_h, dsto) in enumerate(((hsr, srs), (hsi, sis))):
                    for hh in range(2):
                        bk = 4 + hh
                        dve(lambda src_h=src_h, hh=hh: V.tensor_copy(out=rt1[:, 0:128].rearrange("p (s P) -> p P s", s=8),
                                                                      in_=src_h[:, :, hh * 8:(hh + 1) * 8]), R=[Bhs, Bhs2], W=[Brt])
                        pe(lambda bk=bk: T.matmul(ps[:, bk, 0:128], lhsT=rt1[:, 0:128], rhs=ident, is_transpose=True, start=True, stop=True), R=[Brt, consts], W=[pb[bk]])
                        act(lambda bk=bk: A.copy(out=rt2[:, 0:128], in_=ps[:, bk, 0:128]), R=[pb[bk]], W=[Bhs2])
                        K.dma(SP, lambda dsto=dsto, hh=hh: S.dma_start(out=dsto[hh * 128:(hh + 1) * 128, :], in_=rt2[:, 0:128]), stslot, R=[Bhs2])
                K.barrier(slots=[stslot, cslot])
                K.barrier()
            else:
                pend["barrier"] = True
            if si == 0:
                dump("useg", useg, [Buseg]); dump("ossm", ossm, [Bossm]); dump("hl_r", hl_r, [Bhl]); dump("hprev", hprev, [Bhp])
                dump("xnT_A", xnT, BxnTk); dump("hb_r", hb_r, [Bhb]); dump("zr", zr, [Bz]); dump("gr", gr, [Bg]); dump("Gt", Gt, [BGt])
            milestone("A%d" % si)
            for mt0 in range(0, len(seg), 4):
                mtl = seg[mt0:mt0 + 4]
                phase_B(mtl, mt0 * 128, mt0 + 4 >= len(seg))
                if mt0 + 4 >= len(seg):
                    pend["barrier"] = True
                milestone("B%d_%d" % (si, mt0))

    try:
        _main()
    except _StopBuild as e:
        print("build truncated at milestone", e)
    K.barrier()
    for sl in [oslot, stslot] + xslot + yslot + ([dbg_slot[0]] if dbg_slot[0] else []):
        if sl.n:
            S.wait_ge(sl.sem, sl.n)
    nc_ctx.__exit__(None, None, None)
    return nc


_NC_CACHE = {}


def kernel(x_prompt, x_sample, cache_k_win, cache_v_win, state_ssm_re, state_ssm_im,
           meta_tokens, g_mix, w_in, g_q, g_k, sinks,
           ssm_a_re, ssm_a_im, ssm_log_dt, ssm_b_re, ssm_b_im, ssm_c_re, ssm_c_im,
           ssm_d, ssm_w_glu, ssm_b_glu, g_att_out, g_ssm_out, w_out,
           g_ffn, w_gate, w_up, w_down):
    f = lambda a: np.ascontiguousarray(np.asarray(a, dtype=np.float32))
    if "nc" not in _NC_CACHE:
        _NC_CACHE["nc"] = build_nc()
    nc = _NC_CACHE["nc"]
    shared = {
        "meta": f(meta_tokens), "g_mix": f(g_mix).reshape(D), "w_in": f(w_in).reshape(D, 1280),
        "g_q": f(g_q).reshape(64), "g_k": f(g_k).reshape(64), "sinks": f(sinks).reshape(8),
        "a_re": f(ssm_a_re).reshape(2048), "a_im": f(ssm_a_im).reshape(2048), "log_dt": f(ssm_log_dt).reshape(32),
        "b_re": f(ssm_b_re).reshape(32768), "b_im": f(ssm_b_im).reshape(32768),
        "c_re": f(ssm_c_re).reshape(512, 64), "c_im": f(ssm_c_im).reshape(512, 64),
        "ssm_d": f(ssm_d).reshape(512), "w_glu": f(ssm_w_glu).reshape(512, 16), "b_glu": f(ssm_b_glu).reshape(512),
        "g_att": f(g_att_out).reshape(512), "g_ssm": f(g_ssm_out).reshape(512), "w_out": f(w_out).reshape(D, D),
        "g_ffn": f(g_ffn).reshape(D), "w_gate": f(w_gate).reshape(D, DFF), "w_up": f(w_up).reshape(D, DFF),
        "w_down": f(w_down).reshape(DFF, D),
    }
    xpf = f(x_prompt); xsf = f(x_sample)
    ckf = f(cache_k_win).reshape(128, 128, 128); cvf = f(cache_v_win).reshape(128, 128, 128)
    sref = f(state_ssm_re).reshape(128, 2048); simf = f(state_ssm_im).reshape(128, 2048)
    in_maps = []
    for b in range(NCORES):
        m = dict(shared)
        m["xp"] = xpf[b]
        m["xs"] = xsf[16 * b:16 * b + 16].reshape(128, D)
        m["ck"] = ckf[16 * b:16 * b + 16]
        m["cv"] = cvf[16 * b:16 * b + 16]
        m["sre"] = sref[16 * b:16 * b + 16].reshape(256, 128)
        m["sim"] = simf[16 * b:16 * b + 16].reshape(256, 128)
        in_maps.append(m)
    res = run_bass_kernel_spmd(nc, in_maps, core_ids=list(range(NCORES)))
    R = res.results
    y_prompt = np.stack([R[b]["yp"] for b in range(NCORES)]).astype(np.float32)
    y_sample = np.concatenate([R[b]["ys"].reshape(16, 8, D) for b in range(NCORES)]).astype(np.float32)
    kwp = np.stack([R[b]["kwp"].reshape(128, 2, 64) for b in range(NCORES)])[None].astype(np.float32)
    vwp = np.stack([R[b]["vwp"].reshape(128, 2, 64) for b in range(NCORES)])[None].astype(np.float32)
    srp = np.stack([R[b]["srp"].reshape(32, 64) for b in range(NCORES)])[None].astype(np.float32)
    sip = np.stack([R[b]["sip"].reshape(32, 64) for b in range(NCORES)])[None].astype(np.float32)
    kws = np.concatenate([R[b]["kws"].reshape(16, 128, 2, 64) for b in range(NCORES)])[None].astype(np.float32)
    vws = np.concatenate([R[b]["vws"].reshape(16, 128, 2, 64) for b in range(NCORES)])[None].astype(np.float32)
    srs = np.concatenate([R[b]["srs"].reshape(16, 32, 64) for b in range(NCORES)])[None].astype(np.float32)
    sis = np.concatenate([R[b]["sis"].reshape(16, 32, 64) for b in range(NCORES)])[None].astype(np.float32)
    return (y_prompt, y_sample, kwp, vwp, srp, sip, kws, vws, srs, sis)
```

```python
import math
import numpy as np
import concourse.bass as bass
import concourse.mybir as mybir
from concourse.bass_utils import run_bass_kernel_spmd

F32 = mybir.dt.float32
BF16 = mybir.dt.bfloat16
AF = mybir.ActivationFunctionType
ALU = mybir.AluOpType

NCORES = 8
D = 1024
DFF = 2816
NF = 22
EPS = 1e-6
NPT = 33
SCALE = 0.125
ARENA_WORDS = 53100
DEBUG = False
STOP_AT = None


class _StopBuild(Exception):
    pass


class Buf:
    __slots__ = ("name", "w", "r", "excl")

    def __init__(self, name, excl=False):
        self.name = name
        self.w = None
        self.r = {}
        self.excl = excl


class EngQ:
    def __init__(self, nc, eng, name, same=False):
        self.eng = eng
        self.name = name
        self.sem = nc.alloc_semaphore("sem_" + name)
        self.n = 0
        self.waited = {}
        self.same = same


class Slot:
    def __init__(self, nc, name):
        self.sem = nc.alloc_semaphore("dsem_" + name)
        self.n = 0
        self.name = name


class KB:
    def __init__(self, nc):
        self.nc = nc
        self.PE = EngQ(nc, nc.tensor, "pe")
        self.ACT = EngQ(nc, nc.scalar, "act", same=True)
        self.DVE = EngQ(nc, nc.vector, "dve", same=True)
        self.POOL = EngQ(nc, nc.gpsimd, "pool", same=True)
        self.SP = EngQ(nc, nc.sync, "sp")
        self.nslots = 0

    def slot(self, name):
        self.nslots += 1
        return Slot(self.nc, name)

    def _deps(self, R, W):
        deps = []
        for b in R:
            if b.w is not None:
                deps.append((b.w, True))
        for b in W:
            if b.w is not None:
                deps.append((b.w, False))
            deps.extend((t, False) for t in b.r.values())
        return deps

    def _wait(self, q, deps):
        for ((obj, val), raw) in deps:
            if obj is q and not (q.same and raw):
                continue
            if isinstance(obj, Slot):
                val = obj.n
            if q.waited.get(id(obj), 0) >= val:
                continue
            q.eng.wait_ge(obj.sem, val)
            q.waited[id(obj)] = val

    def _mark(self, tok, R, W):
        obj, val = tok
        for b in R:
            cur = b.r.get(id(obj))
            if cur is None or cur[1] < val:
                b.r[id(obj)] = tok
        for b in W:
            b.w = tok
            b.r = {}

    def op(self, q, fns, R=(), W=()):
        if not isinstance(fns, (list, tuple)):
            fns = [fns]
        if any(b.excl for b in R):
            W = list(W) + [b for b in R if b.excl]
            R = [b for b in R if not b.excl]
        self._wait(q, self._deps(R, W))
        ins = None
        for f in fns:
            ins = f()
        q.n += 1
        ins.then_inc(q.sem, 1)
        tok = (q, q.n)
        self._mark(tok, R, W)
        return tok

    def dma(self, q, fn, slot, R=(), W=()):
        self._wait(q, self._deps(R, W))
        ins = fn()
        slot.n += 16
        ins.then_inc(slot.sem, 16)
        tok = (slot, slot.n)
        self._mark(tok, R, W)
        return tok

    def barrier(self, qs=None, slots=()):
        qs = qs or [self.PE, self.ACT, self.DVE, self.POOL, self.SP]
        for q in qs:
            for sl in slots:
                if sl.n and q.waited.get(id(sl), 0) < sl.n:
                    q.eng.wait_ge(sl.sem, sl.n)
                    q.waited[id(sl)] = sl.n
            for o in qs:
                if o is q or o.n == 0:
                    continue
                if q.waited.get(id(o), 0) >= o.n:
                    continue
                q.eng.wait_ge(o.sem, o.n)
                q.waited[id(o)] = o.n


class Arena:
    def __init__(self, nc, words):
        self.nc = nc
        self.slab = nc.alloc_sbuf_tensor("arena", [128, words], F32)
        self.base = int(nc.lookup_mloc(self.slab).addr)
        self.words = words
        self.top = 0
        self.peak = 0
        self.n = 0

    def _at(self, off_words, n_elem, dtype):
        self.n += 1
        h = self.nc.alloc_sbuf_tensor_at("b%d" % self.n, [128, n_elem], dtype, offset=self.base + 4 * off_words, align_bytes=4)
        return h[:, :]

    def raw_f32(self, off_words, words):
        return self._at(off_words, words, F32)

    def raw_bf(self, off_words, elems):
        return self._at(off_words, elems, BF16)

    def f32(self, words, shape=None):
        off = self.top
        self.top += words
        self.peak = max(self.peak, self.top)
        assert self.top <= self.words, ("arena overflow", self.top, self.words)
        ap = self._at(off, words, F32)
        if shape is not None:
            ap = self._shape(ap, shape)
        return ap

    def bf(self, elems, shape=None):
        words = (elems + 1) // 2
        off = self.top
        self.top += words
        self.peak = max(self.peak, self.top)
        assert self.top <= self.words, ("arena overflow", self.top, self.words)
        ap = self._at(off, elems, BF16)
        if shape is not None:
            ap = self._shape(ap, shape)
        return ap

    @staticmethod
    def _shape(ap, shape):
        if len(shape) == 2:
            return ap.rearrange("p (a b) -> p a b", a=shape[0])
        if len(shape) == 3:
            return ap.rearrange("p (a b c) -> p a b c", a=shape[0], b=shape[1])
        if len(shape) == 4:
            return ap.rearrange("p (a b c d) -> p a b c d", a=shape[0], b=shape[1], c=shape[2])
        raise ValueError(shape)


def build_nc():
    nc = bass.Bass("TRN2", target_bir_lowering=False)

    def din(name, shape):
        return nc.dram_tensor(name, list(shape), F32, kind="ExternalInput").ap()

    def dout(name, shape):
        return nc.dram_tensor(name, list(shape), F32, kind="ExternalOutput").ap()

    xp = din("xp", [4096, D]); xs = din("xs", [128, D]); meta = din("meta", [16, D])
    ck = din("ck", [16, 128, 128]); cv = din("cv", [16, 128, 128])
    sre = din("sre", [256, 128]); sim = din("sim", [256, 128])
    g_mix = din("g_mix", [D]); w_in = din("w_in", [D, 1280]); g_q = din("g_q", [64]); g_k = din("g_k", [64])
    sinks = din("sinks", [8]); a_re = din("a_re", [2048]); a_im = din("a_im", [2048]); log_dt = din("log_dt", [32])
    b_re = din("b_re", [32768]); b_im = din("b_im", [32768]); c_re = din("c_re", [512, 64]); c_im = din("c_im", [512, 64])
    ssm_d = din("ssm_d", [512]); w_glu = din("w_glu", [512, 16]); b_glu = din("b_glu", [512])
    g_att = din("g_att", [512]); g_ssm = din("g_ssm", [512]); w_out = din("w_out", [D, D]); g_ffn = din("g_ffn", [D])
    w_gate = din("w_gate", [D, DFF]); w_up = din("w_up", [D, DFF]); w_down = din("w_down", [DFF, D])

    yp = dout("yp", [4096, D]); ys = dout("ys", [128, D])
    kwp = dout("kwp", [128, 128]); vwp = dout("vwp", [128, 128])
    srp = dout("srp", [16, 128]); sip = dout("sip", [16, 128])
    kws = dout("kws", [16, 128, 128]); vws = dout("vws", [16, 128, 128])
    srs = dout("srs", [256, 128]); sis = dout("sis", [256, 128])

    K = KB(nc)
    PE, ACT, DVE, POOL, SP = K.PE, K.ACT, K.DVE, K.POOL, K.SP
    T, A, V, G, S = nc.tensor, nc.scalar, nc.vector, nc.gpsimd, nc.sync
    ar = Arena(nc, ARENA_WORDS)
    ps = nc.alloc_psum_tensor("ps", [128, 8, 512], F32)
    pb = [Buf("psb%d" % i, excl=True) for i in range(8)]
    out_slots = []
    ms_ctr = [0]
    dbg_slot = [None]
    dbg_names = []

    def dump(name, ap, bufs):
        if not DEBUG:
            return
        if dbg_slot[0] is None:
            dbg_slot[0] = K.slot("dbg")
        shp = list(ap.shape)
        dt_ = nc.dram_tensor("dbg_" + name, shp, ap.dtype, kind="ExternalOutput").ap()
        dbg_names.append(name)
        K.dma(SP, lambda: S.dma_start(out=dt_, in_=ap), dbg_slot[0], R=bufs)

    def milestone(name):
        ms_ctr[0] += 1
        if STOP_AT is not None and ms_ctr[0] >= STOP_AT:
            raise _StopBuild(name)

    nc_ctx = nc.allow_non_contiguous_dma(reason="small parameter layouts")
    nc_ctx.__enter__()

    ident = ar.f32(128); ones_f = ar.f32(128)
    mcur = ar.bf(128); mprev = ar.bf(128); m1prev = ar.bf(128); m0cur = ar.bf(128); msamp = ar.bf(128); mcache = ar.bf(128); mzero = ar.bf(128)
    blk64 = ar.bf(128); onesA = ar.bf(128); onesF = ar.bf(128); ones_b = ar.bf(128)
    gmix_c = ar.f32(8); gffn_c = ar.f32(8); gatt_c = ar.f32(4); gssm_c = ar.f32(4)
    gq_c = ar.f32(1); gk_c = ar.f32(1); esk = ar.f32(4); dcol = ar.f32(4); bglu_c = ar.f32(4)
    mh16 = ar.f32(16); rho8 = ar.f32(16); bd32 = ar.f32(128)
    hl_r = ar.f32(16); hl_i = ar.f32(16)
    NWIN = 1408
    winb = ar.bf(8 * NWIN, [8, NWIN])
    woutb = ar.bf(8 * D, [8, D])
    KT = ar.bf(4 * 8 * 128, [4, 8, 128])
    BL = ar.bf(4 * 8 * 2 * 128, [4, 8, 2, 128])
    CL = ar.bf(8 * 2 * 512, [8, 2, 512])
    glub = ar.bf(4 * 128, [4, 128])
    cs = ar.f32(2048, [16, 128]); sn = ar.f32(2048, [16, 128])
    kdupT = ar.bf(2 * 640, [2, 640])
    vtok = ar.bf(5 * 128, [5, 128])
    PERS_TOP = ar.top

    B = {}

    def bb(name):
        if name not in B:
            B[name] = Buf(name)
        return B[name]

    consts = bb("consts")
    Bwin = bb("winb"); Bwout = bb("woutb")
    setup_slot = K.slot("setup")
    wslot = K.slot("wres")

    def pool(fn, R=(), W=()):
        return K.op(POOL, fn, R, W)

    def dve(fn, R=(), W=()):
        return K.op(DVE, fn, R, W)

    def act(fn, R=(), W=()):
        return K.op(ACT, fn, R, W)

    def pe(fns, R=(), W=()):
        return K.op(PE, fns, R, W)

    C1 = [consts]
    pool(lambda: G.memset(ident, 0.0), W=C1)
    pool(lambda: G.affine_select(out=ident, in_=ident, pattern=[[-1, 128]], compare_op=ALU.not_equal,
                                 fill=1.0, base=0, channel_multiplier=1), R=C1, W=C1)
    pool(lambda: G.memset(ones_f, 1.0), W=C1)
    pool(lambda: G.memset(mh16, -0.5), W=C1)
    SET0 = ar.top
    mtmp = ar.f32(128 * 6, [6, 128])
    Bm = bb("mtmp")
    pool(lambda: G.memset(mtmp, 1.0), W=[Bm])
    pool(lambda: G.affine_select(out=mtmp[:, 0, :], in_=mtmp[:, 0, :], pattern=[[1, 128]], compare_op=ALU.is_ge,
                                 fill=0.0, base=0, channel_multiplier=-1), R=[Bm], W=[Bm])
    pool(lambda: G.affine_select(out=mtmp[:, 1, :], in_=mtmp[:, 1, :], pattern=[[-1, 128]], compare_op=ALU.is_ge,
                                 fill=0.0, base=0, channel_multiplier=1), R=[Bm], W=[Bm])
    pool(lambda: G.affine_select(out=mtmp[:, 2, :], in_=mtmp[:, 2, :], pattern=[[1, 128]], compare_op=ALU.is_ge,
                                 fill=0.0, base=0, channel_multiplier=-1), R=[Bm], W=[Bm])
    pool(lambda: G.affine_select(out=mtmp[:, 2, :], in_=mtmp[:, 2, :], pattern=[[0, 128]], compare_op=ALU.is_ge,
                                 fill=0.0, base=-112, channel_multiplier=1), R=[Bm], W=[Bm])
    v3 = mtmp[:, 3, :].rearrange("p (s i) -> p s i", s=16)
    pool(lambda: G.affine_select(out=v3, in_=v3, pattern=[[8, 16], [1, 8]], compare_op=ALU.is_ge,
                                 fill=0.0, base=0, channel_multiplier=-1), R=[Bm], W=[Bm])
    pool(lambda: G.affine_select(out=v3, in_=v3, pattern=[[-8, 16], [0, 8]], compare_op=ALU.is_ge,
                                 fill=0.0, base=0, channel_multiplier=1), R=[Bm], W=[Bm])
    v4 = mtmp[:, 4, :].rearrange("p (s i) -> p s i", s=16)
    pool(lambda: G.affine_select(out=v4, in_=v4, pattern=[[0, 16], [-1, 8]], compare_op=ALU.is_ge,
                                 fill=0.0, base=0, channel_multiplier=1), R=[Bm], W=[Bm])
    pool(lambda: G.affine_select(out=mtmp[:, 5, :], in_=mtmp[:, 5, :], pattern=[[-1, 128]], compare_op=ALU.is_ge,
                                 fill=0.0, base=0, channel_multiplier=1), R=[Bm], W=[Bm])
    pool(lambda: G.affine_select(out=mtmp[:, 5, :], in_=mtmp[:, 5, :], pattern=[[0, 128]], compare_op=ALU.is_ge,
                                 fill=0.0, base=-112, channel_multiplier=1), R=[Bm], W=[Bm])
    Bbd = bb("bd32")
    pool(lambda: G.memset(bd32, 1.0), W=[Bbd])
    bdv = bd32.rearrange("p (a b) -> p a b", a=4)
    pool(lambda: G.affine_select(out=bdv, in_=bdv, pattern=[[32, 4], [0, 32]], compare_op=ALU.is_ge,
                                 fill=0.0, base=31, channel_multiplier=-1), R=[Bbd], W=[Bbd])
    pool(lambda: G.affine_select(out=bdv, in_=bdv, pattern=[[-32, 4], [0, 32]], compare_op=ALU.is_ge,
                                 fill=0.0, base=0, channel_multiplier=1), R=[Bbd], W=[Bbd])
    bd16 = ar.f32(128)
    Bbd16 = bb("bd16")
    pool(lambda: G.memset(bd16, 1.0), W=[Bbd16])
    bdv16 = bd16.rearrange("p (a b) -> p a b", a=8)
    pool(lambda: G.affine_select(out=bdv16, in_=bdv16, pattern=[[16, 8], [0, 16]], compare_op=ALU.is_ge,
                                 fill=0.0, base=15, channel_multiplier=-1), R=[Bbd16], W=[Bbd16])
    pool(lambda: G.affine_select(out=bdv16, in_=bdv16, pattern=[[-16, 8], [0, 16]], compare_op=ALU.is_ge,
                                 fill=0.0, base=0, channel_multiplier=1), R=[Bbd16], W=[Bbd16])
    for i, m in enumerate([mcur, mprev, m0cur, msamp, mcache, m1prev]):
        dve(lambda i=i, m=m: V.tensor_copy(out=m, in_=mtmp[:, i, :]), R=[Bm], W=C1)
    dve(lambda: V.memset(mzero, 0.0), W=C1)
    dve(lambda: V.memset(blk64, 0.0), W=C1)
    dve(lambda: V.memset(blk64[0:64, 0:64], 1.0 / 64), W=C1)
    dve(lambda: V.memset(blk64[64:128, 64:128], 1.0 / 64), W=C1)
    dve(lambda: V.memset(onesA, 1.0 / 512), W=C1)
    dve(lambda: V.memset(onesF, 1.0 / 1024), W=C1)
    dve(lambda: V.memset(ones_b, 1.0), W=C1)
    Bkd = bb("kdupT"); Bvt = bb("vtok")
    dve(lambda: V.memset(kdupT, 0.0), W=[Bkd])
    dve(lambda: V.memset(vtok, 0.0), W=[Bvt])

    def wload(dst, src):
        K.dma(POOL, lambda: G.dma_start(out=dst, in_=src), wslot, W=[Bwin, Bwout])

    sgate = nc.dram_tensor("scr_gate", [NF, 128, 8 * 128], BF16, kind="Internal").ap()
    sup = nc.dram_tensor("scr_up", [NF, 128, 8 * 128], BF16, kind="Internal").ap()
    sdown = nc.dram_tensor("scr_down", [16, 128, 11 * 128], BF16, kind="Internal").ap()
    Bscr = bb("scratchW")
    cvslot = K.slot("conv")
    conv_jobs = []
    wg_v = w_gate.rearrange("(k p) c -> p k c", p=128)
    wu_v = w_up.rearrange("(k p) c -> p k c", p=128)
    wd_v = w_down.rearrange("(f p) c -> p f c", p=128)
    for f_ in range(NF):
        conv_jobs.append((sgate[f_].rearrange("p (k c) -> p k c", k=8), wg_v[:, :, f_ * 128:(f_ + 1) * 128]))
        conv_jobs.append((sup[f_].rearrange("p (k c) -> p k c", k=8), wu_v[:, :, f_ * 128:(f_ + 1) * 128]))
        if f_ == 10 or f_ == 21:
            hf_ = 0 if f_ == 10 else 1
            for m_ in range(8):
                conv_jobs.append((sdown[hf_ * 8 + m_].rearrange("p (f c) -> p f c", f=11),
                                  wd_v[:, 11 * hf_:11 * hf_ + 11, m_ * 128:(m_ + 1) * 128]))
    conv_state = {"i": 0}

    def conv_issue(n):
        for _ in range(n):
            i = conv_state["i"]
            if i >= len(conv_jobs):
                return
            if i >= 3:
                need = 16 * (i - 2)
                if POOL.waited.get(id(cvslot), 0) < need:
                    G.wait_ge(cvslot.sem, need)
                    POOL.waited[id(cvslot)] = need
            dst, src = conv_jobs[i]
            K.dma(POOL, lambda dst=dst, src=src: G.dma_start(out=dst, in_=src), cvslot, W=[Bscr])
            conv_state["i"] = i + 1

    Bpar = bb("params")
    sload_list = []

    def sload(dst, src):
        K.dma(SP, lambda: S.dma_start(out=dst, in_=src), setup_slot, W=[Bpar])

    BparA = bb("paramsA")
    setupA_slot = K.slot("setupA")

    def sloadA(dst, src):
        K.dma(SP, lambda: S.dma_start(out=dst, in_=src), setupA_slot, W=[BparA])

    are = ar.f32(16); aim = ar.f32(16); ldt = ar.f32(16)
    sloadA(are, a_re.rearrange("(k p) -> p k", p=128))
    sloadA(aim, a_im.rearrange("(k p) -> p k", p=128))
    ldt2 = log_dt.rearrange("(P two) -> two P", two=2)
    sloadA(ldt[0:64, :], ldt2[0, :].partition_broadcast(64))
    sloadA(ldt[64:128, :], ldt2[1, :].partition_broadcast(64))
    Bre = ar.f32(256, [16, 16]); Bim = ar.f32(256, [16, 16])
    sloadA(Bre, b_re.rearrange("(P q h) -> q P h", P=16, q=128))
    sloadA(Bim, b_im.rearrange("(P q h) -> q P h", P=16, q=128))
    BparA.w = (setupA_slot, setupA_slot.n)
    sload(gmix_c, g_mix.rearrange("(k p) -> p k", p=128))
    sload(gffn_c, g_ffn.rearrange("(k p) -> p k", p=128))
    sload(gatt_c, g_att.rearrange("(k p) -> p k", p=128))
    sload(gssm_c, g_ssm.rearrange("(k p) -> p k", p=128))
    sload(dcol, ssm_d.rearrange("(k p) -> p k", p=128))
    sload(bglu_c, b_glu.rearrange("(k p) -> p k", p=128))
    gq2 = g_q.rearrange("(p o) -> p o", o=1)
    gk2 = g_k.rearrange("(p o) -> p o", o=1)
    sload(gq_c[0:64, :], gq2); sload(gq_c[64:128, :], gq2)
    sload(gk_c[0:64, :], gk2); sload(gk_c[64:128, :], gk2)
    sk2 = sinks.rearrange("(t two) -> two t", two=2)
    sload(esk[0:64, :], sk2[0:1, :].partition_broadcast(64) if False else sinks.rearrange("(t two) -> two t", two=2)[0, :].partition_broadcast(64))
    sload(esk[64:128, :], sinks.rearrange("(t two) -> two t", two=2)[1, :].partition_broadcast(64))
    Cin = ar.f32(4 * 2 * 128, [4, 2, 128])
    Cld = ar.f32(4 * 2 * 64, [4, 2, 64])
    BCin = bb("Cin"); BCld = bb("Cld")
    for ri, csrc in enumerate([c_re, c_im]):
        K.dma(SP, lambda ri=ri, csrc=csrc: S.dma_start(out=Cld[:, :, ri, :], in_=csrc.rearrange("(o r) p -> r o p", o=4)),
              setup_slot, W=[BCld])
    gluf = ar.f32(4 * 128, [4, 128])
    gld = ar.f32(4 * 16, [4, 16])
    Bgl = bb("gluf"); Bgld = bb("gld")
    K.dma(SP, lambda: S.dma_start(out=gld, in_=w_glu.rearrange("(o r) k -> r o k", o=4)), setup_slot, W=[Bgld])
    fin = (setup_slot, setup_slot.n)
    for b in (Bpar, BCld, Bgld):
        b.w = fin
    G.wait_ge(setupA_slot.sem, setupA_slot.n)
    POOL.waited[id(setupA_slot)] = setupA_slot.n
    w_in_v = w_in.rearrange("(k p) c -> p k c", p=128)
    wload(winb[:, :, 896:1408], w_in_v[:, :, 768:1280])
    wload(winb[:, :, 0:512], w_in_v[:, :, 0:512])
    wload(winb[:, :, 512:576], w_in_v[:, :, 512:576])
    wload(winb[:, :, 576:640], w_in_v[:, :, 512:576])
    wload(winb[:, :, 640:704], w_in_v[:, :, 576:640])
    wload(winb[:, :, 704:768], w_in_v[:, :, 576:640])
    wload(winb[:, :, 768:896], w_in_v[:, :, 640:768])
    w_out_v = w_out.rearrange("(k p) c -> p k c", p=128)
    for k in range(0, 8, 2):
        wload(woutb[:, k:k + 2, :], w_out_v[:, k:k + 2, :])

    conv_issue(len(conv_jobs))
    P1 = [Bpar]

    def t16():
        return ar.f32(16)

    Bs = bb("ssmtmp")
    RS = [BparA, Bs]
    WS = [Bs]
    act(lambda: A.activation(out=esk, in_=esk, func=AF.Exp), R=P1, W=[Bpar])
    dt_ = t16(); adt = t16(); mag = t16(); th = t16()
    TWO_PI = 2.0 * math.pi

    def sin_of(dst, src, shift):
        u = t16(); ki = ar.f32(16).bitcast(mybir.dt.int32); kf = t16(); r = t16(); m1 = t16(); m2 = t16(); x2 = t16(); qq = t16()
        dve(lambda: V.tensor_scalar(out=u, in0=src, scalar1=shift, scalar2=1.0 / TWO_PI, op0=ALU.add, op1=ALU.mult), R=RS, W=WS)
        dve(lambda: V.tensor_copy(out=ki, in_=u), R=RS, W=WS)
        dve(lambda: V.tensor_copy(out=kf, in_=ki), R=RS, W=WS)
        dve(lambda: V.tensor_tensor(out=r, in0=u, in1=kf, op=ALU.subtract), R=RS, W=WS)
        for (thr, op_, sgn) in ((0.5, ALU.is_gt, -1.0), (-0.5, ALU.is_lt, 1.0)):
            dve(lambda thr=thr, op_=op_: V.tensor_scalar(out=m1, in0=r, scalar1=thr, scalar2=None, op0=op_), R=RS, W=WS)
            dve(lambda sgn=sgn: V.scalar_tensor_tensor(out=r, in0=m1, scalar=sgn, in1=r, op0=ALU.mult, op1=ALU.add), R=RS, W=WS)
        for (thr, op_, c0) in ((0.25, ALU.is_gt, 0.5), (-0.25, ALU.is_lt, -0.5)):
            dve(lambda thr=thr, op_=op_: V.tensor_scalar(out=m1, in0=r, scalar1=thr, scalar2=None, op0=op_), R=RS, W=WS)
            dve(lambda c0=c0: V.tensor_scalar(out=m2, in0=r, scalar1=-2.0, scalar2=c0, op0=ALU.mult, op1=ALU.add), R=RS, W=WS)
            dve(lambda: V.tensor_tensor(out=m2, in0=m2, in1=m1, op=ALU.mult), R=RS, W=WS)
            dve(lambda: V.tensor_tensor(out=r, in0=r, in1=m2, op=ALU.add), R=RS, W=WS)
        dve(lambda: V.tensor_scalar(out=r, in0=r, scalar1=TWO_PI, scalar2=None, op0=ALU.mult), R=RS, W=WS)
        dve(lambda: V.tensor_tensor(out=x2, in0=r, in1=r, op=ALU.mult), R=RS, W=WS)
        cf = [-1.0 / 6, 1.0 / 120, -1.0 / 5040, 1.0 / 362880, -1.0 / 39916800, 1.0 / 6227020800]
        dve(lambda: V.tensor_scalar(out=qq, in0=x2, scalar1=cf[5], scalar2=None, op0=ALU.mult), R=RS, W=WS)
        for c_ in (cf[4], cf[3], cf[2], cf[1], cf[0]):
            dve(lambda c_=c_: V.scalar_tensor_tensor(out=qq, in0=qq, scalar=c_, in1=x2, op0=ALU.add, op1=ALU.mult), R=RS, W=WS)
        dve(lambda: V.scalar_tensor_tensor(out=dst, in0=qq, scalar=1.0, in1=r, op0=ALU.add, op1=ALU.mult), R=RS, W=WS)

    def exp_of(dst, src, nsq):
        y = t16(); qq = t16()
        dve(lambda: V.tensor_scalar(out=y, in0=src, scalar1=1.0 / (2 ** nsq), scalar2=None, op0=ALU.mult), R=RS, W=WS)
        dve(lambda: V.tensor_scalar(out=qq, in0=y, scalar1=1.0 / 8, scalar2=1.0, op0=ALU.mult, op1=ALU.add), R=RS, W=WS)
        for k_ in (7, 6, 5, 4, 3, 2, 1):
            dve(lambda k_=k_: V.scalar_tensor_tensor(out=qq, in0=qq, scalar=1.0 / k_, in1=y, op0=ALU.mult, op1=ALU.mult), R=RS, W=WS)
            dve(lambda: V.tensor_scalar(out=qq, in0=qq, scalar1=1.0, scalar2=None, op0=ALU.add), R=RS, W=WS)
        for _ in range(nsq):
            dve(lambda: V.tensor_tensor(out=qq, in0=qq, in1=qq, op=ALU.mult), R=RS, W=WS)
        dve(lambda: V.tensor_copy(out=dst, in_=qq), R=RS, W=WS)

    exp_of(dt_, ldt, 4)
    dve(lambda: V.tensor_tensor(out=adt, in0=are, in1=dt_, op=ALU.mult), R=RS, W=WS)
    dve(lambda: V.tensor_tensor(out=th, in0=aim, in1=dt_, op=ALU.mult), R=RS, W=WS)
    exp_of(mag, adt, 1)
    sth = t16(); cth = t16()
    sin_of(sth, th, 0.0)
    sin_of(cth, th, math.pi / 2)
    Lr = ar.f32(9 * 16, [9, 16]); Li = ar.f32(9 * 16, [9, 16])
    dve(lambda: V.memset(Lr[:, 0, :], 1.0), W=WS)
    dve(lambda: V.memset(Li[:, 0, :], 0.0), W=WS)
    dve(lambda: V.tensor_tensor(out=Lr[:, 1, :], in0=mag, in1=cth, op=ALU.mult), R=RS, W=WS)
    dve(lambda: V.tensor_tensor(out=Li[:, 1, :], in0=mag, in1=sth, op=ALU.mult), R=RS, W=WS)
    ta = t16(); tb = t16()

    def cmul(dr, di, ar_, ai_, br_, bi_, shape_bc=None):
        dve(lambda: V.tensor_tensor(out=ta, in0=ai_, in1=bi_, op=ALU.mult), R=RS, W=WS)
        dve(lambda: V.tensor_tensor(out=tb, in0=ar_, in1=bi_, op=ALU.mult), R=RS, W=WS)
        dve(lambda: V.tensor_tensor(out=dr, in0=ar_, in1=br_, op=ALU.mult), R=RS, W=WS)
        dve(lambda: V.tensor_tensor(out=di, in0=ai_, in1=br_, op=ALU.mult), R=RS, W=WS)
        dve(lambda: V.tensor_tensor(out=dr, in0=dr, in1=ta, op=ALU.subtract), R=RS, W=WS)
        dve(lambda: V.tensor_tensor(out=di, in0=di, in1=tb, op=ALU.add), R=RS, W=WS)

    for n in range(2, 9):
        cmul(Lr[:, n, :], Li[:, n, :], Lr[:, n - 1, :], Li[:, n - 1, :], Lr[:, 1, :], Li[:, 1, :])
    nr = t16(); den = t16(); fr = t16(); fi = t16(); t3 = t16()
    dve(lambda: V.tensor_scalar(out=nr, in0=Lr[:, 1, :], scalar1=-1.0, scalar2=None, op0=ALU.add), R=RS, W=WS)
    dve(lambda: V.tensor_tensor(out=den, in0=are, in1=are, op=ALU.mult), R=RS, W=WS)
    dve(lambda: V.tensor_tensor(out=t3, in0=aim, in1=aim, op=ALU.mult), R=RS, W=WS)
    dve(lambda: V.tensor_tensor(out=den, in0=den, in1=t3, op=ALU.add), R=RS, W=WS)
    dve(lambda: V.reciprocal(out=den, in_=den), R=RS, W=WS)
    ni = Li[:, 1, :]
    dve(lambda: V.tensor_tensor(out=fr, in0=nr, in1=are, op=ALU.mult), R=RS, W=WS)
    dve(lambda: V.tensor_tensor(out=t3, in0=ni, in1=aim, op=ALU.mult), R=RS, W=WS)
    dve(lambda: V.tensor_tensor(out=fr, in0=fr, in1=t3, op=ALU.add), R=RS, W=WS)
    dve(lambda: V.tensor_tensor(out=fr, in0=fr, in1=den, op=ALU.mult), R=RS, W=WS)
    dve(lambda: V.tensor_tensor(out=fi, in0=ni, in1=are, op=ALU.mult), R=RS, W=WS)
    dve(lambda: V.tensor_tensor(out=t3, in0=nr, in1=aim, op=ALU.mult), R=RS, W=WS)
    dve(lambda: V.tensor_tensor(out=fi, in0=fi, in1=t3, op=ALU.subtract), R=RS, W=WS)
    dve(lambda: V.tensor_tensor(out=fi, in0=fi, in1=den, op=ALU.mult), R=RS, W=WS)
    wr = t16(); wi = t16(); w2 = t16()
    dve(lambda: V.tensor_tensor(out=w2, in0=Lr[:, 8, :], in1=Lr[:, 8, :], op=ALU.mult), R=RS, W=WS)
    dve(lambda: V.tensor_tensor(out=t3, in0=Li[:, 8, :], in1=Li[:, 8, :], op=ALU.mult), R=RS, W=WS)
    dve(lambda: V.tensor_tensor(out=w2, in0=w2, in1=t3, op=ALU.add), R=RS, W=WS)
    w2a = t16(); w2y = t16(); w2t = t16()
    dve(lambda: V.tensor_copy(out=w2a, in_=w2), R=RS, W=WS)
    act(lambda: A.activation(out=w2y, in_=w2a, func=AF.Ln), R=RS, W=WS)
    act(lambda: A.activation(out=w2y, in_=w2y, func=AF.Exp, scale=-0.5), R=RS, W=WS)
    dve(lambda: V.tensor_tensor(out=w2t, in0=w2y, in1=w2y, op=ALU.mult), R=RS, W=WS)
    dve(lambda: V.tensor_tensor(out=w2t, in0=w2t, in1=w2a, op=ALU.mult), R=RS, W=WS)
    dve(lambda: V.tensor_scalar(out=w2t, in0=w2t, scalar1=-0.5, scalar2=1.5, op0=ALU.mult, op1=ALU.add), R=RS, W=WS)
    dve(lambda: V.tensor_tensor(out=w2, in0=w2y, in1=w2t, op=ALU.mult), R=RS, W=WS)
    dve(lambda: V.reciprocal(out=rho8, in_=w2), R=RS, W=C1 + [Bs])
    dve(lambda: V.tensor_tensor(out=wr, in0=Lr[:, 8, :], in1=w2, op=ALU.mult), R=RS, W=WS)
    dve(lambda: V.tensor_tensor(out=wi, in0=Li[:, 8, :], in1=w2, op=ALU.mult), R=RS, W=WS)
    Btab = bb("tables")
    WT = [Btab, Bs]
    RT = [Btab, Bs, Bpar]
    dve(lambda: V.tensor_copy(out=cs[:, :, 0], in_=wr), R=RT, W=WT)
    dve(lambda: V.tensor_copy(out=sn[:, :, 0], in_=wi), R=RT, W=WT)
    tq1 = ar.f32(16 * 64, [16, 64]); tq2 = ar.f32(16 * 64, [16, 64])
    n = 1
    while n < 128:
        kr = cs[:, :, n - 1:n].to_broadcast([128, 16, n]); ki_ = sn[:, :, n - 1:n].to_broadcast([128, 16, n])
        sr = cs[:, :, 0:n]; si = sn[:, :, 0:n]; dr = cs[:, :, n:2 * n]; di = sn[:, :, n:2 * n]
        a1 = tq1[:, :, 0:n]; a2 = tq2[:, :, 0:n]
        dve(lambda a1=a1, si=si, ki_=ki_: V.tensor_tensor(out=a1, in0=si, in1=ki_, op=ALU.mult), R=RT, W=WT)
        dve(lambda a2=a2, sr=sr, ki_=ki_: V.tensor_tensor(out=a2, in0=sr, in1=ki_, op=ALU.mult), R=RT, W=WT)
        dve(lambda dr=dr, sr=sr, kr=kr: V.tensor_tensor(out=dr, in0=sr, in1=kr, op=ALU.mult), R=RT, W=WT)
        dve(lambda di=di, si=si, kr=kr: V.tensor_tensor(out=di, in0=si, in1=kr, op=ALU.mult), R=RT, W=WT)
        dve(lambda dr=dr, a1=a1: V.tensor_tensor(out=dr, in0=dr, in1=a1, op=ALU.subtract), R=RT, W=WT)
        dve(lambda di=di, a2=a2: V.tensor_tensor(out=di, in0=di, in1=a2, op=ALU.add), R=RT, W=WT)
        n *= 2
    bbr = ar.f32(256, [16, 16]); bbi = ar.f32(256, [16, 16]); tB1 = ar.f32(256, [16, 16]); tB2 = ar.f32(256, [16, 16])

    def bc16(x):
        return x.unsqueeze(2).to_broadcast([128, 16, 16])

    def cmulB(dr, di, sr, si, xr_, xi_, dr_halves=None):
        dve(lambda: V.tensor_tensor(out=tB1, in0=si, in1=bc16(xi_), op=ALU.mult), R=RS, W=WS)
        dve(lambda: V.tensor_tensor(out=tB2, in0=sr, in1=bc16(xi_), op=ALU.mult), R=RS, W=WS)
        dve(lambda: V.tensor_tensor(out=dr, in0=sr, in1=bc16(xr_), op=ALU.mult), R=RS, W=WS)
        dve(lambda: V.tensor_tensor(out=di, in0=si, in1=bc16(xr_), op=ALU.mult), R=RS, W=WS)
        dve(lambda: V.tensor_tensor(out=dr, in0=dr, in1=tB1, op=ALU.subtract), R=RS, W=WS)
        dve(lambda: V.tensor_tensor(out=di, in0=di, in1=tB2, op=ALU.add), R=RS, W=WS)

    cmulB(bbr, bbi, Bre, Bim, fr, fi)
    Mr = ar.f32(512, [16, 2, 16]); Mi = ar.f32(512, [16, 2, 16])
    MB0r = ar.f32(512, [16, 2, 16]); MB0i = ar.f32(512, [16, 2, 16])
    Wr_ = ar.f32(256, [16, 16]); Wi_ = ar.f32(256, [16, 16])
    BM = bb("Mtiles")
    for t_ in (Mr, Mi, MB0r, MB0i):
        dve(lambda t_=t_: V.memset(t_, 0.0), W=[BM])
    BBL = bb("BL")
    pbank = [0]

    def next_bank():
        b = pbank[0]
        pbank[0] = (b + 1) % 8
        return b

    for n in range(8):
        if n == 0:
            srcr, srci = bbr, bbi
        else:
            cmulB(Wr_, Wi_, bbr, bbi, Lr[:, n, :], Li[:, n, :])
            srcr, srci = Wr_, Wi_
        dstr, dsti = (MB0r, MB0i) if n == 0 else (Mr, Mi)
        for (dst, src) in ((dstr, srcr), (dsti, srci)):
            dve(lambda dst=dst, src=src: V.tensor_copy(out=dst[0:64, :, 0, :], in_=src[0:64]), R=RS, W=[BM])
            dve(lambda dst=dst, src=src: V.tensor_copy(out=dst[64:128, :, 1, :], in_=src[64:128]), R=RS, W=[BM])
        for part, msrc in enumerate((dstr, dsti)):
            for o in range(4):
                bk = next_bank()
                mv = msrc[:, 4 * o:4 * o + 4, :, :].rearrange("p a b c -> p (a b c)")
                pe(lambda bk=bk, mv=mv: T.matmul(ps[:, bk, 0:128], lhsT=mv, rhs=ident, is_transpose=True, start=True, stop=True), R=[BM, consts], W=[pb[bk]])
                act(lambda bk=bk, o=o, n=n, part=part: A.copy(out=BL[:, o, 7 - n, part, :], in_=ps[:, bk, 0:128]),
                    R=[pb[bk]], W=[BBL])
    glhot = ar.f32(2)
    dve(lambda: V.tensor_reduce(out=glhot, in_=bd16.rearrange("p (pl gl k) -> p gl pl k", pl=4, gl=2), axis=mybir.AxisListType.XY, op=ALU.add),
        R=[Bbd16], W=[BCin])
    dve(lambda: V.tensor_scalar(out=glhot, in0=glhot, scalar1=1.0 / 16, scalar2=None, op0=ALU.mult), R=[BCin], W=[BCin])
    for o in range(4):
        for ri in range(2):
            for gl in range(2):
                dve(lambda o=o, ri=ri, gl=gl: V.tensor_scalar(out=Cin[:, o, ri, gl * 64:(gl + 1) * 64], in0=Cld[:, o, ri, :],
                                                              scalar1=glhot[:, gl:gl + 1], scalar2=None, op0=ALU.mult), R=[BCld, BCin], W=[BCin])
        dve(lambda o=o: V.tensor_tensor(out=gluf[:, o, :].rearrange("p (g k) -> p g k", g=8),
                                        in0=gld[:, o, :].unsqueeze(1).to_broadcast([128, 8, 16]),
                                        in1=bd16.rearrange("p (g k) -> p g k", g=8), op=ALU.mult), R=[Bgld, Bbd16], W=[Bgl])

    BCT = bb("CT")
    CTB = [pb[0], pb[1]]
    for ri in range(2):
        pe([lambda o=o, ri=ri: T.matmul(ps[:, ri, o * 128:(o + 1) * 128], lhsT=Cin[:, o, ri, :], rhs=ident, is_transpose=True, start=True, stop=True)
            for o in range(4)], R=[BCin, consts], W=[pb[ri]])
    CTr = ps[:, 0, :].rearrange("p (a b) -> p a b", a=16)
    CTi = ps[:, 1, :].rearrange("p (a b) -> p a b", a=16)
    CLf = ar.f32(9 * 2 * 512, [9, 2, 512])
    BCLf = bb("CLf"); BCL = bb("CL")
    tC1 = ps[:, 2, :].rearrange("p (a b) -> p a b", a=16)
    tC2 = ar.f32(512, [16, 32])
    T1 = [pb[2]]

    def bc32(x):
        return x.unsqueeze(2).to_broadcast([128, 16, 32])

    def v512(x):
        return x.rearrange("p (a b) -> p a b", a=16)

    dve(lambda: V.tensor_copy(out=v512(CLf[:, 0, 0, :]), in_=CTr), R=CTB, W=[BCLf])
    dve(lambda: V.tensor_scalar(out=v512(CLf[:, 0, 1, :]), in0=CTi, scalar1=-1.0, scalar2=None, op0=ALU.mult), R=CTB, W=[BCLf])
    for n in range(1, 9):
        lr_, li_ = Lr[:, n, :], Li[:, n, :]
        o_r = v512(CLf[:, n, 0, :]); o_i = v512(CLf[:, n, 1, :])
        dve(lambda lr_=lr_: V.tensor_tensor(out=tC1, in0=CTr, in1=bc32(lr_), op=ALU.mult), R=RS + CTB, W=T1)
        dve(lambda li_=li_: V.tensor_tensor(out=tC2, in0=CTi, in1=bc32(li_), op=ALU.mult), R=RS + CTB, W=WS)
        dve(lambda o_r=o_r: V.tensor_tensor(out=o_r, in0=tC1, in1=tC2, op=ALU.subtract), R=RS + T1, W=[BCLf, Bs])
        dve(lambda li_=li_: V.tensor_tensor(out=tC1, in0=CTr, in1=bc32(li_), op=ALU.mult), R=RS + CTB, W=T1)
        dve(lambda lr_=lr_: V.tensor_tensor(out=tC2, in0=CTi, in1=bc32(lr_), op=ALU.mult), R=RS + CTB, W=WS)
        dve(lambda o_i=o_i: V.scalar_tensor_tensor(out=o_i, in0=tC1, scalar=-1.0, in1=tC2, op0=ALU.mult, op1=ALU.subtract),
            R=RS + T1, W=[BCLf, Bs])
        act(lambda n=n: A.copy(out=CL[:, n - 1, :, :], in_=CLf[:, n, :, :]), R=[BCLf], W=[BCL])
    BKT = bb("KT")
    ktmp = ar.f32(128)
    Bkt = bb("ktmp")
    for tau in range(8):
        for o in range(4):
            bk = next_bank()
            l_r = MB0r[:, 4 * o:4 * o + 4, :, :].rearrange("p a b c -> p (a b c)")
            l_i = MB0i[:, 4 * o:4 * o + 4, :, :].rearrange("p a b c -> p (a b c)")
            r_r = CLf[:, tau, 0, 128 * o:128 * o + 128]
            r_i = CLf[:, tau, 1, 128 * o:128 * o + 128]
            pe([lambda bk=bk, l_r=l_r, r_r=r_r: T.matmul(ps[:, bk, 0:128], lhsT=l_r, rhs=r_r, start=True, stop=False),
                lambda bk=bk, l_i=l_i, r_i=r_i: T.matmul(ps[:, bk, 0:128], lhsT=l_i, rhs=r_i, start=False, stop=True)],
               R=[BM, BCLf], W=[pb[bk]])
            if tau == 0:
                dve(lambda bk=bk: V.tensor_tensor(out=ktmp, in0=ps[:, bk, 0:128], in1=bd32, op=ALU.mult), R=[pb[bk], Bbd], W=[Bkt])
                dve(lambda o=o: V.scalar_tensor_tensor(out=KT[:, o, 0, :], in0=ident, scalar=dcol[:, o:o + 1], in1=ktmp,
                                                       op0=ALU.mult, op1=ALU.add), R=[Bkt, consts, Bpar], W=[BKT])
            else:
                dve(lambda bk=bk, o=o, tau=tau: V.tensor_tensor(out=KT[:, o, tau, :], in0=ps[:, bk, 0:128], in1=bd32, op=ALU.mult),
                    R=[pb[bk], Bbd], W=[BKT])
    Bglu = bb("glub")
    dve(lambda: V.tensor_copy(out=glub, in_=gluf), R=[Bgl], W=[Bglu])
    for k in range(8):
        act(lambda k=k: A.activation(out=winb[:, k, :], in_=winb[:, k, :], func=AF.Copy, scale=gmix_c[:, k:k + 1]),
            R=[Bwin, Bpar], W=[Bwin])
    Bhl = bb("hlast")
    dve(lambda: V.memset(hl_r, 0.0), W=[Bhl])
    dve(lambda: V.memset(hl_i, 0.0), W=[Bhl])
    K.barrier()
    dump("cs", cs, [Btab]); dump("sn", sn, [Btab]); dump("rho8", rho8, [consts]); dump("esk", esk, [Bpar])
    dump("KT", KT, [BKT]); dump("BL", BL, [BBL]); dump("CL", CL, [BCL]); dump("glub", glub, [Bglu])
    dump("Lr", Lr, [Bs]); dump("Li", Li, [Bs]); dump("fr", fr, [Bs]); dump("fi", fi, [Bs])
    dump("mcur", mcur, [consts]); dump("mprev", mprev, [consts]); dump("m0cur", m0cur, [consts]); dump("msamp", msamp, [consts])
    dump("mcache", mcache, [consts]); dump("m1prev", m1prev, [consts]); dump("bd32", bd32, [Bbd]); dump("ident", ident, [consts])
    dump("winb", winb, [Bwin])
    SSMW = [BKT, BBL, BCL, Bglu, Btab, consts, Bpar]

    ar.top = PERS_TOP
    xst = [ar.f32(1024), ar.f32(1024)]
    junk = ar.bf(1024)
    aT = ar.bf(11 * 512, [11, 512])
    R0_END = ar.top
    xnT = ar.bf(8 * 512, [8, 512])
    ossm = ar.bf(4 * 1024, [4, 1024])
    rbc = ar.f32(512)
    diag = ar.f32(128)
    ssq = ar.f32(4); rst = ar.f32(4)
    PH0 = ar.top
    useg = ar.bf(4 * 8 * 128, [4, 8, 128])
    rt1 = ar.f32(1024); rt2 = ar.f32(1024)
    zr = ar.f32(1024); zi = ar.f32(1024); gr = ar.f32(1024); gi = ar.f32(1024)
    hb_r = ar.f32(8 * 129, [8, 129]); hb_i = ar.f32(8 * 129, [8, 129])
    hprev = ar.bf(16 * 2 * 128, [16, 2, 128])
    Gt = ar.bf(1024); sg_ = ar.bf(1024); Gt2 = ar.bf(1024); sg2 = ar.bf(1024)
    h0r = ar.f32(256, [16, 16]); h0i = ar.f32(256, [16, 16])
    hsr = ar.f32(256, [16, 16]); hsi = ar.f32(256, [16, 16])
    hs_st = ar.f32(256)
    A_TOP = ar.top
    ar.top = PH0
    xT = ar.f32(8 * 512, [8, 512])
    qn = ar.bf(4 * 512, [4, 512])
    sqt = ar.bf(512)
    rq = ar.f32(512)
    et = ar.bf(16 * 128, [16, 128]); pt = ar.bf(16 * 128, [16, 128])
    rden = ar.f32(512, [4, 128])
    oatt = ar.f32(4 * 512, [4, 512])
    NGU = 3
    NDN = 3
    WG_OFF = ar.top
    wgu = [ar.bf(2 * 8 * 128, [2, 8, 128]) for _ in range(NGU)]
    wdn = [ar.bf(11 * 128, [11, 128]) for _ in range(NDN)]
    sgb = ar.bf(512)
    kvo = ar.f32(256, [2, 128])
    kTf = ar.f32(256, [2, 128])
    B_TOP = ar.top
    sq8 = aT[:, 0:8, :]
    _aTtail = aT[:, 8:11, :].rearrange("p a b -> p (a b)")
    sqtB = _aTtail[:, 0:512]
    rqB = _aTtail[:, 512:1536].bitcast(F32)
    cbase = WG_OFF
    ckst = ar.raw_f32(cbase, 2048).rearrange("p (a b c) -> p a b c", a=8, b=2)
    KcT = ar.raw_bf(cbase + 2048, 4096).rearrange("p (a b c) -> p a b c", a=16, b=2)
    Vc = ar.raw_bf(cbase + 4096, 2048).rearrange("p (a b) -> p a b", a=16)
    assert NGU * 1024 + NDN * 704 >= 5120
    assert max(A_TOP, B_TOP) < ARENA_WORDS - 1, (A_TOP, B_TOP)

    Bx = [bb("xst0"), bb("xst1")]
    Bjunk = bb("junk"); BaT = bb("aT"); BxnTk = [bb("xnT%d" % i) for i in range(8)]; Bossm = bb("ossm"); Brbc = bb("rbc"); Bdiag = bb("diag")
    Bssq = bb("ssq"); Brst = bb("rst")
    Buseg = bb("useg"); Brt = bb("rt"); Bz = bb("z"); Bg = bb("g"); Bhb = bb("hb"); Bhp = bb("hprev")
    BGt = bb("Gt"); Bsg = bb("sg"); BGt2 = bb("Gt2"); Bsg2 = bb("sg2"); Bh0 = bb("h0"); Bhs = bb("hs")
    BxT = bb("xT"); Bqn = bb("qn"); Bsqt = bb("sqt"); Brq = bb("rq"); Bet = [bb("et0"), bb("et1")]; Bpt = [bb("pt0"), bb("pt1")]
    Brden = bb("rden"); Boatt = bb("oatt"); Bwgu = [bb("wgu%d" % i) for i in range(NGU)]; Bwdn = [bb("wdn%d" % i) for i in range(NDN)]
    Bsgb = bb("sgb"); Bkvo = bb("kvo"); BkTf = bb("kTf"); BsqtB = bb("sqtB"); BrqB = bb("rqB"); Bsq8 = [bb("sq8a"), bb("sq8b")]
    CACHE_B = Bwgu + Bwdn
    xslot = [K.slot("x0"), K.slot("x1")]
    yslot = [K.slot("y0"), K.slot("y1")]
    Bost = [bb("ost0"), bb("ost1")]
    wgslot = [K.slot("wg%d" % i) for i in range(NGU)]
    wdslot = [K.slot("wd%d" % i) for i in range(NDN)]
    oslot = K.slot("out")
    cslot = K.slot("cache")
    stslot = K.slot("stout")
    out_slots.append(oslot)

    def tile_src(gt):
        if gt == 33:
            return xs[:, :]
        return xp[(gt - 1) * 128:gt * 128, :]

    def load_x(gt, par):
        if gt == 0:
            dve(lambda: V.memset(xst[par], 0.0), W=[Bx[par]])
            K.dma(SP, lambda: S.dma_start(out=xst[par][112:128, :], in_=meta[:, :]), xslot[par], W=[Bx[par]])
        else:
            K.dma(SP, lambda: S.dma_start(out=xst[par], in_=tile_src(gt)), xslot[par], W=[Bx[par]])

    XB = [(5, 6, 7), (3, 4, 2)]

    def x_stage1(gt, par, tl):
        ba, bb_, br = XB[tl % 2]
        act(lambda: A.activation(out=junk, in_=xst[par], func=AF.Square, accum_out=ssq[:, tl:tl + 1]), R=[Bx[par]], W=[Bjunk, Bssq])
        act(lambda: A.activation(out=rst[:, tl:tl + 1], in_=ssq[:, tl:tl + 1], func=AF.Ln, scale=1.0 / D, bias=EPS_c[:, 0:1]), R=[Bssq, consts], W=[Brst])
        act(lambda: A.activation(out=rst[:, tl:tl + 1], in_=rst[:, tl:tl + 1], func=AF.Exp, scale=-0.5), R=[Brst], W=[Brst])
        dve(lambda: V.tensor_scalar(out=diag, in0=ident, scalar1=rst[:, tl:tl + 1], scalar2=None, op0=ALU.mult), R=[Brst, consts], W=[Bdiag])
        for half, bk2 in enumerate((ba, bb_)):
            pe([lambda k=k, bk2=bk2: T.matmul(ps[:, bk2, (k % 4) * 128:(k % 4 + 1) * 128], lhsT=xst[par][:, k * 128:(k + 1) * 128], rhs=ident,
                                              is_transpose=True, start=True, stop=True)
                for k in range(4 * half, 4 * half + 4)], R=[Bx[par], consts], W=[pb[bk2]])
        pe(lambda: T.matmul(ps[:, br, 0:128], lhsT=ones_f, rhs=diag, start=True, stop=True), R=[Bdiag, consts], W=[pb[br]])

    def x_stage2(gt, par, tl, want_xT):
        ba, bb_, br = XB[tl % 2]
        act(lambda: A.copy(out=rbc[:, tl * 128:(tl + 1) * 128], in_=ps[:, br, 0:128]), R=[pb[br]], W=[Brbc])
        for half, bk2 in enumerate((ba, bb_)):
            if want_xT:
                dve(lambda half=half, bk2=bk2: V.tensor_copy(out=xT[:, 4 * half:4 * half + 4, tl * 128:(tl + 1) * 128],
                                                             in_=ps[:, bk2, :].rearrange("p (a b) -> p a b", a=4)), R=[pb[bk2]], W=[BxT])
            dve(lambda half=half, bk2=bk2: V.tensor_tensor(
                out=xnT[:, 4 * half:4 * half + 4, tl * 128:(tl + 1) * 128], in0=ps[:, bk2, :].rearrange("p (a b) -> p a b", a=4),
                in1=rbc[:, tl * 128:(tl + 1) * 128].unsqueeze(1).to_broadcast([128, 4, 128]), op=ALU.mult),
                R=[pb[bk2], Brbc], W=BxnTk[4 * half:4 * half + 4])

    XSEQ = []
    for s_i in range(5):
        seg_t = list(range(8 * s_i, 8 * s_i + 8)) if s_i < 4 else [32, 33]
        XSEQ += seg_t
        XSEQ += seg_t
    xq = {"issued": 0, "consumed": 0}

    def _x_fetch():
        while xq["issued"] < min(len(XSEQ), xq["consumed"] + 2):
            i = xq["issued"]
            load_x(XSEQ[i], i % 2)
            xq["issued"] = i + 1

    pend = {"barrier": False}

    def barrier_if_pending():
        if pend["barrier"]:
            K.barrier()
            pend["barrier"] = False

    def run_x_tiles(tiles, want_xT, pre_stage2=None):
        pars = []
        hook = [pre_stage2]

        def st2(j):
            if hook[0] is not None:
                hook[0]()
                hook[0] = None
            x_stage2(tiles[j], pars[j], j, want_xT)

        for i, gt in enumerate(tiles):
            c = xq["consumed"]
            assert XSEQ[c] == gt, (c, XSEQ[c], gt)
            _x_fetch()
            xq["consumed"] = c + 1
            pars.append(c % 2)
            x_stage1(gt, c % 2, i)
            _x_fetch()
            if i >= 1:
                st2(i - 1)
        st2(len(tiles) - 1)

    def phase_A(seg_tiles, is_last, post_on_dve=False):
        ntile = len(seg_tiles)
        Nc = 16 * ntile
        for h0_ in range(0, ntile, 4):
            tl_tiles = seg_tiles[h0_:h0_ + 4]
            TT = 128 * len(tl_tiles)
            run_x_tiles(tl_tiles, False)
            for o in range(4):
                bk = o % 4
                pe([lambda k=k, o=o, bk=bk, TT=TT: T.matmul(ps[:, bk, 0:TT], lhsT=winb[:, k, 896 + o * 128:896 + (o + 1) * 128],
                                                            rhs=xnT[:, k, 0:TT], start=(k == 0), stop=(k == 7)) for k in range(8)],
                   R=BxnTk + [Bwin], W=[pb[bk]])
                c0 = h0_ * 16
                ncl = TT // 8
                barrier_if_pending()
                act(lambda o=o, bk=bk, TT=TT, c0=c0, ncl=ncl: A.copy(out=useg[:, o, :, c0:c0 + ncl],
                                                                   in_=ps[:, bk, 0:TT].rearrange("p (c j) -> p j c", j=8)),
                    R=[pb[bk]], W=[Buseg])
        npr = 16 if is_last else Nc
        if is_last:
            for ri, (src, dst) in enumerate(((sre, h0r), (sim, h0i))):
                for hh in range(2):
                    K.dma(SP, lambda src=src, hh=hh: S.dma_start(out=hs_st[:, 0:128], in_=src[hh * 128:(hh + 1) * 128, :]), cslot, W=[Bhs])
                    bk = 4 + hh
                    pe(lambda bk=bk: T.matmul(ps[:, bk, 0:128], lhsT=hs_st[:, 0:128], rhs=ident, is_transpose=True, start=True, stop=True), R=[Bhs, consts], W=[pb[bk]])
                    act(lambda bk=bk, dst=dst, hh=hh: A.copy(out=dst[:, :, hh * 8:(hh + 1) * 8],
                                                           in_=ps[:, bk, 0:128].rearrange("p (s P) -> p P s", s=8)), R=[pb[bk]], W=[Bh0])
        for hf in range(2):
            for o in (2 * hf, 2 * hf + 1):
                o2 = o % 2
                for part in range(2):
                    sl = part * 2 + o2
                    grp = []
                    for i in range(8):
                        for pl in range(4):
                            grp.append(lambda i=i, o=o, pl=pl, part=part, sl=sl: T.matmul(
                                ps[:, pl, sl * 128:sl * 128 + Nc], lhsT=BL[32 * pl:32 * pl + 32, o, i, part, :],
                                rhs=useg[32 * pl:32 * pl + 32, o, i, 0:Nc], start=(i == 0), stop=(i == 7), tile_position=(32 * pl, 0)))
                    pe(grp, R=[Buseg] + SSMW, W=[pb[0], pb[1], pb[2], pb[3]])
            def sview(part):
                return ps[:, 0:4, :].rearrange("p b (s m) -> p s b m", s=4)[:, part * 2:part * 2 + 2, :, 0:Nc]
            def tabv(tb_, lo, n_):
                return tb_[:, 8 * hf:8 * hf + 8, lo:lo + n_].rearrange("p (a b) m -> p a b m", a=2)
            def v8(x, n_):
                return x[:, 0:8 * n_].rearrange("p (a b m) -> p a b m", a=2, b=4)
            PSB = [pb[0], pb[1], pb[2], pb[3]]
            def rot_pre(n_, lo, col0, bc=False):
                def tv(tb_):
                    if bc:
                        return tb_[:, 8 * hf:8 * hf + 8, lo:lo + 1].rearrange("p (a b) m -> p a b m", a=2).to_broadcast([128, 2, 4, n_])
                    return tabv(tb_, lo, n_)
                Sr = sview(0)[:, :, :, col0:col0 + n_]; Si = sview(1)[:, :, :, col0:col0 + n_]
                a1 = ps[:, 4:6, :].rearrange("p b c -> p (b c)")[:, 0:8 * n_].rearrange("p (a b m) -> p a b m", a=2, b=4)
                a2 = v8(rt2, n_)
                zrv = v8(zr, Nc)[:, :, :, col0:col0 + n_] if False else zr[:, 0:8 * Nc].rearrange("p (a b m) -> p a b m", a=2, b=4)[:, :, :, col0:col0 + n_]
                ziv = zi[:, 0:8 * Nc].rearrange("p (a b m) -> p a b m", a=2, b=4)[:, :, :, col0:col0 + n_]
                PT = [pb[4], pb[5]]
                dve(lambda: V.tensor_tensor(out=a1, in0=Sr, in1=tv(cs), op=ALU.mult), R=PSB + [Btab], W=PT)
                dve(lambda: V.tensor_tensor(out=a2, in0=Si, in1=tv(sn), op=ALU.mult), R=PSB + [Btab], W=[Brt])
                dve(lambda: V.tensor_tensor(out=zrv, in0=a1, in1=a2, op=ALU.add), R=[Brt] + PT, W=[Bz])
                dve(lambda: V.tensor_tensor(out=a1, in0=Si, in1=tv(cs), op=ALU.mult), R=PSB + [Btab], W=PT)
                dve(lambda: V.tensor_tensor(out=a2, in0=Sr, in1=tv(sn), op=ALU.mult), R=PSB + [Btab], W=[Brt])
                dve(lambda: V.tensor_tensor(out=ziv, in0=a1, in1=a2, op=ALU.subtract), R=[Brt] + PT, W=[Bz])
            rot_pre(npr, 0, 0)
            if is_last:
                rot_pre(16, 0, 16, bc=True)
            def zv(x):
                return x[:, 0:8 * Nc].rearrange("p (l m) -> p l m", l=8)
            for lp in range(8):
                P_ = 8 * hf + lp
                for (zsrc, gdst, hl) in ((zr, gr, hl_r), (zi, gi, hl_i)):
                    dve(lambda lp=lp, P_=P_, zsrc=zsrc, gdst=gdst, hl=hl: V.tensor_tensor_scan(
                        out=zv(gdst)[:, lp, 0:npr], data0=rho8[:, P_:P_ + 1].to_broadcast([128, npr]), data1=zv(zsrc)[:, lp, 0:npr],
                        initial=hl[:, P_:P_ + 1], op0=ALU.mult, op1=ALU.add), R=[Bz, Bhl, consts], W=[Bg])
                    if is_last:
                        h0 = h0r if hl is hl_r else h0i
                        dve(lambda lp=lp, P_=P_, zsrc=zsrc, gdst=gdst, h0=h0: V.scalar_tensor_tensor(
                            out=zv(gdst)[:, lp, 16:32], in0=h0[:, P_, :], scalar=rho8[:, P_:P_ + 1], in1=zv(zsrc)[:, lp, 16:32],
                            op0=ALU.mult, op1=ALU.add), R=[Bz, Bh0, consts], W=[Bg])
            def rot_post(n_, lo, col0, bc=False):
                def tv(tb_):
                    if bc:
                        return tb_[:, 8 * hf:8 * hf + 8, lo:lo + 1].to_broadcast([128, 8, n_])
                    return tb_[:, 8 * hf:8 * hf + 8, lo:lo + n_]
                grv = zv(gr)[:, :, col0:col0 + n_]; giv = zv(gi)[:, :, col0:col0 + n_]
                hr_ = hb_r[:, :, 1 + col0:1 + col0 + n_]; hi_ = hb_i[:, :, 1 + col0:1 + col0 + n_]
                b1 = ps[:, 6:8, :].rearrange("p b c -> p (b c)")[:, 0:8 * n_].rearrange("p (l m) -> p l m", l=8)
                b2 = zi[:, 0:8 * n_].rearrange("p (l m) -> p l m", l=8)
                PT2 = [pb[6], pb[7]]
                dve(lambda: V.tensor_tensor(out=b1, in0=grv, in1=tv(cs), op=ALU.mult), R=[Bg, Btab], W=PT2)
                dve(lambda: V.tensor_tensor(out=b2, in0=giv, in1=tv(sn), op=ALU.mult), R=[Bg, Btab], W=[Bz])
                dve(lambda: V.tensor_tensor(out=hr_, in0=b1, in1=b2, op=ALU.subtract), R=[Bz] + PT2, W=[Bhb])
                dve(lambda: V.tensor_tensor(out=b1, in0=giv, in1=tv(cs), op=ALU.mult), R=[Bg, Btab], W=PT2)
                dve(lambda: V.tensor_tensor(out=b2, in0=grv, in1=tv(sn), op=ALU.mult), R=[Bg, Btab], W=[Bz])
                dve(lambda: V.tensor_tensor(out=hi_, in0=b1, in1=b2, op=ALU.add), R=[Bz] + PT2, W=[Bhb])
            dve(lambda: V.tensor_copy(out=hb_r[:, :, 0], in_=hl_r[:, 8 * hf:8 * hf + 8]), R=[Bhl], W=[Bhb])
            dve(lambda: V.tensor_copy(out=hb_i[:, :, 0], in_=hl_i[:, 8 * hf:8 * hf + 8]), R=[Bhl], W=[Bhb])
            rot_post(npr, 0, 0)
            if is_last:
                rot_post(16, 0, 16, bc=True)
            dve(lambda: V.tensor_copy(out=hl_r[:, 8 * hf:8 * hf + 8], in_=hb_r[:, :, npr]), R=[Bhb], W=[Bhl])
            dve(lambda: V.tensor_copy(out=hl_i[:, 8 * hf:8 * hf + 8], in_=hb_i[:, :, npr]), R=[Bhb], W=[Bhl])
            act(lambda: A.copy(out=hprev[:, 8 * hf:8 * hf + 8, 0, 0:npr], in_=hb_r[:, :, 0:npr]), R=[Bhb], W=[Bhp])
            act(lambda: A.copy(out=hprev[:, 8 * hf:8 * hf + 8, 1, 0:npr], in_=hb_i[:, :, 0:npr]), R=[Bhb], W=[Bhp])
            if is_last:
                act(lambda: A.copy(out=hprev[:, 8 * hf:8 * hf + 8, 0, 16:32], in_=h0r[:, 8 * hf:8 * hf + 8, :]), R=[Bh0], W=[Bhp])
                act(lambda: A.copy(out=hprev[:, 8 * hf:8 * hf + 8, 1, 16:32], in_=h0i[:, 8 * hf:8 * hf + 8, :]), R=[Bh0], W=[Bhp])
                dve(lambda: V.tensor_copy(out=hsr[:, 8 * hf:8 * hf + 8, :], in_=hb_r[:, :, 17:33]), R=[Bhb], W=[Bhs])
                dve(lambda: V.tensor_copy(out=hsi[:, 8 * hf:8 * hf + 8, :], in_=hb_i[:, :, 17:33]), R=[Bhb], W=[Bhs])
        for o in range(4):
            yb = 4 if o % 2 == 0 else 0
            gb0 = 6 if o % 2 == 0 else 2
            Gt_o = (Gt, Gt2)[o % 2]; sg_o = (sg_, sg2)[o % 2]; BGt_o = (BGt, BGt2)[o % 2]; Bsg_o = (Bsg, Bsg2)[o % 2]
            banks = [pb[yb], pb[yb + 1]]
            grp = []
            merged = (Nc == 128)
            if merged:
                for tau in range(8):
                    for jb in range(2):
                        jlo = max(tau, 4 * jb)
                        jhi = 4 * jb + 4
                        if jlo >= jhi:
                            continue
                        grp.append(lambda tau=tau, jb=jb, jlo=jlo, jhi=jhi: T.matmul(
                            ps[:, yb + jb, (jlo - 4 * jb) * 128:512], lhsT=KT[:, o, tau, :],
                            rhs=useg[:, o, jlo - tau:jhi - tau, :].rearrange("p a b -> p (a b)"), start=(tau == 0), stop=False))
            for j in range(8):
                bk = yb + j // 4
                sl = j % 4
                dst = ps[:, bk, sl * 128:sl * 128 + Nc]
                for tau in range(j + 1):
                    if merged:
                        break
                    grp.append(lambda dst=dst, tau=tau, j=j: T.matmul(dst, lhsT=KT[:, o, tau, :], rhs=useg[:, o, j - tau, 0:Nc],
                                                                      start=(tau == 0), stop=False))
                for pl in range(4):
                    P_ = 4 * o + pl
                    for part in range(2):
                        last = (pl == 3 and part == 1) and ((not merged) or j in (3, 7))
                        grp.append(lambda bk=bk, sl=sl, pl=pl, P_=P_, part=part, j=j, last=last: T.matmul(
                            ps[32 * pl:32 * pl + 32, bk, sl * 128:sl * 128 + Nc], lhsT=CL[:, j, part, 32 * P_:32 * P_ + 32],
                            rhs=hprev[:, P_, part, 0:Nc], start=False, stop=last, tile_position=(0, 32 * pl)))
            pe(grp, R=[Buseg, Bhp] + SSMW, W=banks)
            yv = ps[:, yb:yb + 2, :].rearrange("p b (s m) -> p (b s) m", s=4)[:, :, 0:Nc]
            Gv = Gt_o[:, 0:8 * Nc].rearrange("p (j m) -> p j m", j=8)
            act(lambda yv=yv, Gv=Gv: A.activation(out=Gv, in_=yv, func=AF.Gelu_apprx_tanh), R=banks, W=[BGt_o])
            tot = 8 * Nc
            pieces = [(0, min(512, tot))] + ([(512, tot)] if tot > 512 else [])
            gb = [pb[gb0], pb[gb0 + 1]]
            for pi, (lo, hi) in enumerate(pieces):
                pe(lambda lo=lo, hi=hi, pi=pi: T.matmul(ps[:, gb0 + pi, 0:hi - lo], lhsT=glub[:, o, :], rhs=Gt_o[:, lo:hi], start=True, stop=True),
                   R=[BGt_o, Bglu], W=[gb[pi]])
                act(lambda lo=lo, hi=hi, pi=pi: A.activation(out=sg_o[:, lo:hi], in_=ps[:, gb0 + pi, 0:hi - lo], func=AF.Sigmoid,
                                                             bias=bglu_c[:, o:o + 1]), R=[gb[pi], Bpar], W=[Bsg_o])
            dve(lambda o=o, Gv=Gv: V.tensor_tensor(out=ossm[:, o, 0:8 * Nc].rearrange("p (m j) -> p j m", j=8), in0=Gv,
                                                   in1=sg_o[:, 0:8 * Nc].rearrange("p (j m) -> p j m", j=8), op=ALU.mult),
                R=[BGt_o, Bsg_o], W=[Bossm])

    NMT = 9
    GU_TOTAL = NF * NMT
    DN_TOTAL = 16 * NMT
    wq = {"gu_i": 0, "gu_c": 0, "dn_i": 0, "dn_c": 0, "gu_lim": 0, "dn_lim": 0, "mt": 0}

    def _issue_gu():
        i = wq["gu_i"]
        f_ = i % NF
        par = i % NGU
        wq["gu_i"] = i + 1
        for gi_, wsrc in enumerate((sgate, sup)):
            K.dma(SP, lambda gi_=gi_, wsrc=wsrc, par=par, f_=f_: S.dma_start(
                out=wgu[par][:, gi_, :, :].rearrange("p k c -> p (k c)"), in_=wsrc[f_]), wgslot[par], R=[Bscr], W=[Bwgu[par]])

    def _issue_dn():
        i = wq["dn_i"]
        par = i % NDN
        wq["dn_i"] = i + 1
        K.dma(SP, lambda par=par, i=i: S.dma_start(out=wdn[par].rearrange("p f c -> p (f c)"), in_=sdown[i % 16]),
              wdslot[par], R=[Bscr], W=[Bwdn[par]])

    def use_gu():
        conv_issue(100)
        if Bscr.w is not None and Bscr.w[0] is cvslot:
            Bscr.w = (cvslot, cvslot.n)
        c = wq["gu_c"]
        while wq["gu_i"] < min(wq["gu_lim"], c + NGU):
            _issue_gu()
        wq["gu_c"] = c + 1
        return c % NGU

    def use_dn():
        c = wq["dn_c"]
        while wq["dn_i"] < min(wq["dn_lim"], c + NDN):
            _issue_dn()
        wq["dn_c"] = c + 1
        return c % NDN

    def prefetch_w():
        while wq["gu_i"] < min(wq["gu_lim"], wq["gu_c"] + NGU):
            _issue_gu()
        while wq["dn_i"] < min(wq["dn_lim"], wq["dn_c"] + NDN):
            _issue_dn()

    def rstd_from_ms(bk, TT, dst, Rb, Wb):
        act(lambda: A.activation(out=dst[:, 0:TT], in_=ps[:, bk, 0:TT], func=AF.Ln, bias=EPS_c[:, 0:1]), R=[pb[bk]] + Rb, W=Wb)
        act(lambda: A.activation(out=dst[:, 0:TT], in_=dst[:, 0:TT], func=AF.Exp, scale=-0.5), R=Wb, W=Wb)

    def rstd_in_psum(bk, TT):
        act(lambda: A.activation(out=rq[:, 0:TT], in_=ps[:, bk, 0:TT], func=AF.Ln, bias=EPS_c[:, 0:1]), R=[pb[bk]], W=[Brq])
        act(lambda: A.activation(out=ps[:, bk, 0:TT], in_=rq[:, 0:TT], func=AF.Exp, scale=-0.5), R=[Brq], W=[pb[bk]])

    def phase_B(tiles, seg_off, last_in_seg):
        nt = len(tiles)
        TT = 128 * nt
        wq["mt"] += 1
        wq["gu_lim"] = wq["mt"] * NF if last_in_seg else min(GU_TOTAL, (wq["mt"] + 1) * NF)
        wq["dn_lim"] = wq["mt"] * 16 if last_in_seg else min(DN_TOTAL, (wq["mt"] + 1) * 16)
        has_sample = (33 in tiles)

        def _pre():
            barrier_if_pending()
            if not has_sample:
                prefetch_w()

        run_x_tiles(tiles, True, pre_stage2=_pre)
        milestone('B.x')
        def proj(mt):
            pe([lambda k=k: T.matmul(ps[:, mt, 0:TT], lhsT=winb[:, k, mt * 128:(mt + 1) * 128], rhs=xnT[:, k, 0:TT],
                                     start=(k == 0), stop=(k == 7)) for k in range(8)], R=BxnTk + [Bwin], W=[pb[mt]])
        grp = []
        for tl in range(nt):
            for k in range(8):
                grp.append(lambda tl=tl, k=k: T.matmul(ps[:, 6, tl * 128:(tl + 1) * 128], lhsT=xnT[:, k, tl * 128:(tl + 1) * 128],
                                                       rhs=winb[:, k, 768:896], start=(k == 0), stop=(k == 7)))
        pe(grp, R=BxnTk + [Bwin], W=[pb[6]])
        act(lambda: A.copy(out=vtok[:, 1:1 + nt, :], in_=ps[:, 6, 0:TT].rearrange("p (t c) -> p t c", t=nt)), R=[pb[6]], W=[Bvt])
        for tl, gt in enumerate(tiles):
            if gt in (32, 33):
                act(lambda tl=tl: A.copy(out=kvo[:, 1, :], in_=ps[:, 6, tl * 128:(tl + 1) * 128]), R=[pb[6]], W=[Bkvo])
                if gt == 32:
                    K.dma(SP, lambda: S.dma_start(out=vwp[:, :], in_=kvo[:, 1, :]), oslot, R=[Bkvo])
                else:
                    for s_ in range(16):
                        K.dma(SP, lambda s_=s_: S.dma_start(out=vws[s_, 120:128, :], in_=kvo[8 * s_:8 * s_ + 8, 1, :]), oslot, R=[Bkvo])
        milestone('B.proj')
        sq_b = [sqt, sqtB]; rq_b = [rq, rqB]; Bsq_b = [[Bsqt], [BsqtB]]; Brq_b = [[Brq], [BrqB]]

        def qk_A(mt):
            pp = mt % 2
            act(lambda: A.activation(out=sq_b[pp][:, 0:TT], in_=ps[:, mt, 0:TT], func=AF.Square), R=[pb[mt]], W=Bsq_b[pp])
            pe(lambda: T.matmul(ps[:, 6 + pp, 0:TT], lhsT=blk64, rhs=sq_b[pp][:, 0:TT], start=True, stop=True), R=Bsq_b[pp] + [consts], W=[pb[6 + pp]])

        def qk_B(mt):
            pp = mt % 2
            rqc = rq_b[pp]
            rstd_from_ms(6 + pp, TT, rqc, [], Brq_b[pp])
            if mt < 4:
                dve(lambda: V.scalar_tensor_tensor(out=qn[:, mt, 0:TT], in0=ps[:, mt, 0:TT], scalar=gq_c[:, 0:1], in1=rqc[:, 0:TT],
                                                   op0=ALU.mult, op1=ALU.mult), R=[pb[mt], Bpar] + Brq_b[pp], W=[Bqn])
                return
            kv = mt - 4
            dve(lambda: V.scalar_tensor_tensor(out=kdupT[:, kv, 128:128 + TT], in0=ps[:, mt, 0:TT], scalar=gk_c[:, 0:1],
                                               in1=rqc[:, 0:TT], op0=ALU.mult, op1=ALU.mult), R=[pb[mt], Bpar] + Brq_b[pp], W=[Bkd])
            for tl, gt in enumerate(tiles):
                if gt in (32, 33):
                    lo, hi = kv * 64, kv * 64 + 64
                    dve(lambda tl=tl, lo=lo, hi=hi: V.scalar_tensor_tensor(
                        out=kTf[lo:hi, tl % 2, :], in0=ps[lo:hi, mt, tl * 128:(tl + 1) * 128], scalar=gk_c[lo:hi, 0:1],
                        in1=rqc[lo:hi, tl * 128:(tl + 1) * 128], op0=ALU.mult, op1=ALU.mult), R=[pb[mt], Bpar] + Brq_b[pp], W=[BkTf])
                    if kv == 1:
                        pe(lambda tl=tl: T.matmul(ps[:, 0, 0:128], lhsT=kTf[:, tl % 2, :], rhs=ident, is_transpose=True, start=True, stop=True),
                           R=[BkTf, consts], W=[pb[0]])
                        act(lambda: A.copy(out=kvo[:, 0, :], in_=ps[:, 0, 0:128]), R=[pb[0]], W=[Bkvo])
                        if gt == 32:
                            K.dma(SP, lambda: S.dma_start(out=kwp[:, :], in_=kvo[:, 0, :]), oslot, R=[Bkvo])
                        else:
                            for s_ in range(16):
                                K.dma(SP, lambda s_=s_: S.dma_start(out=kws[s_, 120:128, :], in_=kvo[8 * s_:8 * s_ + 8, 0, :]), oslot, R=[Bkvo])

        act(lambda: A.activation(out=sq8[:, 4:8, 0:TT], in_=ossm[:, :, seg_off:seg_off + TT], func=AF.Square), R=[Bossm], W=[Bsq8[1], BaT])
        proj(0); proj(1); qk_A(0); proj(2); qk_A(1)
        for mt in range(6):
            qk_B(mt)
            if mt + 3 < 6:
                proj(mt + 3)
            if mt + 2 < 6:
                qk_A(mt + 2)
        pe([lambda k=k: T.matmul(ps[:, 6, 0:TT], lhsT=onesA, rhs=sq8[:, 4 + k, 0:TT], start=(k == 0), stop=(k == 3)) for k in range(4)],
           R=[Bsq8[1], consts], W=[pb[6]])
        rstd_in_psum(6, TT)
        for k in range(4):
            dve(lambda k=k: V.scalar_tensor_tensor(out=xnT[:, 4 + k, 0:TT], in0=ossm[:, k, seg_off:seg_off + TT], scalar=gssm_c[:, k:k + 1],
                                                   in1=ps[:, 6, 0:TT], op0=ALU.mult, op1=ALU.mult), R=[Bossm, pb[6], Bpar], W=[BxnTk[4 + k]])
        if tiles[0] == 0:
            dump('qn', qn, [Bqn]); dump('kdupT', kdupT, [Bkd]); dump('vtok', vtok, [Bvt]); dump('xT0', xT, [BxT]); dump('xnT_B', xnT, BxnTk)
        milestone('B.qknorm')
        et5 = et.rearrange("p (par t b) q -> p par t b q", par=2, t=4)
        pt5 = pt.rearrange("p (par t b) q -> p par t b q", par=2, t=4)
        units = [(tl, kvg) for tl in range(nt) for kvg in range(2)]

        def masks_for(gt):
            samp = (gt == 33)
            mk_cur = m0cur if gt == 0 else (msamp if samp else mcur)
            mk_prev = mzero if gt == 0 else (mcache if samp else (m1prev if gt == 1 else mprev))
            return mk_prev, mk_cur

        def att_S(tl, kvg):
            gt = tiles[tl]
            samp = (gt == 33)
            if samp and kvg == 0:
                sample_cache_prep()
            grp = []
            for h in range(4 * kvg, 4 * kvg + 4):
                par = h % 2
                base = par * 64
                kv = kvg
                hp = par * 4 + h // 2
                for blk in range(2):
                    idx = hp * 2 + blk
                    kc0 = tl * 128 + blk * 128
                    if samp and blk == 0:
                        var = 0 if kv == par else 1
                        for s_ in range(16):
                            grp.append(lambda var=var, idx=idx, s_=s_, h=h, base=base: T.matmul(
                                ps[:, idx // 4, (idx % 4) * 128 + s_ * 8:(idx % 4) * 128 + s_ * 8 + 8], lhsT=KcT[base:base + 64, s_, var, :],
                                rhs=qn[base:base + 64, h // 2, tl * 128 + s_ * 8:tl * 128 + s_ * 8 + 8], start=True, stop=True,
                                tile_position=(base, 0)))
                        continue
                    grp.append(lambda kv=kv, idx=idx, kc0=kc0, h=h, base=base: T.matmul(
                        ps[:, idx // 4, (idx % 4) * 128:(idx % 4 + 1) * 128], lhsT=kdupT[base:base + 64, kv, kc0:kc0 + 128],
                        rhs=qn[base:base + 64, h // 2, tl * 128:(tl + 1) * 128], start=True, stop=True, tile_position=(base, 0)))
            pe(grp, R=[Bqn, Bkd] + (CACHE_B if samp else []), W=[pb[kvg], pb[2 + kvg]])

        def att_EM(tl, kvg):
            gt = tiles[tl]
            mk_prev, mk_cur = masks_for(gt)
            for bk in (kvg, 2 + kvg):
                act(lambda bk=bk: A.activation(out=et[:, 4 * bk:4 * bk + 4, :], in_=ps[:, bk, :].rearrange("p (a b) -> p a b", a=4),
                                               func=AF.Exp, scale=SCALE), R=[pb[bk]], W=[Bet[kvg]])
            ts_ = slice(2 * kvg, 2 * kvg + 2)
            for blk, mk in ((0, mk_prev), (1, mk_cur)):
                dve(lambda blk=blk, mk=mk: V.tensor_tensor(out=pt5[:, :, ts_, blk, :], in0=et5[:, :, ts_, blk, :],
                                                           in1=mk.unsqueeze(1).unsqueeze(1).to_broadcast([128, 2, 2, 128]), op=ALU.mult),
                    R=[Bet[kvg], consts], W=[Bpt[kvg]])

        def att_P(tl, kvg):
            gt = tiles[tl]
            samp = (gt == 33)
            bo = 4 + 2 * (tl % 2)
            bd_ = bo + 1
            grp = []
            grp2 = []
            for h in range(4 * kvg, 4 * kvg + 4):
                par = h % 2
                base = par * 64
                kv = kvg
                hp = par * 4 + h // 2
                t2 = h // 2
                for blk in range(2):
                    idx = hp * 2 + blk
                    if samp and blk == 0:
                        for s_ in range(16):
                            grp.append(lambda kv=kv, idx=idx, t2=t2, s_=s_, base=base: T.matmul(
                                ps[base:base + 64, bo, t2 * 128 + s_ * 8:t2 * 128 + s_ * 8 + 8], lhsT=Vc[:, s_, kv * 64:(kv + 1) * 64],
                                rhs=pt[:, idx, s_ * 8:s_ * 8 + 8], start=(s_ == 0), stop=False, tile_position=(0, base)))
                    else:
                        grp.append(lambda kv=kv, idx=idx, t2=t2, blk=blk, base=base: T.matmul(
                            ps[base:base + 64, bo, t2 * 128:(t2 + 1) * 128], lhsT=vtok[:, tl + blk, kv * 64:(kv + 1) * 64], rhs=pt[:, idx, :],
                            start=(blk == 0), stop=(blk == 1), tile_position=(0, base)))
                    grp2.append(lambda idx=idx, t2=t2, blk=blk, base=base: T.matmul(
                        ps[base:base + 64, bd_, t2 * 128:(t2 + 1) * 128], lhsT=ones_b[:, 0:64], rhs=pt[:, idx, :],
                        start=(blk == 0), stop=(blk == 1), tile_position=(0, base)))
            mix_ = []
            for a_, b_ in zip(grp, grp2):
                mix_ += [a_, b_]
            mix_ += grp[len(grp2):]
            pe(mix_ if not samp else grp + grp2, R=[Bpt[kvg], Bvt, consts] + (CACHE_B if samp else []), W=[pb[bo], pb[bd_]])

        def att_F(tl):
            bo = 4 + 2 * (tl % 2)
            bd_ = bo + 1
            for t2 in range(4):
                act(lambda t2=t2: A.activation(out=rden[:, t2, :], in_=ps[:, bd_, t2 * 128:(t2 + 1) * 128], func=AF.Ln, bias=esk[:, t2:t2 + 1]),
                    R=[pb[bd_], Bpar], W=[Brden])
            act(lambda: A.activation(out=rden, in_=rden, func=AF.Exp, scale=-1.0), R=[Brden], W=[Brden])
            dve(lambda: V.tensor_tensor(out=oatt[:, :, tl * 128:(tl + 1) * 128], in0=ps[:, bo, :].rearrange("p (a b) -> p a b", a=4),
                                        in1=rden, op=ALU.mult), R=[pb[bo], Brden], W=[Boatt] + Bost)
            pool(lambda: G.tensor_tensor(out=sq8[:, 0:4, tl * 128:(tl + 1) * 128], in0=oatt[:, :, tl * 128:(tl + 1) * 128],
                                         in1=oatt[:, :, tl * 128:(tl + 1) * 128], op=ALU.mult), R=[Boatt], W=[Bsq8[0]])

        nu = len(units)
        att_S(*units[0])
        if nu > 1:
            att_S(*units[1])
        att_EM(*units[0])
        for ui, (tl, par) in enumerate(units):
            if ui + 2 < nu:
                att_S(*units[ui + 2])
            if ui + 1 < nu:
                att_EM(*units[ui + 1])
            att_P(tl, par)
            if par == 1:
                att_F(tl)
        if tiles[0] == 0:
            dump('oatt', oatt, [Boatt]); dump('rden', rden, [Brden])
        if has_sample:
            prefetch_w()
        milestone('B.att')
        dve(lambda: V.tensor_copy(out=kdupT[:, :, 0:128], in_=kdupT[:, :, TT:TT + 128]), R=[Bkd], W=[Bkd])
        dve(lambda: V.tensor_copy(out=vtok[:, 0, :], in_=vtok[:, nt, :]), R=[Bvt], W=[Bvt])
        pe([lambda k=k: T.matmul(ps[:, 6, 0:TT], lhsT=onesA, rhs=sq8[:, k, 0:TT], start=(k == 0), stop=(k == 3)) for k in range(4)],
           R=[Bsq8[0], consts], W=[pb[6]])
        rstd_in_psum(6, TT)
        for k in range(4):
            dve(lambda k=k: V.scalar_tensor_tensor(out=xnT[:, k, 0:TT], in0=oatt[:, k, 0:TT], scalar=gatt_c[:, k:k + 1], in1=ps[:, 6, 0:TT],
                                                   op0=ALU.mult, op1=ALU.mult), R=[Boatt, pb[6], Bpar], W=[BxnTk[k]])
        if tiles[0] == 0:
            dump('mix', xnT, BxnTk)
        milestone('B.mix')
        for m in range(8):
            bk = m % 4
            if m == 0:
                for i_, k in enumerate((4, 5, 6, 7, 0, 1, 2, 3)):
                    pe(lambda k=k, m=m, bk=bk, i_=i_: T.matmul(ps[:, bk, 0:TT], lhsT=woutb[:, k, m * 128:(m + 1) * 128], rhs=xnT[:, k, 0:TT],
                                                               start=(i_ == 0), stop=(i_ == 7)), R=[BxnTk[k], Bwout], W=[pb[bk]])
            else:
                pe([lambda k=k, m=m, bk=bk: T.matmul(ps[:, bk, 0:TT], lhsT=woutb[:, k, m * 128:(m + 1) * 128], rhs=xnT[:, k, 0:TT],
                                                     start=(k == 0), stop=(k == 7)) for k in range(8)], R=BxnTk + [Bwout], W=[pb[bk]])
            dve(lambda m=m, bk=bk: V.tensor_tensor(out=xT[:, m, 0:TT], in0=ps[:, bk, 0:TT], in1=xT[:, m, 0:TT], op=ALU.add), R=[pb[bk], BxT], W=[BxT])
            act(lambda m=m: A.activation(out=sq8[:, m, 0:TT], in_=xT[:, m, 0:TT], func=AF.Square), R=[BxT], W=[Bsq8[m // 4]])
        if tiles[0] == 0:
            dump('hT', xT, [BxT])
        milestone('B.wout')
        pe([lambda k=k: T.matmul(ps[:, 6, 0:TT], lhsT=onesF, rhs=sq8[:, k, 0:TT], start=(k == 0), stop=(k == 7)) for k in range(8)],
           R=Bsq8 + [consts], W=[pb[6]])
        rstd_in_psum(6, TT)
        for k in range(8):
            dve(lambda k=k: V.scalar_tensor_tensor(out=xnT[:, k, 0:TT], in0=xT[:, k, 0:TT], scalar=gffn_c[:, k:k + 1], in1=ps[:, 6, 0:TT],
                                                   op0=ALU.mult, op1=ALU.mult), R=[BxT, pb[6], Bpar], W=[BxnTk[k]])
        if tiles[0] == 0:
            dump('fT', xnT, BxnTk)
        milestone('B.ffnnorm')
        for half in range(2):
            for fl in range(11):
                par = use_gu()
                bg, bu = (0, 1) if fl % 2 == 0 else (2, 3)
                if half == 0 and fl == 0:
                    for k in range(8):
                        pe(lambda k=k, par=par, bg=bg: T.matmul(ps[:, bg, 0:TT], lhsT=wgu[par][:, 0, k, :], rhs=xnT[:, k, 0:TT], start=(k == 0), stop=(k == 7)),
                           R=[BxnTk[k], Bwgu[par]], W=[pb[bg]])
                else:
                    pe([lambda k=k, par=par, bg=bg: T.matmul(ps[:, bg, 0:TT], lhsT=wgu[par][:, 0, k, :], rhs=xnT[:, k, 0:TT], start=(k == 0), stop=(k == 7))
                        for k in range(8)], R=BxnTk + [Bwgu[par]], W=[pb[bg]])
                pe([lambda k=k, par=par, bu=bu: T.matmul(ps[:, bu, 0:TT], lhsT=wgu[par][:, 1, k, :], rhs=xnT[:, k, 0:TT], start=(k == 0), stop=(k == 7))
                    for k in range(8)], R=BxnTk + [Bwgu[par]], W=[pb[bu]])
                act(lambda bg=bg: A.activation(out=sgb[:, 0:TT], in_=ps[:, bg, 0:TT], func=AF.Silu), R=[pb[bg]], W=[Bsgb])
                dve(lambda fl=fl, bu=bu: V.tensor_tensor(out=aT[:, fl, 0:TT], in0=ps[:, bu, 0:TT], in1=sgb[:, 0:TT], op=ALU.mult),
                    R=[pb[bu], Bsgb], W=[BaT, BsqtB, BrqB] + Bsq8)
            for m in range(8):
                par = use_dn()
                bk = 4 + (m % 2)
                pe([lambda fl=fl, par=par, bk=bk: T.matmul(ps[:, bk, 0:TT], lhsT=wdn[par][:, fl, :], rhs=aT[:, fl, 0:TT], start=(fl == 0), stop=(fl == 10))
                    for fl in range(11)], R=[BaT, Bwdn[par]], W=[pb[bk]])
                dve(lambda m=m, bk=bk: V.tensor_tensor(out=xT[:, m, 0:TT], in0=ps[:, bk, 0:TT], in1=xT[:, m, 0:TT], op=ALU.add), R=[pb[bk], BxT], W=[BxT])
        prefetch_w()
        if tiles[0] == 0:
            dump('yT', xT, [BxT])
        milestone('B.ffn')
        ost = [oatt[:, 0:2, :].rearrange("p a b -> p (a b)"), oatt[:, 2:4, :].rearrange("p a b -> p (a b)")]
        for tl, gt in enumerate(tiles):
            if gt == 0:
                continue
            par = tl % 2
            for half in range(2):
                bk = 6 + half
                pe([lambda m=m, bk=bk, tl=tl: T.matmul(ps[:, bk, (m % 4) * 128:(m % 4 + 1) * 128], lhsT=xT[:, m, tl * 128:(tl + 1) * 128], rhs=ident, is_transpose=True, start=True, stop=True)
                    for m in range(4 * half, 4 * half + 4)], R=[BxT, consts], W=[pb[bk]])
                if half == 0:
                    act(lambda half=half, bk=bk, par=par: A.copy(out=ost[par][:, 512 * half:512 * half + 512], in_=ps[:, bk, :]), R=[pb[bk]], W=[Bost[par]])
                else:
                    dve(lambda half=half, bk=bk, par=par: V.tensor_copy(out=ost[par][:, 512 * half:512 * half + 512], in_=ps[:, bk, :]), R=[pb[bk]], W=[Bost[par]])
            dst = ys[:, :] if gt == 33 else yp[(gt - 1) * 128:gt * 128, :]
            K.dma(SP, lambda dst=dst, par=par: S.dma_start(out=dst, in_=ost[par]), yslot[par], R=[Bost[par]])

    def sample_cache_prep():
        K.dma(POOL, lambda: G.dma_start(out=Vc, in_=cv.rearrange("s k c -> k s c")), cslot, W=CACHE_B)
        K.dma(SP, lambda: S.dma_start(out=kws[:, 0:120, :], in_=ck[:, 8:128, :]), oslot)
        K.dma(SP, lambda: S.dma_start(out=vws[:, 0:120, :], in_=cv[:, 8:128, :]), oslot)
        ckv = ck.rearrange("s k c -> k s c")
        for g8 in range(2):
            K.dma(SP, lambda g8=g8: S.dma_start(out=ckst[:, :, 0, :], in_=ckv[:, 8 * g8:8 * g8 + 8, :]), cslot, W=CACHE_B)
            K.dma(SP, lambda g8=g8: S.dma_start(out=ckst[:, :, 1, 0:64], in_=ckv[:, 8 * g8:8 * g8 + 8, 64:128]), cslot, W=CACHE_B)
            K.dma(SP, lambda g8=g8: S.dma_start(out=ckst[:, :, 1, 64:128], in_=ckv[:, 8 * g8:8 * g8 + 8, 0:64]), cslot, W=CACHE_B)
            for s8 in range(8):
                for var in range(2):
                    idx = s8 * 2 + var
                    bk = 6 + (idx // 4) % 2
                    pe(lambda s8=s8, var=var, bk=bk, idx=idx: T.matmul(ps[:, bk, (idx % 4) * 128:(idx % 4 + 1) * 128], lhsT=ckst[:, s8, var, :], rhs=ident, is_transpose=True, start=True, stop=True),
                       R=CACHE_B + [consts], W=[pb[bk]])
                    act(lambda s8=s8, var=var, bk=bk, idx=idx, g8=g8: A.copy(out=KcT[:, 8 * g8 + s8, var, :],
                                                                              in_=ps[:, bk, (idx % 4) * 128:(idx % 4 + 1) * 128]),
                        R=[pb[bk]], W=CACHE_B)

    EPS_c = ar.raw_f32(ARENA_WORDS - 1, 1)
    dve(lambda: V.memset(EPS_c, EPS), W=[consts])
    segs = [list(range(8 * s, 8 * s + 8)) for s in range(4)] + [[32, 33]]

    def _main():
        milestone("setup")
        for si, seg in enumerate(segs):
            is_last = (si == 4)
            phase_A(seg, is_last, post_on_dve=(si == 0))
            if is_last:
                for ri, (src_h, dsto) in enumerate(((hl_r, srp), (hl_i, sip))):
                    bk = 4 + ri
                    pe(lambda bk=bk, src_h=src_h: T.matmul(ps[0:16, bk, 0:128], lhsT=src_h, rhs=ident, is_transpose=True, start=True, stop=True), R=[Bhl, consts], W=[pb[bk]])
                    act(lambda bk=bk, ri=ri: A.copy(out=hs_st[0:16, ri * 128:(ri + 1) * 128], in_=ps[0:16, bk, 0:128]), R=[pb[bk]], W=[Bhs])
                    K.dma(SP, lambda dsto=dsto, ri=ri: S.dma_start(out=dsto[:, :], in_=hs_st[0:16, ri * 128:(ri + 1) * 128]), stslot, R=[Bhs])
                K.barrier(slots=[stslot])
                Bhs2 = bb("hs2")
                for ri, (src_h, dsto) in enumerate(((hsr, srs), (hsi, sis))):
                    for hh in range(2):
                        bk = 4 + hh
                        dve(lambda src_h=src_h, hh=hh: V.tensor_copy(out=rt1[:, 0:128].rearrange("p (s P) -> p P s", s=8),
                                                                      in_=src_h[:, :, hh * 8:(hh + 1) * 8]), R=[Bhs, Bhs2], W=[Brt])
                        pe(lambda bk=bk: T.matmul(ps[:, bk, 0:128], lhsT=rt1[:, 0:128], rhs=ident, is_transpose=True, start=True, stop=True), R=[Brt, consts], W=[pb[bk]])
                        act(lambda bk=bk: A.copy(out=rt2[:, 0:128], in_=ps[:, bk, 0:128]), R=[pb[bk]], W=[Bhs2])
                        K.dma(SP, lambda dsto=dsto, hh=hh: S.dma_start(out=dsto[hh * 128:(hh + 1) * 128, :], in_=rt2[:, 0:128]), stslot, R=[Bhs2])
                K.barrier(slots=[stslot, cslot])
                K.barrier()
            else:
                pend["barrier"] = True
            if si == 0:
                dump("useg", useg, [Buseg]); dump("ossm", ossm, [Bossm]); dump("hl_r", hl_r, [Bhl]); dump("hprev", hprev, [Bhp])
                dump("xnT_A", xnT, BxnTk); dump("hb_r", hb_r, [Bhb]); dump("zr", zr, [Bz]); dump("gr", gr, [Bg]); dump("Gt", Gt, [BGt])
            milestone("A%d" % si)
            for mt0 in range(0, len(seg), 4):
                mtl = seg[mt0:mt0 + 4]
                phase_B(mtl, mt0 * 128, mt0 + 4 >= len(seg))
                if mt0 + 4 >= len(seg):
                    pend["barrier"] = True
                milestone("B%d_%d" % (si, mt0))

    try:
        _main()
    except _StopBuild as e:
        print("build truncated at milestone", e)
    K.barrier()
    for sl in [oslot, stslot] + xslot + yslot + ([dbg_slot[0]] if dbg_slot[0] else []):
        if sl.n:
            S.wait_ge(sl.sem, sl.n)
    nc_ctx.__exit__(None, None, None)
    return nc


_NC_CACHE = {}


def kernel(x_prompt, x_sample, cache_k_win, cache_v_win, state_ssm_re, state_ssm_im,
           meta_tokens, g_mix, w_in, g_q, g_k, sinks,
           ssm_a_re, ssm_a_im, ssm_log_dt, ssm_b_re, ssm_b_im, ssm_c_re, ssm_c_im,
           ssm_d, ssm_w_glu, ssm_b_glu, g_att_out, g_ssm_out, w_out,
           g_ffn, w_gate, w_up, w_down):
    f = lambda a: np.ascontiguousarray(np.asarray(a, dtype=np.float32))
    if "nc" not in _NC_CACHE:
        _NC_CACHE["nc"] = build_nc()
    nc = _NC_CACHE["nc"]
    shared = {
        "meta": f(meta_tokens), "g_mix": f(g_mix).reshape(D), "w_in": f(w_in).reshape(D, 1280),
        "g_q": f(g_q).reshape(64), "g_k": f(g_k).reshape(64), "sinks": f(sinks).reshape(8),
        "a_re": f(ssm_a_re).reshape(2048), "a_im": f(ssm_a_im).reshape(2048), "log_dt": f(ssm_log_dt).reshape(32),
        "b_re": f(ssm_b_re).reshape(32768), "b_im": f(ssm_b_im).reshape(32768),
        "c_re": f(ssm_c_re).reshape(512, 64), "c_im": f(ssm_c_im).reshape(512, 64),
        "ssm_d": f(ssm_d).reshape(512), "w_glu": f(ssm_w_glu).reshape(512, 16), "b_glu": f(ssm_b_glu).reshape(512),
        "g_att": f(g_att_out).reshape(512), "g_ssm": f(g_ssm_out).reshape(512), "w_out": f(w_out).reshape(D, D),
        "g_ffn": f(g_ffn).reshape(D), "w_gate": f(w_gate).reshape(D, DFF), "w_up": f(w_up).reshape(D, DFF),
        "w_down": f(w_down).reshape(DFF, D),
    }
    xpf = f(x_prompt); xsf = f(x_sample)
    ckf = f(cache_k_win).reshape(128, 128, 128); cvf = f(cache_v_win).reshape(128, 128, 128)
    sref = f(state_ssm_re).reshape(128, 2048); simf = f(state_ssm_im).reshape(128, 2048)
    in_maps = []
    for b in range(NCORES):
        m = dict(shared)
        m["xp"] = xpf[b]
        m["xs"] = xsf[16 * b:16 * b + 16].reshape(128, D)
        m["ck"] = ckf[16 * b:16 * b + 16]
        m["cv"] = cvf[16 * b:16 * b + 16]
        m["sre"] = sref[16 * b:16 * b + 16].reshape(256, 128)
        m["sim"] = simf[16 * b:16 * b + 16].reshape(256, 128)
        in_maps.append(m)
    res = run_bass_kernel_spmd(nc, in_maps, core_ids=list(range(NCORES)))
    R = res.results
    y_prompt = np.stack([R[b]["yp"] for b in range(NCORES)]).astype(np.float32)
    y_sample = np.concatenate([R[b]["ys"].reshape(16, 8, D) for b in range(NCORES)]).astype(np.float32)
    kwp = np.stack([R[b]["kwp"].reshape(128, 2, 64) for b in range(NCORES)])[None].astype(np.float32)
    vwp = np.stack([R[b]["vwp"].reshape(128, 2, 64) for b in range(NCORES)])[None].astype(np.float32)
    srp = np.stack([R[b]["srp"].reshape(32, 64) for b in range(NCORES)])[None].astype(np.float32)
    sip = np.stack([R[b]["sip"].reshape(32, 64) for b in range(NCORES)])[None].astype(np.float32)
    kws = np.concatenate([R[b]["kws"].reshape(16, 128, 2, 64) for b in range(NCORES)])[None].astype(np.float32)
    vws = np.concatenate([R[b]["vws"].reshape(16, 128, 2, 64) for b in range(NCORES)])[None].astype(np.float32)
    srs = np.concatenate([R[b]["srs"].reshape(16, 32, 64) for b in range(NCORES)])[None].astype(np.float32)
    sis = np.concatenate([R[b]["sis"].reshape(16, 32, 64) for b in range(NCORES)])[None].astype(np.float32)
    return (y_prompt, y_sample, kwp, vwp, srp, sip, kws, vws, srs, sis)
```
